# Optimizing a Trainium2 kernel written in Bass

```python
import math
import jax, jax.numpy as jnp
from jax import lax
import numpy as np

D_MODEL = 1024
BATCH = 4
SEQ = 4096
DEPTH = 4

HEAD_DIM = 64
ROPE_DIMS = HEAD_DIM // 4
ROPE_THETA = 500000.0
Q_BLOCK = 128
CONV_W = 3
CONV_WIDTH = 512
DIFF_HEADS = 4
DIFF_V_DIM = 2 * HEAD_DIM
NSA_HEADS = 8
NSA_GROUPS = 2
NSA_HPG = NSA_HEADS // NSA_GROUPS
CMP_BLOCK = 32
CMP_STRIDE = 16
SEL_BLOCK = 64
SEL_TOPK = 16
N_LOCAL_SEL = 2
WINDOW = 512
PHI_HIDDEN = 128
BRANCH_WIDTH = 512
N_BRANCHES = 3
D_FF = 2816
ALPHA = (2 * DEPTH) ** 0.25
BETA = (8 * DEPTH) ** -0.25
LN_EPS = 1e-5
NEG = -1e30
BIG = 1e30

COL_SIZES = (CONV_WIDTH, CONV_WIDTH, CONV_WIDTH,
             DIFF_HEADS * 2 * HEAD_DIM, DIFF_HEADS * 2 * HEAD_DIM, DIFF_HEADS * DIFF_V_DIM,
             NSA_HEADS * HEAD_DIM, 6 * NSA_GROUPS * HEAD_DIM, N_BRANCHES * NSA_HEADS,
             N_BRANCHES * D_MODEL)
D_IN = sum(COL_SIZES)
SPLIT_POINTS = tuple(int(c) for c in np.cumsum(COL_SIZES)[:-1])

kernel_name = 'hybrid_conv_diffattn_nsa_trunk'


def rope_tables(seq):
    pos = jnp.arange(seq, dtype=jnp.float32)
    inv_freq = ROPE_THETA ** (-jnp.arange(0, ROPE_DIMS, 2, dtype=jnp.float32) / ROPE_DIMS)
    ang = pos[:, None] * inv_freq[None, :]
    return jnp.cos(ang), jnp.sin(ang)


def partial_rope(t, cos, sin):
    half = ROPE_DIMS // 2
    c = cos[None, :, None, :].astype(t.dtype)
    s = sin[None, :, None, :].astype(t.dtype)
    t1, t2, rest = t[..., :half], t[..., half:ROPE_DIMS], t[..., ROPE_DIMS:]
    return jnp.concatenate([t1 * c - t2 * s, t1 * s + t2 * c, rest], axis=-1)


def causal_dwconv(x, w):
    c = x.shape[-1]
    return lax.conv_general_dilated(
        x, w[:, None, :].astype(x.dtype), window_strides=(1,),
        padding=[(w.shape[0] - 1, 0)], dimension_numbers=('NWC', 'WIO', 'NWC'),
        feature_group_count=c)


def layer_norm(x, g, b):
    xf = x.astype(jnp.float32)
    mu = jnp.mean(xf, axis=-1, keepdims=True)
    var = jnp.mean(jnp.square(xf - mu), axis=-1, keepdims=True)
    return ((xf - mu) * lax.rsqrt(var + LN_EPS) * g.astype(jnp.float32) + b.astype(jnp.float32)).astype(x.dtype)


def rms_norm(x, g):
    xf = x.astype(jnp.float32)
    ms = jnp.mean(jnp.square(xf), axis=-1, keepdims=True)
    return (xf * lax.rsqrt(ms + LN_EPS) * g.astype(jnp.float32)).astype(x.dtype)


def diff_attention(q, k, v, lam, lam_init, subln_g):
    b, s = q.shape[0], q.shape[1]
    nb = s // Q_BLOCK
    scale = HEAD_DIM ** -0.5
    qb = q.reshape(b, nb, Q_BLOCK, DIFF_HEADS, 2, HEAD_DIM).transpose(1, 0, 2, 3, 4, 5)
    kpos = jnp.arange(s)

    def one_block(args):
        i, q_i = args
        qpos = i * Q_BLOCK + jnp.arange(Q_BLOCK)
        sc = jnp.einsum('bqhmd,bkhmd->bhmqk', q_i, k).astype(jnp.float32) * scale
        sc = jnp.where(kpos[None, :] <= qpos[:, None], sc, -jnp.inf)
        p = jax.nn.softmax(sc, axis=-1)
        a = p[:, :, 0] - lam * p[:, :, 1]
        return jnp.einsum('bhqk,bkhd->bqhd', a.astype(v.dtype), v)

    o = lax.map(one_block, (jnp.arange(nb), qb))
    o = o.transpose(1, 0, 2, 3, 4).reshape(b, s, DIFF_HEADS, DIFF_V_DIM)
    o = rms_norm(o, subln_g) * (1.0 - lam_init)
    return o.reshape(b, s, DIFF_HEADS * DIFF_V_DIM)


def compress_kv(kv, pos_emb, w1, w2):
    b, s = kv.shape[0], kv.shape[1]
    n_cmp = (s - CMP_BLOCK) // CMP_STRIDE + 1
    idx = jnp.arange(n_cmp)[:, None] * CMP_STRIDE + jnp.arange(CMP_BLOCK)[None, :]
    blocks = kv[:, idx] + pos_emb[None, None, :, None, :]
    flat = blocks.transpose(0, 1, 3, 2, 4).reshape(b, n_cmp, NSA_GROUPS, CMP_BLOCK * HEAD_DIM)
    return jax.nn.gelu(flat @ w1, approximate=False) @ w2


def nsa_attention(q, kc, vc, ks, vs, kw, vw, gate_logits, cmp_pos, phi_w1, phi_w2):
    b, s = q.shape[0], q.shape[1]
    scale = HEAD_DIM ** -0.5
    qg = q.reshape(b, s, NSA_GROUPS, NSA_HPG, HEAD_DIM)
    tpos = jnp.arange(s)

    kc_c = compress_kv(kc, cmp_pos[0], phi_w1[0], phi_w2[0])
    vc_c = compress_kv(vc, cmp_pos[1], phi_w1[1], phi_w2[1])
    n_cmp = kc_c.shape[1]
    cstart = jnp.arange(n_cmp) * CMP_STRIDE
    cvalid = (cstart + CMP_BLOCK - 1)[None, :] <= tpos[:, None]
    sc = jnp.einsum('bsghd,bngd->bghsn', qg, kc_c).astype(jnp.float32) * scale
    p_cmp = jax.nn.softmax(jnp.where(cvalid, sc, NEG), axis=-1) * cvalid
    o_cmp = jnp.einsum('bghsn,bngd->bsghd', p_cmp.astype(vc_c.dtype), vc_c)

    n_sel = s // SEL_BLOCK
    sstart = jnp.arange(n_sel) * SEL_BLOCK
    overlap = jnp.maximum(
        jnp.minimum((cstart + CMP_BLOCK)[:, None], (sstart + SEL_BLOCK)[None, :])
        - jnp.maximum(cstart[:, None], sstart[None, :]), 0).astype(jnp.float32) / CMP_BLOCK
    imp = jnp.einsum('bghsn,nj->bgsj', p_cmp, overlap)
    cur = tpos // SEL_BLOCK
    j = jnp.arange(n_sel)
    forced = (j[None, :] == 0) | ((j[None, :] <= cur[:, None]) & (j[None, :] > cur[:, None] - N_LOCAL_SEL))
    future = j[None, :] > cur[:, None]
    imp = jnp.where(forced, BIG, jnp.where(future, NEG, imp))
    top_k = min(SEL_TOPK, n_sel)
    _, sel_idx = lax.top_k(imp, top_k)

    nb = s // Q_BLOCK
    qb = qg.reshape(b, nb, Q_BLOCK, NSA_GROUPS, NSA_HPG, HEAD_DIM).transpose(1, 0, 2, 3, 4, 5)
    idxb = sel_idx.reshape(b, NSA_GROUPS, nb, Q_BLOCK, top_k).transpose(2, 0, 1, 3, 4)
    ks_blk = ks.reshape(b, n_sel, SEL_BLOCK, NSA_GROUPS, HEAD_DIM).transpose(0, 3, 1, 2, 4)
    vs_blk = vs.reshape(b, n_sel, SEL_BLOCK, NSA_GROUPS, HEAD_DIM).transpose(0, 3, 1, 2, 4)
    gather = jax.vmap(jax.vmap(lambda blocks, ix: blocks[ix]))
    kw_p = jnp.pad(kw, ((0, 0), (WINDOW, 0), (0, 0), (0, 0)))
    vw_p = jnp.pad(vw, ((0, 0), (WINDOW, 0), (0, 0), (0, 0)))
    offs = jnp.arange(SEL_BLOCK)
    n_keys = top_k * SEL_BLOCK

    def one_block(args):
        i, q_i, ix = args
        qpos = i * Q_BLOCK + jnp.arange(Q_BLOCK)
        kg = gather(ks_blk, ix).reshape(b, NSA_GROUPS, Q_BLOCK, n_keys, HEAD_DIM)
        vg = gather(vs_blk, ix).reshape(b, NSA_GROUPS, Q_BLOCK, n_keys, HEAD_DIM)
        kpos = (ix[..., None] * SEL_BLOCK + offs).reshape(b, NSA_GROUPS, Q_BLOCK, n_keys)
        sc = jnp.einsum('bqghd,bgqnd->bghqn', q_i, kg).astype(jnp.float32) * scale
        sc = jnp.where((kpos <= qpos[None, None, :, None])[:, :, None], sc, -jnp.inf)
        o_sel = jnp.einsum('bghqn,bgqnd->bqghd', jax.nn.softmax(sc, axis=-1).astype(vg.dtype), vg)
        kwin = lax.dynamic_slice_in_dim(kw_p, i * Q_BLOCK, Q_BLOCK + WINDOW, axis=1)
        vwin = lax.dynamic_slice_in_dim(vw_p, i * Q_BLOCK, Q_BLOCK + WINDOW, axis=1)
        wpos = i * Q_BLOCK - WINDOW + jnp.arange(Q_BLOCK + WINDOW)
        wmask = ((wpos[None, :] <= qpos[:, None]) & (wpos[None, :] > qpos[:, None] - WINDOW)
                 & (wpos[None, :] >= 0))
        sc = jnp.einsum('bqghd,bkgd->bghqk', q_i, kwin).astype(jnp.float32) * scale
        sc = jnp.where(wmask, sc, -jnp.inf)
        o_win = jnp.einsum('bghqk,bkgd->bqghd', jax.nn.softmax(sc, axis=-1).astype(vwin.dtype), vwin)
        return o_sel, o_win

    o_sel, o_win = lax.map(one_block, (jnp.arange(nb), qb, idxb))
    o_sel = o_sel.transpose(1, 0, 2, 3, 4, 5).reshape(b, s, NSA_GROUPS, NSA_HPG, HEAD_DIM)
    o_win = o_win.transpose(1, 0, 2, 3, 4, 5).reshape(b, s, NSA_GROUPS, NSA_HPG, HEAD_DIM)
    g = jax.nn.sigmoid(gate_logits).reshape(b, s, NSA_GROUPS, NSA_HPG, N_BRANCHES)
    o = g[..., 0:1] * o_cmp + g[..., 1:2] * o_sel + g[..., 2:3] * o_win
    return o.reshape(b, s, NSA_HEADS * HEAD_DIM)


def token_mixing(x, w_in, conv_a_w, lam_vec, subln_g, cmp_pos, phi_w1, phi_w2, w_branch, w_o,
                 lam_init, cos, sin):
    b, s = x.shape[0], x.shape[1]
    z = x @ w_in
    a_b, a_c, a_v, d_q, d_k, d_v, n_q, n_kv, n_g, m_g = jnp.split(z, SPLIT_POINTS, axis=-1)
    y_a = a_b * causal_dwconv(a_c * a_v, conv_a_w)
    dq = partial_rope(d_q.reshape(b, s, DIFF_HEADS * 2, HEAD_DIM), cos, sin).reshape(b, s, DIFF_HEADS, 2, HEAD_DIM)
    dk = partial_rope(d_k.reshape(b, s, DIFF_HEADS * 2, HEAD_DIM), cos, sin).reshape(b, s, DIFF_HEADS, 2, HEAD_DIM)
    dv = d_v.reshape(b, s, DIFF_HEADS, DIFF_V_DIM)
    lf = lam_vec.astype(jnp.float32)
    lam = jnp.exp(jnp.dot(lf[0], lf[1])) - jnp.exp(jnp.dot(lf[2], lf[3])) + lam_init
    y_b = diff_attention(dq, dk, dv, lam, lam_init, subln_g)
    nq = partial_rope(n_q.reshape(b, s, NSA_HEADS, HEAD_DIM), cos, sin)
    kv6 = n_kv.reshape(b, s, 6, NSA_GROUPS, HEAD_DIM)
    kc = partial_rope(kv6[:, :, 0], cos, sin)
    ks = partial_rope(kv6[:, :, 2], cos, sin)
    kw = partial_rope(kv6[:, :, 4], cos, sin)
    y_c = nsa_attention(nq, kc, kv6[:, :, 1], ks, kv6[:, :, 3], kw, kv6[:, :, 5], n_g,
                        cmp_pos, phi_w1, phi_w2)
    gates = jax.nn.sigmoid(m_g).reshape(b, s, N_BRANCHES, D_MODEL)
    merged = (gates[:, :, 0] * (y_a @ w_branch[0])
              + gates[:, :, 1] * (y_b @ w_branch[1])
              + gates[:, :, 2] * (y_c @ w_branch[2]))
    return merged @ w_o


def conv_ffn(x, w_up, conv_w, conv_b, w_down):
    h = causal_dwconv(x @ w_up, conv_w) + conv_b
    a, u = jnp.split(h, 2, axis=-1)
    return (jax.nn.gelu(a, approximate=False) * u) @ w_down


def setup_inputs(seed: int = 0) -> dict:
    key = jax.random.key(seed)
    k = jax.random.split(key, 20)

    def nrm(kk, shape, scale):
        return jax.random.normal(kk, shape, jnp.float32) * scale

    return {
        'x': nrm(k[0], (BATCH, SEQ, D_MODEL), 1.0),
        'w_in': nrm(k[1], (DEPTH, D_MODEL, D_IN), D_MODEL ** -0.5),
        'conv_a_w': nrm(k[2], (DEPTH, CONV_W, CONV_WIDTH), CONV_W ** -0.5),
        'diff_lambda': nrm(k[3], (DEPTH, 4, HEAD_DIM), 0.1),
        'diff_subln': 1.0 + nrm(k[4], (DEPTH, DIFF_V_DIM), 0.02),
        'nsa_cmp_pos': nrm(k[5], (DEPTH, 2, CMP_BLOCK, HEAD_DIM), 0.1),
        'nsa_phi_w1': nrm(k[6], (DEPTH, 2, CMP_BLOCK * HEAD_DIM, PHI_HIDDEN), (CMP_BLOCK * HEAD_DIM) ** -0.5),
        'nsa_phi_w2': nrm(k[7], (DEPTH, 2, PHI_HIDDEN, HEAD_DIM), PHI_HIDDEN ** -0.5),
        'w_branch': nrm(k[8], (DEPTH, N_BRANCHES, BRANCH_WIDTH, D_MODEL), BRANCH_WIDTH ** -0.5),
        'w_o': nrm(k[9], (DEPTH, D_MODEL, D_MODEL), BETA * D_MODEL ** -0.5),
        'ln1_g': 1.0 + nrm(k[10], (DEPTH, D_MODEL), 0.02),
        'ln1_b': nrm(k[11], (DEPTH, D_MODEL), 0.02),
        'ffn_w_up': nrm(k[12], (DEPTH, D_MODEL, 2 * D_FF), D_MODEL ** -0.5),
        'ffn_conv_w': nrm(k[13], (DEPTH, CONV_W, 2 * D_FF), CONV_W ** -0.5),
        'ffn_conv_b': nrm(k[14], (DEPTH, 2 * D_FF), 0.02),
        'ffn_w_down': nrm(k[15], (DEPTH, D_FF, D_MODEL), BETA * D_FF ** -0.5),
        'ln2_g': 1.0 + nrm(k[16], (DEPTH, D_MODEL), 0.02),
        'ln2_b': nrm(k[17], (DEPTH, D_MODEL), 0.02),
    }


def reference(x, w_in, conv_a_w, diff_lambda, diff_subln, nsa_cmp_pos, nsa_phi_w1, nsa_phi_w2,
              w_branch, w_o, ln1_g, ln1_b, ffn_w_up, ffn_conv_w, ffn_conv_b, ffn_w_down,
              ln2_g, ln2_b):
    cos, sin = rope_tables(x.shape[1])
    for l in range(DEPTH):
        lam_init = 0.8 - 0.6 * math.exp(-0.3 * l)
        mix = token_mixing(x, w_in[l], conv_a_w[l], diff_lambda[l], diff_subln[l], nsa_cmp_pos[l],
                           nsa_phi_w1[l], nsa_phi_w2[l], w_branch[l], w_o[l], lam_init, cos, sin)
        x = layer_norm(ALPHA * x + mix, ln1_g[l], ln1_b[l])
        ffn = conv_ffn(x, ffn_w_up[l], ffn_conv_w[l], ffn_conv_b[l], ffn_w_down[l])
        x = layer_norm(ALPHA * x + ffn, ln2_g[l], ln2_b[l])
    return x
```

```python
import math
import os
from contextlib import ExitStack

import numpy as np
import concourse.bass as bass
import concourse.mybir as mybir
from concourse.bass_utils import run_bass_kernel_spmd

F32 = mybir.dt.float32
F32R = mybir.dt.float32r
BF16 = mybir.dt.bfloat16
AF = mybir.ActivationFunctionType
ALU = mybir.AluOpType
AX = mybir.AxisListType

S_LEN = 4096
NCH = 8
CH = 512
ALPHA = 8.0 ** 0.25
LN_EPS = 1e-5
NEGB = -30000.0
N_JUNK = int(os.environ.get('N_JUNK', '0'))

PV_CA, PV_SL, PV_G1, PV_B1, PV_G2, PV_B2, PV_FW, PV_FB, PV_N = 0, 12, 13, 21, 29, 37, 45, 177, 221


class Buf:
    __slots__ = ("name", "w", "r", "excl")

    def __init__(self, name="", excl=False):
        self.name = name
        self.w = None
        self.r = {}
        self.excl = excl


class Sched:
    def __init__(self, nc, es):
        self.nc = nc
        self.eng = {"pe": nc.tensor, "act": nc.scalar, "dve": nc.vector,
                    "pool": nc.gpsimd, "sp": nc.sync}
        self.sem = {}
        self.cnt = {}
        for k in ("pe", "act", "dve", "pool"):
            self.sem[k] = es.enter_context(nc.semaphore("s_" + k))
            self.cnt[k] = 0
        self.dsems = {}
        self.dnext = {}
        for q, n in {"sp": 16, "pool": 8}.items():
            self.dsems[q] = []
            for i in range(n):
                key = "d_%s_%d" % (q, i)
                self.sem[key] = es.enter_context(nc.semaphore(key))
                self.cnt[key] = 0
                self.dsems[q].append(key)
            self.dnext[q] = 0
        self.seen = {k: {} for k in self.eng}
        self.n_inst = 0
        self.n_wait = 0

    def _wait(self, ek, deps):
        e = self.eng[ek]
        seen = self.seen[ek]
        for key, val in deps.items():
            if val <= 0:
                continue
            if key == ek and ek == "pe":
                continue
            if seen.get(key, 0) >= val:
                continue
            e.wait_ge(self.sem[key], val)
            seen[key] = val
            self.n_wait += 1

    @staticmethod
    def _collect(reads, writes):
        deps = {}
        for b in reads:
            if b.w is not None and deps.get(b.w[0], 0) < b.w[1]:
                deps[b.w[0]] = b.w[1]
        for b in writes:
            if b.w is not None and deps.get(b.w[0], 0) < b.w[1]:
                deps[b.w[0]] = b.w[1]
            for k, v in b.r.items():
                if deps.get(k, 0) < v:
                    deps[k] = v
        return deps

    @staticmethod
    def _update(key, val, reads, writes):
        for b in reads:
            if b.r.get(key, 0) < val:
                b.r[key] = val
        for b in writes:
            b.w = (key, val)
            b.r = {}

    def op(self, ek, fn, reads=(), writes=()):
        xr = [b for b in reads if b.excl]
        self._wait(ek, self._collect(reads, list(writes) + xr))
        inst = fn(self.eng[ek])
        self.cnt[ek] += 1
        inst.then_inc(self.sem[ek], 1)
        self._update(ek, self.cnt[ek], reads, writes)
        self.n_inst += 1

    def dma(self, q, out, in_, reads=(), writes=()):
        lst = self.dsems[q]
        key = lst[self.dnext[q] % len(lst)]
        self.dnext[q] += 1
        deps = self._collect(reads, writes)
        if self.cnt[key] > 0:
            deps[key] = max(deps.get(key, 0), self.cnt[key])
        self._wait(q, deps)
        inst = self.eng[q].dma_start(out=out, in_=in_)
        self.cnt[key] += 16
        inst.then_inc(self.sem[key], 16)
        self._update(key, self.cnt[key], reads, writes)
        self.n_inst += 1

    def cc(self, kind, ins, outs, groups, reads=(), writes=()):
        q = "pool"
        lst = self.dsems[q]
        key = lst[self.dnext[q] % len(lst)]
        self.dnext[q] += 1
        deps = self._collect(reads, writes)
        if self.cnt[key] > 0:
            deps[key] = max(deps.get(key, 0), self.cnt[key])
        self._wait(q, deps)
        inst = self.eng[q].collective_compute(kind, ALU.bypass, replica_groups=groups, ins=ins, outs=outs)
        self.cnt[key] += 16
        inst.then_inc(self.sem[key], 16)
        self._update(key, self.cnt[key], reads, writes)
        self.n_inst += 1

    def barrier(self):
        deps = {k: v for k, v in self.cnt.items() if v > 0}
        for ek in self.eng:
            self._wait(ek, dict(deps))


class Stream:
    def __init__(self, lookahead=2):
        self.pend = []
        self.la = lookahead
        self.count = 0
        self.deferred = []

    def push(self, s_fn, post_fn):
        s_fn()
        self.pend.append(post_fn)
        if len(self.pend) > self.la:
            self._pop()

    def _pop(self):
        fn = self.pend.pop(0)
        fn()
        self.count += 1
        while self.deferred and self.deferred[0][0] <= self.count:
            self.deferred.pop(0)[1]()

    def defer(self, n, fn):
        self.deferred.append((self.count + 1 + n, fn))

    def flush(self):
        while self.pend:
            self._pop()
        while self.deferred:
            self.deferred.pop(0)[1]()


def build(n_layers=4, debug=False, stop_phase=99, max_tiles=999, do_v=True, tile_lo=0):
    nc = bass.Bass("TRN2", target_bir_lowering=False)

    def DI(name, shape, dt=F32):
        return nc.dram_tensor(name, list(shape), dt, kind="ExternalInput").ap()

    def DS(name, shape, dt=BF16):
        kind = "ExternalOutput" if debug else "Internal"
        return nc.dram_tensor(name, list(shape), dt, kind=kind).ap()

    xT_in = DI("xT", [1024, S_LEN])
    w_in = DI("w_in", [4, 1024, 7448])
    w_branch = DI("w_branch", [4, 3, 512, 1024])
    w_o = DI("w_o", [4, 1024, 1024])
    w_up = DI("ffn_w_up", [4, 1024, 5632])
    w_down = DI("ffn_w_down", [4, 2816, 1024])
    phi_w1 = DI("nsa_phi_w1", [4, 2, 2048, 128])
    phi_w2 = DI("nsa_phi_w2", [4, 2, 128, 64])
    pvec = DI("pvec", [4, 128, PV_N])
    posT = DI("posT", [4, 2, 128, 32])
    lamrep = DI("lamrep", [4, 128, 256])
    ropeC_d = DI("ropeC", [128, S_LEN])
    ropeS_d = DI("ropeS", [128, S_LEN])
    permT_d = DI("permT", [128, 128])
    ident_d = DI("ident", [128, 128])
    expand_d = DI("expand", [64, S_LEN])
    ovl1_d = DI("ovl1", [128, 2, 65])
    cbig_d = DI("cbig", [128, 32, 64])
    cfut_d = DI("cfut", [128, 32, 64])
    gsel_d = DI("gsel", [24, 24 * 64])

    outT = nc.dram_tensor("outT", [1024, S_LEN], F32, kind="ExternalOutput").ap()

    yT = DS("yT", [1536, S_LEN])
    qdT = DS("qdT", [512, S_LEN])
    kdT = DS("kdT", [512, S_LEN])
    vd = DS("vd", [S_LEN, 512])
    qnT = DS("qnT", [512, S_LEN])
    kcT = DS("kcT", [128, S_LEN])
    vcT = DS("vcT", [128, S_LEN])
    ksT = DS("ksT", [128, S_LEN])
    kwT = DS("kwT", [128, S_LEN])
    vaug = DS("vaug", [4, S_LEN, 128])
    ngT = DS("ngT", [24, S_LEN])
    mgT = DS("mgT", [3072, S_LEN])
    gT = DS("gT", [2816, S_LEN])
    xres = [DS("xresA", [1024, S_LEN], F32), DS("xresB", [1024, S_LEN], F32)]

    with ExitStack() as es:
        S = Sched(nc, es)

        uid = [0]

        def T(stack, name, shape, dt):
            uid[0] += 1
            return stack.enter_context(nc.sbuf_tensor("%s_u%d" % (name, uid[0]), list(shape), dt)), Buf(name)

        PSP = [es.enter_context(nc.psum_tensor("psp%d" % i, [128, 2 * 512], F32)) for i in range(2)]
        PS = []
        PB = []
        for i in range(8):
            if i < 4:
                PS.append(PSP[i // 2][:, (i % 2) * 512:(i % 2 + 1) * 512])
            else:
                PS.append(es.enter_context(nc.psum_tensor("ps%d" % i, [128, 512], F32)))
            PB.append(Buf("ps%d" % i, excl=True))

        xT_bf = es.enter_context(nc.sbuf_tensor("xT_bf", [128, 8, S_LEN], BF16))
        XB = [Buf("xb%d" % c) for c in range(NCH)]
        pv, PVB = T(es, "pv", [128, PV_N], F32)
        ones_bf, ONB = T(es, "ones_bf", [128, 128], BF16)
        ones_f, ONF = T(es, "ones_f", [128, 128], F32)
        permT, PMB = T(es, "permT_sb", [128, 128], BF16)
        ident, IDB = T(es, "ident_sb", [128, 128], BF16)
        lamt, LAMB = T(es, "lamt", [128, 256], F32)
        lamw, LAMW = T(es, "lamw", [128, 64], F32)
        lams, LAMS = T(es, "lams", [128, 8], F32)

        S.op("dve", lambda e: e.memset(ones_bf[:], 1.0), writes=[ONB])
        S.op("dve", lambda e: e.memset(ones_f[:], 1.0), writes=[ONF])
        S.dma("pool", permT[:], permT_d[:], writes=[PMB])
        S.dma("pool", ident[:], ident_d[:], writes=[IDB])

        for c in range(NCH):
            S.dma("pool", xT_bf[:, :, c * CH:(c + 1) * CH],
                  xT_in[:, c * CH:(c + 1) * CH].rearrange("(kc p) t -> p kc t", p=128), writes=[XB[c]])

        psrot = [0]

        def next_ps(lo=0, hi=4):
            i = lo + psrot[0] % (hi - lo)
            psrot[0] += 1
            return i

        rot6 = [0]

        def next_ps6(banks=(0, 1, 2, 3, 6, 7)):
            i = banks[rot6[0] % len(banks)]
            rot6[0] += 1
            return i

        for l in range(n_layers):
            lam_init = 0.8 - 0.6 * math.exp(-0.3 * l)
            xr_in = xT_in if l == 0 else xres[(l + 1) % 2]
            xr_mid = xres[l % 2]
            xr_out = outT if l == n_layers - 1 else xres[(l + 1) % 2]
            if l > 0:
                xr_in = xres[1]
                xr_mid = xres[0]
                xr_out = outT if l == n_layers - 1 else xres[1]
            else:
                xr_in = xT_in
                xr_mid = xres[0]
                xr_out = outT if l == n_layers - 1 else xres[1]

            S.dma("sp", pv[:], pvec[l], writes=[PVB])
            S.dma("sp", lamt[:], lamrep[l], writes=[LAMB])
            for j in range(2):
                S.op("dve", lambda e, j=j: e.tensor_tensor(out=lamw[:], in0=lamt[:, j * 128:j * 128 + 64],
                                                           in1=lamt[:, j * 128 + 64:j * 128 + 128], op=ALU.mult),
                     reads=[LAMB], writes=[LAMW])
                S.op("dve", lambda e, j=j: e.reduce_sum(out=lams[:, j:j + 1], in_=lamw[:], axis=AX.X),
                     reads=[LAMW], writes=[LAMS])
                S.op("act", lambda e, j=j: e.activation(out=lams[:, 2 + j:3 + j], in_=lams[:, j:j + 1], func=AF.Exp),
                     reads=[LAMS], writes=[LAMS])
            S.op("dve", lambda e: e.tensor_tensor(out=lams[:, 4:5], in0=lams[:, 2:3], in1=lams[:, 3:4], op=ALU.subtract),
                 reads=[LAMS], writes=[LAMS])
            S.op("dve", lambda e: e.tensor_scalar(out=lams[:, 5:6], in0=lams[:, 4:5], scalar1=lam_init, scalar2=-1.0,
                                                  op0=ALU.add, op1=ALU.mult), reads=[LAMS], writes=[LAMS])
            S.op("dve", lambda e: e.tensor_scalar(out=lams[:, 6:7], in0=pv[:, PV_SL:PV_SL + 1], scalar1=1.0 - lam_init,
                                                  scalar2=None, op0=ALU.mult), reads=[PVB, LAMS], writes=[LAMS])

            with ExitStack() as ph:
                ropeC, RCB = T(ph, "ropeC_sb", [128, S_LEN], F32)
                ropeS, RSB = T(ph, "ropeS_sb", [128, S_LEN], F32)
                S.dma("sp", ropeC[:], ropeC_d[:], writes=[RCB])
                S.dma("sp", ropeS[:], ropeS_d[:], writes=[RSB])
                wb = []
                for i in range(2):
                    wb.append(T(ph, "wb%d" % i, [128, 8, 128], BF16))
                ac_buf = ph.enter_context(nc.sbuf_tensor("ac_buf%d" % l, [128, S_LEN], F32))
                ACB = [Buf() for _ in range(NCH)]
                cvp = ph.enter_context(nc.sbuf_tensor("cvp%d" % l, [128, 2 + S_LEN], F32))
                CVB = [Buf() for _ in range(NCH)]
                CVH = Buf()
                S.op("pool", lambda e: e.memset(cvp[:, 0:2], 0.0), writes=[CVH])
                stg = [T(ph, "stg%d" % i, [128, CH], BF16) for i in range(4)]
                zbs = [T(ph, "zb%d" % i, [128, CH], BF16) for i in range(2)]
                t1s = [T(ph, "t1_%d" % i, [128, CH], F32) for i in range(2)]
                t2s = [T(ph, "t2_%d" % i, [128, CH], F32) for i in range(2)]
                rot = {"stg": 0, "zb": 0, "t": 0}

                def nxt(key, lst):
                    i = rot[key] % len(lst)
                    rot[key] += 1
                    return lst[i]

                wv, WVB = T(ph, "wv", [128, 8, 512], BF16)
                wv2, WV2B = T(ph, "wv2", [128, 8, 256], BF16)
                vst = [T(ph, "vst%d" % i, [128, 4, 128], BF16) for i in range(2)]
                for (vs_t, vs_b) in vst:
                    S.op("pool", lambda e, vs_t=vs_t: e.memset(vs_t[:], 1.0), writes=[vs_b])
                S.dma("pool", wv[:], w_in[l, :, 2560:3072].rearrange("(kc p) m -> p kc m", p=128), writes=[WVB])
                S.dma("pool", wv2[:, :, 0:128], w_in[l, :, 3968:4096].rearrange("(kc p) m -> p kc m", p=128), writes=[WV2B])
                S.dma("pool", wv2[:, :, 128:256], w_in[l, :, 4224:4352].rearrange("(kc p) m -> p kc m", p=128), writes=[WV2B])
                tiles = []
                for i in range(4):
                    tiles.append((512 + 128 * i, 128, "ac", None))
                    tiles.append((1024 + 128 * i, 128, "av", None))
                    tiles.append((128 * i, 128, "ab", i))
                for i in range(4):
                    tiles.append((1536 + 128 * i, 128, "rope", qdT[128 * i:128 * (i + 1), :]))
                for i in range(4):
                    tiles.append((2048 + 128 * i, 128, "rope", kdT[128 * i:128 * (i + 1), :]))
                for h in range(8):
                    tiles.append((3072 + 64 * h, 64, "rope", qnT[64 * h:64 * (h + 1), :]))
                tiles.append((3584 + 0 * 128, 128, "rope", kcT))
                tiles.append((3584 + 1 * 128, 128, "copy", vcT))
                tiles.append((3584 + 2 * 128, 128, "rope", ksT))
                tiles.append((3584 + 4 * 128, 128, "rope", kwT))
                tiles.append((4352, 24, "sig", ngT))
                for i in range(24):
                    tiles.append((4376 + 128 * i, 128, "sig", mgT[128 * i:128 * (i + 1), :]))

                def load_w(n):
                    c0, M, _, _ = tiles[n]
                    wt, wbuf = wb[n % 2]
                    S.dma("pool", wt[:, :, 0:M],
                          w_in[l, :, c0:c0 + M].rearrange("(kc p) m -> p kc m", p=128), writes=[wbuf])

                tiles = tiles[tile_lo:max_tiles]
                load_w(0)
                for n, (c0, M, kind, dst) in enumerate(tiles):
                    if n + 1 < len(tiles):
                        load_w(n + 1)
                    wt, wbuf = wb[n % 2]
                    for c in range(NCH):
                        tok = slice(c * CH, (c + 1) * CH)
                        pb = next_ps6()
                        for kc in range(8):
                            S.op("pe", lambda e, kc=kc, pb=pb: e.matmul(PS[pb][0:M, :], lhsT=wt[:, kc, 0:M],
                                                                      rhs=xT_bf[:, kc, tok], start=(kc == 0), stop=(kc == 7)),
                                 reads=[wbuf, XB[c]], writes=[PB[pb]])
                        if kind == "ac":
                            S.op("act", lambda e: e.activation(out=ac_buf[:, tok], in_=PS[pb][:, :], func=AF.Copy),
                                 reads=[PB[pb]], writes=[ACB[c]])
                        elif kind == "av":
                            S.op("dve", lambda e: e.tensor_tensor(out=cvp[:, 2 + c * CH:2 + (c + 1) * CH], in0=PS[pb][:, :],
                                                                  in1=ac_buf[:, tok], op=ALU.mult),
                                 reads=[PB[pb], ACB[c]], writes=[CVB[c]])
                        elif kind == "ab":
                            i = dst
                            t1, t1b = nxt("t", t1s)
                            sg, sgb = nxt("stg", stg)
                            halo = [CVB[c - 1]] if c > 0 else [CVH]
                            cw = lambda k: pv[:, PV_CA + i * 3 + k:PV_CA + i * 3 + k + 1]
                            S.op("dve", lambda e: e.tensor_scalar(out=t1[:], in0=cvp[:, c * CH:c * CH + CH], scalar1=cw(0),
                                                                  scalar2=None, op0=ALU.mult),
                                 reads=[CVB[c], PVB] + halo, writes=[t1b])
                            S.op("dve", lambda e: e.scalar_tensor_tensor(out=t1[:], in0=cvp[:, c * CH + 1:c * CH + CH + 1],
                                                                         scalar=cw(1), in1=t1[:], op0=ALU.mult, op1=ALU.add),
                                 reads=[CVB[c], PVB, t1b] + halo, writes=[t1b])
                            S.op("dve", lambda e: e.scalar_tensor_tensor(out=t1[:], in0=cvp[:, c * CH + 2:c * CH + CH + 2],
                                                                         scalar=cw(2), in1=t1[:], op0=ALU.mult, op1=ALU.add),
                                 reads=[CVB[c], PVB, t1b], writes=[t1b])
                            S.op("dve", lambda e: e.tensor_tensor(out=sg[:], in0=PS[pb][:, :], in1=t1[:], op=ALU.mult),
                                 reads=[PB[pb], t1b], writes=[sgb])
                            S.dma("sp", yT[128 * i:128 * (i + 1), tok], sg[:], reads=[sgb])
                        elif kind == "rope":
                            zb, zbb = nxt("zb", zbs)
                            t1, t1b = t1s[rot["t"] % 2]
                            t2, t2b = t2s[rot["t"] % 2]
                            rot["t"] += 1
                            sg, sgb = nxt("stg", stg)
                            pw = next_ps(4, 6)
                            S.op("act", lambda e: e.activation(out=zb[0:M, :], in_=PS[pb][0:M, :], func=AF.Copy),
                                 reads=[PB[pb]], writes=[zbb])
                            if os.environ.get("ROPE_DBG") != "nope":
                                S.op("pe", lambda e: e.matmul(PS[pw][0:M, :], lhsT=permT[0:M, 0:M], rhs=zb[0:M, :],
                                                              start=True, stop=True),
                                     reads=[PMB, zbb], writes=[PB[pw]])
                            dbg = os.environ.get("ROPE_DBG", "")
                            if dbg == "b":
                                S.dma("sp", dst[0:M, tok], zb[0:M, :], reads=[zbb])
                                continue
                            if dbg == "c":
                                S.op("dve", lambda e: e.tensor_tensor(out=t1[0:M, :], in0=PS[pb][0:M, :], in1=ac_buf[0:M, tok],
                                                                      op=ALU.mult), reads=[PB[pb]], writes=[t1b])
                            else:
                                S.op("dve", lambda e: e.tensor_tensor(out=t1[0:M, :], in0=PS[pb][0:M, :], in1=ropeC[0:M, tok],
                                                                      op=ALU.mult), reads=[PB[pb], RCB, zbb], writes=[t1b])
                            if dbg == "c":
                                S.op("dve", lambda e: e.tensor_copy(out=sg[0:M, :], in_=t1[0:M, :]), reads=[t1b], writes=[sgb])
                                S.dma("sp", dst[0:M, tok], sg[0:M, :], reads=[sgb])
                                continue
                            S.op("dve", lambda e: e.tensor_tensor(out=t2[0:M, :], in0=PS[pw][0:M, :], in1=ropeS[0:M, tok],
                                                                  op=ALU.mult), reads=[PB[pw], RSB], writes=[t2b])
                            S.op("pool", lambda e: e.tensor_tensor(out=sg[0:M, :], in0=t1[0:M, :], in1=t2[0:M, :], op=ALU.add),
                                 reads=[t1b, t2b], writes=[sgb])
                            S.dma("sp", dst[0:M, tok], sg[0:M, :], reads=[sgb])
                        elif kind in ("sig", "copy"):
                            sg, sgb = nxt("stg", stg)
                            fn = AF.Sigmoid if kind == "sig" else AF.Copy
                            S.op("act", lambda e: e.activation(out=sg[0:M, :], in_=PS[pb][0:M, :], func=fn),
                                 reads=[PB[pb]], writes=[sgb])
                            S.dma("sp", dst[0:M, tok], sg[0:M, :], reads=[sgb])

                for tt in range(32 if do_v else 0):
                    c = tt // 4
                    tk = slice(tt * 128, (tt + 1) * 128)
                    pb = next_ps6()
                    for kc in range(8):
                        S.op("pe", lambda e, kc=kc: e.matmul(PS[pb][:, :], lhsT=xT_bf[:, kc, tk], rhs=wv[:, kc, :],
                                                             start=(kc == 0), stop=(kc == 7)),
                             reads=[WVB, XB[c]], writes=[PB[pb]])
                    sg, sgb = nxt("stg", stg)
                    S.op("act", lambda e: e.activation(out=sg[:], in_=PS[pb][:, :], func=AF.Copy), reads=[PB[pb]], writes=[sgb])
                    S.dma("sp", vd[tk, :], sg[:], reads=[sgb])
                    pb2 = next_ps6()
                    for kc in range(8):
                        S.op("pe", lambda e, kc=kc: e.matmul(PS[pb2][:, 0:256], lhsT=xT_bf[:, kc, tk], rhs=wv2[:, kc, :],
                                                             start=(kc == 0), stop=(kc == 7)),
                             reads=[WV2B, XB[c]], writes=[PB[pb2]])
                    vs_t, vs_b = vst[tt % 2]
                    S.op("act", lambda e: e.activation(out=vs_t[:, :, 0:64],
                                                       in_=PS[pb2][:, 0:256].rearrange("p (i d) -> p i d", d=64), func=AF.Copy),
                         reads=[PB[pb2]], writes=[vs_b])
                    S.dma("sp", vaug[:, tk, :].rearrange("i p c -> p i c"), vs_t[:], reads=[vs_b])
                S.barrier()
            if stop_phase <= 1:
                break

            with ExitStack() as ph:
                vall, VAB = T(ph, "vall", [128, 32, 512], BF16)
                S.dma("sp", vall[:], vd.rearrange("(kt p) c -> p kt c", p=128), writes=[VAB])
                qk = [(T(ph, "dq%d" % i, [128, S_LEN], BF16), T(ph, "dk%d" % i, [128, S_LEN], BF16)) for i in range(2)]
                pbufs = [T(ph, "pd%d" % i, [128, 2 * CH], BF16) for i in range(3)]
                osets = []
                for i in range(2):
                    osets.append(dict(r1=T(ph, "r1_%d" % i, [128, CH], F32), o1=T(ph, "o1_%d" % i, [128, CH], F32),
                                      o2=T(ph, "o2_%d" % i, [128, CH], F32), sq=T(ph, "sq_%d" % i, [128, CH], F32)))
                ysg = [T(ph, "ysg%d" % i, [128, CH], BF16) for i in range(2)]
                prot = [0]
                st = Stream(1)
                prot2 = [0]

                def load_qk(h):
                    (qt, qb), (kt_, kb) = qk[h % 2]
                    S.dma("sp", qt[:], qdT[128 * h:128 * (h + 1), :], writes=[qb])
                    S.dma("sp", kt_[:], kdT[128 * h:128 * (h + 1), :], writes=[kb])

                def mk_step(h, qc, m, kts, nkt, oset, gi):
                    (qt, qb), (kt_, kb) = qk[h % 2]
                    po, pl = 4 + m, 6 + m
                    rows = slice(m * 64, (m + 1) * 64)
                    kt = kts[-1]
                    j = kts[0] - 4 * qc
                    c0 = 128 * j if j > 0 else 0
                    box = {}

                    def s_fn():
                        pr = prot2[0] % 2
                        prot2[0] += 1
                        box["pr"] = pr
                        for idx, k in enumerate(kts):
                            bk = 2 * pr + idx
                            S.op("pe", lambda e: e.matmul(PS[bk][:, c0:CH], lhsT=kt_[rows, k * 128:(k + 1) * 128],
                                                          rhs=qt[rows, qc * CH + c0:(qc + 1) * CH], start=True, stop=True),
                                 reads=[qb, kb], writes=[PB[bk]])

                    def post_fn():
                        pr = box["pr"]
                        pt, ptb = pbufs[prot[0] % 3]
                        prot[0] += 1
                        nk = len(kts)
                        banks = [PB[2 * pr + idx] for idx in range(nk)]
                        if nk == 2 and os.environ.get("PAIR_EXP", "1") == "1":
                            S.op("act", lambda e: e.activation(out=pt[:, 0:2 * CH], in_=PSP[pr][:, :], func=AF.Exp,
                                                               scale=0.125), reads=banks, writes=[ptb])
                        elif nk == 2:
                            for idx in range(2):
                                S.op("act", lambda e: e.activation(out=pt[:, idx * CH:(idx + 1) * CH], in_=PS[2 * pr + idx][:, :], func=AF.Exp,
                                                                   scale=0.125), reads=banks, writes=[ptb])
                        else:
                            S.op("act", lambda e: e.activation(out=pt[:, c0:CH], in_=PS[2 * pr][:, c0:CH], func=AF.Exp,
                                                               scale=0.125), reads=banks, writes=[ptb])
                            if j >= 0:
                                S.op("pool", lambda e: e.affine_select(out=pt[:, c0:c0 + 128], in_=pt[:, c0:c0 + 128],
                                                                       pattern=[[1, 128]], compare_op=ALU.is_ge, fill=0.0,
                                                                       base=0, channel_multiplier=-1),
                                     reads=[ptb], writes=[ptb])
                        for idx, k in enumerate(kts):
                            o0 = idx * CH
                            S.op("pe", lambda e: e.matmul(PS[po][:, c0:CH], lhsT=vall[:, k, h * 128:(h + 1) * 128],
                                                          rhs=pt[:, o0 + c0:o0 + CH], start=(k == 0), stop=(k == nkt - 1)),
                                 reads=[VAB, ptb], writes=[PB[po]])
                            S.op("pe", lambda e: e.matmul(PS[pl][:, c0:CH], lhsT=ones_bf[:, :],
                                                          rhs=pt[:, o0 + c0:o0 + CH], start=(k == 0), stop=(k == nkt - 1)),
                                 reads=[ONB, ptb], writes=[PB[pl]])
                        if kt != nkt - 1:
                            return
                        r1, R1B = oset["r1"]
                        o1, O1B = oset["o1"]
                        o2, O2B = oset["o2"]
                        sq, SQB = oset["sq"]
                        om, OMB = (o1, O1B) if m == 0 else (o2, O2B)
                        S.op("dve", lambda e: e.reciprocal(out=r1[:], in_=PS[pl][:, :]), reads=[PB[pl], R1B], writes=[R1B])
                        S.op("dve", lambda e: e.tensor_tensor(out=om[:], in0=PS[po][:, :], in1=r1[:], op=ALU.mult),
                             reads=[PB[po], R1B, OMB], writes=[OMB])
                        if m == 0:
                            return
                        S.op("dve", lambda e: e.scalar_tensor_tensor(out=o1[:], in0=o2[:], scalar=lams[:, 5:6], in1=o1[:],
                                                                     op0=ALU.mult, op1=ALU.add),
                             reads=[O2B, O1B, LAMS], writes=[O1B])
                        S.op("pool", lambda e: e.tensor_tensor(out=sq[:], in0=o1[:], in1=o1[:], op=ALU.mult),
                             reads=[O1B, SQB], writes=[SQB])

                        def final():
                            tok = slice(qc * CH, (qc + 1) * CH)
                            S.op("pe", lambda e: e.matmul(PS[7][:, :], lhsT=ones_f[:, :], rhs=sq[:], start=True, stop=True),
                                 reads=[ONF, SQB], writes=[PB[7]])
                            S.op("dve", lambda e: e.tensor_scalar(out=o2[:], in0=PS[7][:, :], scalar1=1.0 / 128.0, scalar2=LN_EPS,
                                                                  op0=ALU.mult, op1=ALU.add), reads=[PB[7], O2B], writes=[O2B])
                            S.op("act", lambda e: e.activation(out=o2[:], in_=o2[:], func=AF.Ln), reads=[O2B], writes=[O2B])
                            S.op("act", lambda e: e.activation(out=o2[:], in_=o2[:], func=AF.Exp, scale=-0.5), reads=[O2B], writes=[O2B])
                            yt, ytb = ysg[gi % 2]
                            S.op("dve", lambda e: e.scalar_tensor_tensor(out=yt[:], in0=o1[:], scalar=lams[:, 6:7], in1=o2[:],
                                                                         op0=ALU.mult, op1=ALU.mult),
                                 reads=[O1B, O2B, LAMS], writes=[ytb])
                            S.dma("sp", yT[512 + 128 * h:512 + 128 * (h + 1), tok], yt[:], reads=[ytb])
                        st.defer(3, final)

                    return s_fn, post_fn

                load_qk(0)
                gi = 0
                for h in range(4):
                    if h + 1 < 4:
                        load_qk(h + 1)
                    for qc in range(NCH):
                        nkt = 4 * qc + 4
                        oset = osets[gi % 2]
                        for m in range(2):
                            for kt in range(0, 4 * qc, 2):
                                st.push(*mk_step(h, qc, m, (kt, kt + 1), nkt, oset, gi))
                            for kt in range(4 * qc, nkt):
                                st.push(*mk_step(h, qc, m, (kt,), nkt, oset, gi))
                        gi += 1
                st.flush()
                S.barrier()
            if stop_phase <= 2:
                break

            with ExitStack() as ph:
                hT, HTB = T(ph, "hT", [128, 256], BF16)
                kccT = [T(ph, "kccT%d" % g, [64, 256], BF16) for g in range(2)]
                vcE = [T(ph, "vcE%d" % g, [128, 2, 128], BF16) for g in range(2)]
                ovl1, OVB = T(ph, "ovl1_sb", [128, 2, 65], BF16)
                S.dma("pool", ovl1[:], ovl1_d[:], writes=[OVB])
                gsel, GSB = T(ph, "gsel_sb", [24, 24 * 64], BF16)
                S.dma("pool", gsel[:], gsel_d[:], writes=[GSB])
                ng_sb, NGB = T(ph, "ng_sb", [24, S_LEN], BF16)
                S.dma("sp", ng_sb[:], ngT[:], writes=[NGB])
                S.op("pool", lambda e: e.memset(hT[:, 255:256], 0.0), writes=[HTB])
                for g in range(2):
                    S.op("pool", lambda e, g=g: e.memset(vcE[g][0][:], 1.0), writes=[vcE[g][1]])
                with ExitStack() as sub:
                    kc_sb, KCB = T(sub, "kc_sb", [128, S_LEN], BF16)
                    vc_sb, VCB = T(sub, "vc_sb", [128, S_LEN], BF16)
                    S.dma("sp", kc_sb[:], kcT[:], writes=[KCB])
                    S.dma("sp", vc_sb[:], vcT[:], writes=[VCB])
                    w1s, W1B = T(sub, "w1s", [128, 32, 128], BF16)
                    w2s, W2B = T(sub, "w2s", [128, 64], BF16)
                    pos_sb, POSB = T(sub, "pos_sb", [128, 32], BF16)
                    bcol, BCB = T(sub, "bcol", [128, 1], F32)
                    for which, src, srcb in ((0, kc_sb, KCB), (1, vc_sb, VCB)):
                        for half in range(2):
                            S.dma("pool", w1s[half * 64:(half + 1) * 64, :, :],
                                  phi_w1[l, which].rearrange("(l d) h -> d l h", d=64), writes=[W1B])
                        S.dma("pool", w2s[:], phi_w2[l, which], writes=[W2B])
                        S.dma("pool", pos_sb[:], posT[l, which], writes=[POSB])
                        for g in range(2):
                            rows = slice(g * 64, (g + 1) * 64)
                            pbias = next_ps(0, 4)
                            for li in range(32):
                                S.op("pe", lambda e, li=li: e.matmul(PS[pbias][:, 0:1], lhsT=w1s[rows, li, :], rhs=pos_sb[rows, li:li + 1],
                                                                     start=(li == 0), stop=(li == 31)),
                                     reads=[W1B, POSB], writes=[PB[pbias]])
                            S.op("act", lambda e: e.activation(out=bcol[:], in_=PS[pbias][:, 0:1], func=AF.Copy),
                                 reads=[PB[pbias]], writes=[BCB])
                            ph_ = next_ps(0, 4)
                            for li in range(32):
                                S.op("pe", lambda e, li=li: e.matmul(PS[ph_][:, 0:255], lhsT=w1s[rows, li, :],
                                                                     rhs=src[rows, li:li + 16 * 254 + 1:16],
                                                                     start=(li == 0), stop=(li == 31)),
                                     reads=[W1B, srcb], writes=[PB[ph_]])
                            S.op("act", lambda e: e.activation(out=hT[:, 0:255], in_=PS[ph_][:, 0:255], func=AF.Gelu, bias=bcol[:, 0:1]),
                                 reads=[PB[ph_], BCB], writes=[HTB])
                            if which == 0:
                                po_ = next_ps(0, 4)
                                S.op("pe", lambda e: e.matmul(PS[po_][0:64, 0:256], lhsT=w2s[:, :], rhs=hT[:, :], start=True, stop=True),
                                     reads=[W2B, HTB], writes=[PB[po_]])
                                S.op("act", lambda e: e.activation(out=kccT[g][0][:], in_=PS[po_][0:64, 0:256], func=AF.Copy),
                                     reads=[PB[po_]], writes=[kccT[g][1]])
                            else:
                                for nt in range(2):
                                    po_ = next_ps(0, 4)
                                    S.op("pe", lambda e: e.matmul(PS[po_][:, 0:64], lhsT=hT[:, nt * 128:(nt + 1) * 128], rhs=w2s[:, :],
                                                                  start=True, stop=True),
                                         reads=[W2B, HTB], writes=[PB[po_]])
                                    S.op("act", lambda e: e.activation(out=vcE[g][0][:, nt, 0:64], in_=PS[po_][:, 0:64], func=AF.Copy),
                                         reads=[PB[po_]], writes=[vcE[g][1]])
                    S.barrier()
                cbig, CBB = T(ph, "cbig_sb", [128, 32, 64], BF16)
                cfut, CFB = T(ph, "cfut_sb", [128, 32, 64], BF16)
                S.dma("pool", cbig[:], cbig_d[:], writes=[CBB])
                S.dma("pool", cfut[:], cfut_d[:], writes=[CFB])
                impacc = ph.enter_context(nc.sbuf_tensor("impacc%d" % l, [128, 32, 64], F32))
                IMB = [Buf() for _ in range(NCH)]
                impm, IPMB = T(ph, "impm", [128, 64], F32)
                imp2, IP2B = T(ph, "imp2", [128, 64], F32)
                v8a, V8AB = T(ph, "v8a", [128, 8], F32)
                v8b, V8BB = T(ph, "v8b", [128, 8], F32)
                rl4, RL4B = T(ph, "rl4", [128, 4], F32)
                IMQ = [Buf() for _ in range(32)]
                biasq, BQB = T(ph, "biasq", [128, 4, 64], BF16)
                biasT, BTB = T(ph, "biasT", [64, S_LEN], BF16)
                ksE, KSB = T(ph, "ksE", [128, S_LEN], BF16)
                kwg, KWB = T(ph, "kwg", [64, S_LEN], BF16)
                vsE, VSB = T(ph, "vsE", [128, 32, 128], BF16)
                vwE, VWB = T(ph, "vwE", [128, 32, 128], BF16)
                qaug = [T(ph, "qaug%d" % i, [128, S_LEN], BF16) for i in range(2)]
                pnset = [[T(ph, "pn%d_%d" % (s_, i), [128, CH], BF16) for i in range(2)] for s_ in range(2)]
                pt3 = [T(ph, "pt3_%d" % i, [128, CH], BF16) for i in range(5)]
                rLt, RLTB = T(ph, "rLt", [64, CH], F32)
                tA, TAB = T(ph, "tA", [64, CH], F32)
                acc, ACCB = T(ph, "acc", [64, CH], F32)
                ycs = [T(ph, "ycs%d" % i, [64, CH], BF16) for i in range(2)]
                prot = [0]
                qload = [0]
                urot = [0]
                grot = [0]
                st = Stream(3)

                def mk_cmp(g, qa, qc, nt, pn, last_fn):
                    box = {}

                    def s_fn():
                        ps_s = next_ps(0, 4)
                        box["ps"] = ps_s
                        S.op("pe", lambda e: e.matmul(PS[ps_s][:, :], lhsT=kccT[g][0][:, nt * 128:(nt + 1) * 128],
                                                      rhs=qa[0][0:64, qc * CH:(qc + 1) * CH], start=True, stop=True),
                             reads=[kccT[g][1], qa[1]], writes=[PB[ps_s]])

                    def post_fn():
                        ps_s = box["ps"]
                        p_t, p_b = pn[nt]
                        S.op("act", lambda e: e.activation(out=p_t[:], in_=PS[ps_s][:, :], func=AF.Exp, scale=0.125),
                             reads=[PB[ps_s]], writes=[p_b])
                        S.op("pool", lambda e: e.affine_select(out=p_t[:], in_=p_t[:], pattern=[[1, CH]], compare_op=ALU.is_ge,
                                                               fill=0.0, base=qc * CH - 2048 * nt - 31, channel_multiplier=-16),
                             reads=[p_b], writes=[p_b])
                        if last_fn is not None:
                            last_fn()

                    return s_fn, post_fn

                def mk_att(qa, qc, kt, c0, c1, klhs, krows, kbuf, vE, vbuf, obank, first, last, mask, after):
                    box = {}

                    def s_fn():
                        ps_s = next_ps(0, 4)
                        box["ps"] = ps_s
                        S.op("pe", lambda e: e.matmul(PS[ps_s][:, c0:c1], lhsT=klhs[krows, kt * 128:(kt + 1) * 128],
                                                      rhs=qa[0][krows, qc * CH + c0:qc * CH + c1], start=True, stop=True),
                             reads=[kbuf, qa[1]], writes=[PB[ps_s]])

                    def post_fn():
                        ps_s = box["ps"]
                        p_t, p_b = pt3[prot[0] % 5]
                        prot[0] += 1
                        S.op("act", lambda e: e.activation(out=p_t[:, c0:c1], in_=PS[ps_s][:, c0:c1], func=AF.Exp, scale=0.125),
                             reads=[PB[ps_s]], writes=[p_b])
                        if mask == "lo":
                            S.op("pool", lambda e: e.affine_select(out=p_t[:, c0:c0 + 128], in_=p_t[:, c0:c0 + 128],
                                                                   pattern=[[1, 128]], compare_op=ALU.is_ge, fill=0.0,
                                                                   base=0, channel_multiplier=-1),
                                 reads=[p_b], writes=[p_b])
                        elif mask == "hi":
                            S.op("pool", lambda e: e.affine_select(out=p_t[:, c1 - 128:c1], in_=p_t[:, c1 - 128:c1],
                                                                   pattern=[[-1, 128]], compare_op=ALU.is_gt, fill=0.0,
                                                                   base=0, channel_multiplier=1),
                                 reads=[p_b], writes=[p_b])
                        S.op("pe", lambda e: e.matmul(PS[obank][:, c0:c1], lhsT=vE[:, kt, :], rhs=p_t[:, c0:c1],
                                                      start=first, stop=last),
                             reads=[vbuf, p_b], writes=[PB[obank]])
                        if after is not None:
                            after()

                    return s_fn, post_fn

                for g in range(2):
                    S.dma("sp", ksE[0:64, :], ksT[g * 64:(g + 1) * 64, :], writes=[KSB])
                    S.dma("pool", ksE[64:128, :], expand_d[:], writes=[KSB])
                    S.dma("sp", kwg[:], kwT[g * 64:(g + 1) * 64, :], writes=[KWB])
                    S.dma("sp", vsE[:], vaug[g].rearrange("(kt p) c -> p kt c", p=128), writes=[VSB])
                    S.dma("sp", vwE[:], vaug[2 + g].rearrange("(kt p) c -> p kt c", p=128), writes=[VWB])
                    for hp in range(4):
                        h = 4 * g + hp
                        qa = qaug[qload[0] % 2]
                        qload[0] += 1
                        S.dma("sp", qa[0][0:64, :], qnT[h * 64:(h + 1) * 64, :], writes=[qa[1]])
                        for qc in range(NCH):
                            nts = [0] if qc < 4 else [0, 1]
                            pn = pnset[(hp * NCH + qc) % 2]

                            def imp_fn(qc=qc, nts=nts, pn=pn, hp=hp):
                                ub = 4 + urot[0] % 4
                                urot[0] += 1
                                for qs in range(4):
                                    for nt in nts:
                                        S.op("pe", lambda e: e.matmul(PS[ub][:, qs * 128:qs * 128 + 65], lhsT=pn[nt][0][:, qs * 128:(qs + 1) * 128],
                                                                      rhs=ovl1[:, nt, :], start=(nt == nts[0]), stop=(nt == nts[-1])),
                                             reads=[pn[nt][1], OVB], writes=[PB[ub]])
                                S.op("dve", lambda e: e.tensor_scalar(out=rl4[:], in0=PS[ub][:, 64:512:128], scalar1=1e-30,
                                                                      scalar2=None, op0=ALU.add), reads=[PB[ub], RL4B], writes=[RL4B])
                                S.op("dve", lambda e: e.reciprocal(out=rl4[:], in_=rl4[:]), reads=[RL4B], writes=[RL4B])
                                for qs in range(4):
                                    qt = qc * 4 + qs
                                    if hp == 0:
                                        S.op("dve", lambda e: e.tensor_scalar(out=impacc[:, qt, :], in0=PS[ub][:, qs * 128:qs * 128 + 64],
                                                                              scalar1=rl4[:, qs:qs + 1], scalar2=None, op0=ALU.mult),
                                             reads=[PB[ub], RL4B], writes=[IMQ[qt]])
                                    else:
                                        S.op("dve", lambda e: e.scalar_tensor_tensor(out=impacc[:, qt, :], in0=PS[ub][:, qs * 128:qs * 128 + 64],
                                                                                     scalar=rl4[:, qs:qs + 1], in1=impacc[:, qt, :],
                                                                                     op0=ALU.mult, op1=ALU.add),
                                             reads=[PB[ub], RL4B, IMQ[qt]], writes=[IMQ[qt]])

                            for nt in nts:
                                st.push(*mk_cmp(g, qa, qc, nt, pn, imp_fn if nt == nts[-1] else None))
                    st.flush()
                    for qc in range(NCH):
                        for qs in range(4):
                            qt = qc * 4 + qs
                            S.op("dve", lambda e: e.tensor_tensor(out=impm[:], in0=impacc[:, qt, :], in1=cbig[:, qt, :], op=ALU.max),
                                 reads=[IMQ[qt], CBB, IPMB], writes=[IPMB])
                            S.op("dve", lambda e: e.tensor_tensor(out=impm[:], in0=impm[:], in1=cfut[:, qt, :], op=ALU.min),
                                 reads=[IPMB, CFB], writes=[IPMB])
                            S.op("dve", lambda e: e.max(out=v8a[:], in_=impm[:]), reads=[IPMB], writes=[V8AB])
                            S.op("dve", lambda e: e.match_replace(out=imp2[:], in_to_replace=v8a[:], in_values=impm[:], imm_value=-3.0e38),
                                 reads=[IPMB, V8AB], writes=[IP2B])
                            S.op("dve", lambda e: e.max(out=v8b[:], in_=imp2[:]), reads=[IP2B], writes=[V8BB])
                            S.op("dve", lambda e: e.tensor_scalar(out=biasq[:, qs, :], in0=impm[:], scalar1=v8b[:, 7:8], scalar2=NEGB,
                                                                  op0=ALU.is_lt, op1=ALU.mult),
                                 reads=[IPMB, V8BB], writes=[BQB])
                        for qs in range(4):
                            S.op("pe", lambda e: e.matmul(PS[4][0:64, qs * 128:(qs + 1) * 128], lhsT=biasq[:, qs, :], rhs=ident[:, :],
                                                          start=True, stop=True), reads=[BQB, IDB], writes=[PB[4]])
                        S.op("act", lambda e: e.activation(out=biasT[:, qc * CH:(qc + 1) * CH], in_=PS[4][0:64, :], func=AF.Copy),
                             reads=[PB[4]], writes=[BTB])
                    for hp in range(4):
                        h = 4 * g + hp
                        qa = qaug[qload[0] % 2]
                        qload[0] += 1
                        S.dma("sp", qa[0][0:64, :], qnT[h * 64:(h + 1) * 64, :], writes=[qa[1]])
                        S.dma("sp", qa[0][64:128, :], biasT[:, :], reads=[BTB], writes=[qa[1]])
                        for qc in range(NCH):
                            nts = [0] if qc < 4 else [0, 1]
                            pn = pnset[(hp * NCH + qc) % 2]
                            ci = hp * NCH + qc

                            def epi(br, pbk, which, h=h, qc=qc, ci=ci):
                                def fn():
                                    tok = slice(qc * CH, (qc + 1) * CH)
                                    col = h * 3 + br
                                    gb = 7
                                    grot[0] += 1
                                    yc, ycb = ycs[ci % 2]
                                    S.op("pe", lambda e: e.matmul(PS[gb][0:64, :], lhsT=gsel[0:24, col * 64:(col + 1) * 64], rhs=ng_sb[0:24, tok],
                                                                  start=True, stop=True), reads=[GSB, NGB], writes=[PB[gb]])
                                    if br == 0:
                                        S.op("dve", lambda e: e.tensor_scalar(out=rLt[:], in0=PS[pbk][64:128, :], scalar1=1e-30, scalar2=None,
                                                                              op0=ALU.add), reads=[PB[pbk], RLTB], writes=[RLTB])
                                        S.op("dve", lambda e: e.reciprocal(out=rLt[:], in_=rLt[:]), reads=[RLTB], writes=[RLTB])
                                    else:
                                        S.op("dve", lambda e: e.reciprocal(out=rLt[:], in_=PS[pbk][64:128, :]), reads=[PB[pbk], RLTB], writes=[RLTB])
                                    S.op("dve", lambda e: e.tensor_tensor(out=tA[:], in0=PS[pbk][0:64, :], in1=rLt[:], op=ALU.mult),
                                         reads=[PB[pbk], RLTB, TAB], writes=[TAB])
                                    if which == 0:
                                        S.op("dve", lambda e: e.tensor_tensor(out=acc[:], in0=PS[gb][0:64, :], in1=tA[:], op=ALU.mult),
                                             reads=[PB[gb], TAB, ACCB], writes=[ACCB])
                                    else:
                                        S.op("dve", lambda e: e.tensor_tensor(out=tA[:], in0=PS[gb][0:64, :], in1=tA[:], op=ALU.mult),
                                             reads=[PB[gb], TAB], writes=[TAB])
                                        if which == 1:
                                            S.op("pool", lambda e: e.tensor_tensor(out=acc[:], in0=acc[:], in1=tA[:], op=ALU.add),
                                                 reads=[TAB, ACCB], writes=[ACCB])
                                        else:
                                            S.op("pool", lambda e: e.tensor_tensor(out=yc[:], in0=acc[:], in1=tA[:], op=ALU.add),
                                                 reads=[TAB, ACCB], writes=[ycb])
                                            S.dma("sp", yT[1024 + 64 * h:1024 + 64 * (h + 1), tok], yc[:], reads=[ycb])
                                return lambda: st.defer(2, fn)

                            def cmp_o(nts=nts, pn=pn, g=g):
                                for nt in nts:
                                    S.op("pe", lambda e: e.matmul(PS[6][:, :], lhsT=vcE[g][0][:, nt, :], rhs=pn[nt][0][:, :],
                                                                  start=(nt == nts[0]), stop=(nt == nts[-1])),
                                         reads=[vcE[g][1], pn[nt][1]], writes=[PB[6]])
                                epi(0, 6, 0)()
                            for nt in nts:
                                st.push(*mk_cmp(g, qa, qc, nt, pn, cmp_o if nt == nts[-1] else None))
                            jls = [jl for jl in (-1, -4, -3, -2, 0, 1, 2, 3) if 4 * qc + jl >= 0]
                            for wi, jl in enumerate(jls):
                                kt = 4 * qc + jl
                                if jl >= 0:
                                    c0, c1, mk = 128 * jl, CH, "lo"
                                else:
                                    c0, c1, mk = 0, 128 * (jl + 5), "hi"
                                lastw = (wi == len(jls) - 1)
                                st.push(*mk_att(qa, qc, kt, c0, c1, kwg, slice(0, 64), KWB, vwE, VWB, 5, wi == 0, lastw, mk,
                                                epi(2, 5, 1) if lastw else None))
                            nkt = 4 * qc + 4
                            for kt in range(nkt):
                                j = kt - 4 * qc
                                c0 = 128 * j if j > 0 else 0
                                lasts = (kt == nkt - 1)
                                st.push(*mk_att(qa, qc, kt, c0, CH, ksE, slice(0, 128), KSB, vsE, VSB, 4, kt == 0, lasts,
                                                "lo" if j >= 0 else None, epi(1, 4, 2) if lasts else None))
                    st.flush()
                S.barrier()
            if stop_phase <= 3:
                break

            def layer_norm(ph, r, RB, c, goff, boff, dst, tmps):
                (sqt, mean, MEB, msq, MSB, tt, outf, rbs) = tmps
                tok = slice(c * CH, (c + 1) * CH)
                for ot in range(8):
                    rb_t, rb_b = rbs[ot % 2]
                    S.op("pool", lambda e, ot=ot: e.tensor_copy(out=rb_t[:], in_=r[:, ot, :]), reads=[RB], writes=[rb_b])
                    S.op("pe", lambda e, ot=ot: e.matmul(PS[4][:, :], lhsT=ones_bf[:, :], rhs=rb_t[:], start=(ot == 0), stop=(ot == 7)),
                         reads=[ONB, rb_b], writes=[PB[4]])
                    sq_t, sq_b = sqt[ot % 2]
                    S.op("act", lambda e, ot=ot: e.activation(out=sq_t[:], in_=r[:, ot, :], func=AF.Square), reads=[RB], writes=[sq_b])
                    S.op("pe", lambda e, ot=ot: e.matmul(PS[5][:, :], lhsT=ones_bf[:, :], rhs=sq_t[:], start=(ot == 0), stop=(ot == 7)),
                         reads=[ONB, sq_b], writes=[PB[5]])
                S.op("act", lambda e: e.activation(out=mean[:], in_=PS[4][:, :], func=AF.Copy, scale=1.0 / 1024.0),
                     reads=[PB[4]], writes=[MEB])
                S.op("act", lambda e: e.activation(out=msq[:], in_=mean[:], func=AF.Square), reads=[MEB], writes=[MSB])
                S.op("dve", lambda e: e.scalar_tensor_tensor(out=msq[:], in0=PS[5][:, :], scalar=1.0 / 1024.0, in1=msq[:],
                                                             op0=ALU.mult, op1=ALU.subtract), reads=[PB[5], MSB], writes=[MSB])
                S.op("dve", lambda e: e.tensor_scalar(out=msq[:], in0=msq[:], scalar1=LN_EPS, scalar2=None, op0=ALU.add),
                     reads=[MSB], writes=[MSB])
                S.op("act", lambda e: e.activation(out=msq[:], in_=msq[:], func=AF.Ln), reads=[MSB], writes=[MSB])
                S.op("act", lambda e: e.activation(out=msq[:], in_=msq[:], func=AF.Exp, scale=-0.5), reads=[MSB], writes=[MSB])
                for ot in range(8):
                    t_t, t_b = tt[ot % 2]
                    o_t, o_b = outf[ot % 2]
                    S.op("dve", lambda e, ot=ot: e.tensor_tensor(out=t_t[:], in0=r[:, ot, :], in1=mean[:], op=ALU.subtract),
                         reads=[RB, MEB], writes=[t_b])
                    S.op("dve", lambda e: e.tensor_tensor(out=t_t[:], in0=t_t[:], in1=msq[:], op=ALU.mult),
                         reads=[t_b, MSB], writes=[t_b])
                    S.op("act", lambda e, ot=ot: e.activation(out=o_t[:], in_=t_t[:], func=AF.Identity,
                                                              scale=pv[:, goff + ot:goff + ot + 1], bias=pv[:, boff + ot:boff + ot + 1]),
                         reads=[t_b, PVB], writes=[o_b])
                    S.dma("sp", dst[ot * 128:(ot + 1) * 128, tok], o_t[:], reads=[o_b])
                    S.op("pool", lambda e, ot=ot: e.tensor_copy(out=xT_bf[:, ot, tok], in_=o_t[:]),
                         reads=[o_b], writes=[XB[c]])

            def ln_tmps(ph):
                sqt = [T(ph, "sqt%d" % i, [128, CH], BF16) for i in range(2)]
                rbs = [T(ph, "rbs%d" % i, [128, CH], BF16) for i in range(2)]
                mean, MEB = T(ph, "mean", [128, CH], F32)
                msq, MSB = T(ph, "msq", [128, CH], F32)
                tt = [T(ph, "lt%d" % i, [128, CH], F32) for i in range(2)]
                outf = [T(ph, "lo%d" % i, [128, CH], F32) for i in range(2)]
                return (sqt, mean, MEB, msq, MSB, tt, outf, rbs)

            with ExitStack() as ph:
                wbr, WBRB = T(ph, "wbr", [128, 12, 1024], BF16)
                wo, WOB = T(ph, "wo", [128, 8, 1024], BF16)
                WBRL = [Buf() for _ in range(3)]
                for br in range(3):
                    S.dma("pool", wbr[:, br * 4:(br + 1) * 4, :], w_branch[l, br].rearrange("(kc p) m -> p kc m", p=128), writes=[WBRL[br]])
                S.dma("pool", wo[:], w_o[l].rearrange("(kc p) m -> p kc m", p=128), writes=[WOB])
                ych = [T(ph, "ych%d" % i, [128, 12, CH], BF16) for i in range(2)]
                gtl = [T(ph, "gtl%d" % i, [128, 3, CH], BF16) for i in range(4)]
                xo = [T(ph, "xo%d" % i, [128, CH], F32) for i in range(4)]
                mg_v = mgT.rearrange("(br ot p) t -> ot p br t", br=3, ot=8)
                merged, MGB = T(ph, "merged", [128, 8, CH], BF16)
                r, RB = T(ph, "r", [128, 8, CH], F32)
                tmp, TMB = T(ph, "tmp", [128, CH], F32)
                macc, MAB = T(ph, "macc", [128, CH], F32)
                tmps = ln_tmps(ph)

                def load_y(c):
                    tok = slice(c * CH, (c + 1) * CH)
                    S.dma("sp", ych[c % 2][0][:], yT[:, tok].rearrange("(kc p) t -> p kc t", p=128), writes=[ych[c % 2][1]])

                def load_g(idx):
                    if idx >= NCH * 8:
                        return
                    c_, ot_ = idx // 8, idx % 8
                    g_t, g_b = gtl[idx % 4]
                    S.dma("sp", g_t[:], mg_v[ot_][:, :, c_ * CH:(c_ + 1) * CH], writes=[g_b])

                def load_x(c_, ot_):
                    x_t, x_b = xo[ot_ % 4]
                    S.dma("sp", x_t[:], xr_in[ot_ * 128:(ot_ + 1) * 128, c_ * CH:(c_ + 1) * CH], writes=[x_b])

                load_y(0)
                load_g(0)
                load_g(1)
                for c in range(NCH):
                    if c + 1 < NCH:
                        load_y(c + 1)
                    y_t, y_b = ych[c % 2]
                    for ot in range(8):
                        load_g(c * 8 + ot + 2)
                        if ot == 5:
                            load_x(c, 0)
                            load_x(c, 1)
                        g_t, g_b = gtl[(c * 8 + ot) % 4]
                        for br in range(3):
                            pb = next_ps6()
                            for kc in range(4):
                                S.op("pe", lambda e, kc=kc: e.matmul(PS[pb][:, :], lhsT=wbr[:, br * 4 + kc, ot * 128:(ot + 1) * 128],
                                                                     rhs=y_t[:, br * 4 + kc, :], start=(kc == 0), stop=(kc == 3)),
                                     reads=[WBRL[br], y_b], writes=[PB[pb]])
                            if br == 0:
                                S.op("dve", lambda e: e.tensor_tensor(out=macc[:], in0=PS[pb][:, :], in1=g_t[:, br, :], op=ALU.mult),
                                     reads=[PB[pb], g_b, MAB], writes=[MAB])
                            else:
                                S.op("dve", lambda e: e.tensor_tensor(out=tmp[:], in0=PS[pb][:, :], in1=g_t[:, br, :], op=ALU.mult),
                                     reads=[PB[pb], g_b, TMB], writes=[TMB])
                                if br == 1:
                                    S.op("pool", lambda e: e.tensor_tensor(out=macc[:], in0=macc[:], in1=tmp[:], op=ALU.add),
                                         reads=[TMB, MAB], writes=[MAB])
                                else:
                                    S.op("pool", lambda e: e.tensor_tensor(out=merged[:, ot, :], in0=macc[:], in1=tmp[:], op=ALU.add),
                                         reads=[TMB, MAB], writes=[MGB])
                    for ot in range(8):
                        if ot + 2 < 8:
                            load_x(c, ot + 2)
                        x_t, x_b = xo[ot % 4]
                        pb = next_ps6()
                        for kc in range(8):
                            S.op("pe", lambda e, kc=kc: e.matmul(PS[pb][:, :], lhsT=wo[:, kc, ot * 128:(ot + 1) * 128], rhs=merged[:, kc, :],
                                                                 start=(kc == 0), stop=(kc == 7)),
                                 reads=[WOB, MGB], writes=[PB[pb]])
                        S.op("dve", lambda e: e.scalar_tensor_tensor(out=r[:, ot, :], in0=x_t[:], scalar=ALPHA, in1=PS[pb][:, :],
                                                                     op0=ALU.mult, op1=ALU.add),
                             reads=[x_b, PB[pb]], writes=[RB])
                    layer_norm(ph, r, RB, c, PV_G1, PV_B1, xr_mid, tmps)
                S.barrier()
            if stop_phase <= 4:
                break

            p5o = ExitStack()
            wd, WDB = T(p5o, "wd", [128, 22, 1024], BF16)
            for q4 in range(2):
                S.dma("pool", wd[:, q4 * 11:(q4 + 1) * 11, :],
                      w_down[l, q4 * 1408:(q4 + 1) * 1408, :].rearrange("(kc p) m -> p kc m", p=128), writes=[WDB])
            with ExitStack() as ph:
                wu = [T(ph, "wu%d" % i, [128, 8, 256], BF16) for i in range(2)]
                hb = [[T(ph, "hb%d_%d" % (w_, i), [128, 2 + CH], F32) for i in range(2)] for w_ in range(2)]
                tcv = [[T(ph, "tcv%d_%d" % (w_, i), [128, CH], F32) for i in range(2)] for w_ in range(2)]
                gls = [T(ph, "gl%d" % i, [128, CH], F32) for i in range(2)]
                gsg = [T(ph, "gsg%d" % i, [128, CH], BF16) for i in range(2)]

                def load_wu(ft):
                    w_t, w_b = wu[ft % 2]
                    S.dma("pool", w_t[:, :, 0:128], w_up[l, :, ft * 128:(ft + 1) * 128].rearrange("(kc p) m -> p kc m", p=128), writes=[w_b])
                    S.dma("pool", w_t[:, :, 128:256], w_up[l, :, 2816 + ft * 128:2816 + (ft + 1) * 128].rearrange("(kc p) m -> p kc m", p=128),
                          writes=[w_b])

                def tail5(ft, c, par):
                    tok = slice(c * CH, (c + 1) * CH)
                    gl, GLB = gls[par]
                    g_t, g_b = gsg[par]
                    S.op("act", lambda e: e.activation(out=gl[:], in_=tcv[0][par][0][:], func=AF.Gelu), reads=[tcv[0][par][1], GLB], writes=[GLB])
                    S.op("pool", lambda e: e.tensor_tensor(out=g_t[:], in0=gl[:], in1=tcv[1][par][0][:], op=ALU.mult),
                         reads=[GLB, tcv[1][par][1]], writes=[g_b])
                    S.dma("sp", gT[ft * 128:(ft + 1) * 128, tok], g_t[:], reads=[g_b])

                load_wu(0)
                pend5 = None
                it5 = 0
                for ft in range(22):
                    if ft + 1 < 22:
                        load_wu(ft + 1)
                    w_t, w_b = wu[ft % 2]
                    for c in range(NCH):
                        tok = slice(c * CH, (c + 1) * CH)
                        par = it5 % 2
                        it5 += 1
                        for which in range(2):
                            pb = next_ps6((0, 1, 2, 3, 4, 5, 6, 7))
                            for kc in range(8):
                                S.op("pe", lambda e, kc=kc: e.matmul(PS[pb][:, :], lhsT=w_t[:, kc, which * 128:(which + 1) * 128],
                                                                     rhs=xT_bf[:, kc, tok], start=(kc == 0), stop=(kc == 7)),
                                     reads=[w_b, XB[c]], writes=[PB[pb]])
                            h_t, h_b = hb[which][c % 2]
                            p_t, p_b = hb[which][(c + 1) % 2]
                            col = ft + 22 * which
                            fw = lambda k: pv[:, PV_FW + col * 3 + k:PV_FW + col * 3 + k + 1]
                            t_t, t_b = tcv[which][par]
                            S.op("act", lambda e: e.activation(out=h_t[:, 2:2 + CH], in_=PS[pb][:, :], func=AF.Copy),
                                 reads=[PB[pb]], writes=[h_b])
                            S.op("act", lambda e: e.activation(out=t_t[:], in_=h_t[:, 2:2 + CH], func=AF.Identity, scale=fw(2),
                                                               bias=pv[:, PV_FB + col:PV_FB + col + 1]),
                                 reads=[h_b, PVB, t_b], writes=[t_b])
                            if c == 0:
                                S.op("pool", lambda e: e.memset(h_t[:, 0:2], 0.0), writes=[h_b])
                            else:
                                S.op("pool", lambda e: e.tensor_copy(out=h_t[:, 0:2], in_=p_t[:, CH:CH + 2]), reads=[p_b], writes=[h_b])
                            S.op("dve", lambda e: e.scalar_tensor_tensor(out=t_t[:], in0=h_t[:, 0:CH], scalar=fw(0), in1=t_t[:],
                                                                         op0=ALU.mult, op1=ALU.add), reads=[h_b, PVB, t_b], writes=[t_b])
                            S.op("dve", lambda e: e.scalar_tensor_tensor(out=t_t[:], in0=h_t[:, 1:CH + 1], scalar=fw(1), in1=t_t[:],
                                                                         op0=ALU.mult, op1=ALU.add), reads=[h_b, PVB, t_b], writes=[t_b])
                        if pend5 is not None:
                            tail5(*pend5)
                        pend5 = (ft, c, par)
                tail5(*pend5)
                S.barrier()

            with ExitStack() as ph:
                gch = [T(ph, "gdch%d" % i, [128, 22, CH], BF16) for i in range(2)]
                xo = [T(ph, "xdo%d" % i, [128, CH], F32) for i in range(4)]
                r, RB = T(ph, "r2", [128, 8, CH], F32)
                tmps = ln_tmps(ph)

                def load_gc(c):
                    tok = slice(c * CH, (c + 1) * CH)
                    S.dma("sp", gch[c % 2][0][:], gT[:, tok].rearrange("(kc p) t -> p kc t", p=128), writes=[gch[c % 2][1]])

                def load_x5(c_, ot_):
                    x_t, x_b = xo[ot_ % 4]
                    S.dma("sp", x_t[:], xr_mid[ot_ * 128:(ot_ + 1) * 128, c_ * CH:(c_ + 1) * CH], writes=[x_b])

                load_gc(0)
                for c in range(NCH):
                    if c + 1 < NCH:
                        load_gc(c + 1)
                    load_x5(c, 0)
                    load_x5(c, 1)
                    g_t, g_b = gch[c % 2]
                    for ot in range(8):
                        if ot + 2 < 8:
                            load_x5(c, ot + 2)
                        x_t, x_b = xo[ot % 4]
                        pb = next_ps6()
                        for kc in range(22):
                            S.op("pe", lambda e, kc=kc: e.matmul(PS[pb][:, :], lhsT=wd[:, kc, ot * 128:(ot + 1) * 128], rhs=g_t[:, kc, :],
                                                                 start=(kc == 0), stop=(kc == 21)),
                                 reads=[WDB, g_b], writes=[PB[pb]])
                        S.op("dve", lambda e: e.scalar_tensor_tensor(out=r[:, ot, :], in0=x_t[:], scalar=ALPHA, in1=PS[pb][:, :],
                                                                     op0=ALU.mult, op1=ALU.add),
                             reads=[x_b, PB[pb]], writes=[RB])
                    layer_norm(ph, r, RB, c, PV_G2, PV_B2, xr_out, tmps)
                S.barrier()
            p5o.close()
        S.barrier()
    return nc


def host_constants():
    f32 = np.float32
    pos = np.arange(S_LEN, dtype=f32)
    inv_freq = (f32(500000.0) ** (-np.arange(0, 16, 2, dtype=f32) / f32(16))).astype(f32)
    ang = (pos[:, None] * inv_freq[None, :]).astype(f32)
    cos = np.cos(ang.astype(np.float64)).astype(f32).T
    sin = np.sin(ang.astype(np.float64)).astype(f32).T
    C64 = np.ones((64, S_LEN), f32)
    S64 = np.zeros((64, S_LEN), f32)
    C64[0:8] = cos
    C64[8:16] = cos
    S64[0:8] = -sin
    S64[8:16] = sin
    ropeC = np.concatenate([C64, C64], 0)
    ropeS = np.concatenate([S64, S64], 0)
    permT = np.zeros((128, 128), f32)
    for m in range(128):
        r = m % 64
        if r < 8:
            permT[m + 8, m] = 1.0
        elif r < 16:
            permT[m - 8, m] = 1.0
    ident = np.eye(128, dtype=f32)
    expand = np.zeros((64, S_LEN), f32)
    for j in range(64):
        expand[j, 64 * j:64 * (j + 1)] = 1.0
    n = np.arange(256)
    cstart = n * 16
    sstart = np.arange(64) * 64
    ov = np.maximum(np.minimum((cstart + 32)[:, None], (sstart + 64)[None, :])
                    - np.maximum(cstart[:, None], sstart[None, :]), 0).astype(f32) / f32(32)
    ov[255] = 0.0
    ovl = np.concatenate([ov, np.ones((256, 1), f32)], 1)
    ovl1 = np.ascontiguousarray(ovl.reshape(2, 128, 65).transpose(1, 0, 2))
    t = np.arange(S_LEN)
    cur = t // 64
    jj = np.arange(64)
    forced = (jj[None, :] == 0) | ((jj[None, :] <= cur[:, None]) & (jj[None, :] > cur[:, None] - 2))
    future = jj[None, :] > cur[:, None]
    cbig = np.where(forced, f32(1e30), f32(0.0)).astype(f32)
    cfut = np.where(future, f32(-1e30), f32(1e30)).astype(f32)
    cbig = np.ascontiguousarray(cbig.reshape(32, 128, 64).transpose(1, 0, 2))
    cfut = np.ascontiguousarray(cfut.reshape(32, 128, 64).transpose(1, 0, 2))
    gsel = np.zeros((24, 24 * 64), f32)
    for j in range(24):
        gsel[j, j * 64:(j + 1) * 64] = 1.0
    return dict(ropeC=ropeC, ropeS=ropeS, permT=permT, ident=ident, expand=expand, ovl1=ovl1,
                cbig=cbig, cfut=cfut, gsel=gsel)


def host_params(inp):
    f32 = np.float32
    pvec = np.zeros((4, 128, PV_N), f32)
    for l in range(4):
        ca = inp["conv_a_w"][l]
        pvec[l, :, PV_CA:PV_CA + 12] = ca.reshape(3, 4, 128).transpose(2, 1, 0).reshape(128, 12)
        pvec[l, :, PV_SL] = inp["diff_subln"][l]
        pvec[l, :, PV_G1:PV_G1 + 8] = inp["ln1_g"][l].reshape(8, 128).T
        pvec[l, :, PV_B1:PV_B1 + 8] = inp["ln1_b"][l].reshape(8, 128).T
        pvec[l, :, PV_G2:PV_G2 + 8] = inp["ln2_g"][l].reshape(8, 128).T
        pvec[l, :, PV_B2:PV_B2 + 8] = inp["ln2_b"][l].reshape(8, 128).T
        fw = inp["ffn_conv_w"][l]
        pvec[l, :, PV_FW:PV_FW + 132] = fw.reshape(3, 44, 128).transpose(2, 1, 0).reshape(128, 132)
        pvec[l, :, PV_FB:PV_FB + 44] = inp["ffn_conv_b"][l].reshape(44, 128).T
    pos = inp["nsa_cmp_pos"]
    pT = np.ascontiguousarray(pos.transpose(0, 1, 3, 2))
    posT = np.concatenate([pT, pT], axis=2)
    lamrep = np.ascontiguousarray(np.broadcast_to(inp["diff_lambda"].reshape(4, 1, 256), (4, 128, 256))).astype(f32)
    return dict(pvec=pvec, posT=np.ascontiguousarray(posT), lamrep=lamrep)


_NC_CACHE = {}


def make_in_maps(inputs, n_cores=8):
    inp = {k: np.asarray(v) for k, v in inputs.items()}
    shared = dict(host_constants())
    shared.update(host_params(inp))
    for k in ("w_in", "w_branch", "w_o", "ffn_w_up", "ffn_w_down", "nsa_phi_w1", "nsa_phi_w2"):
        shared[k] = np.ascontiguousarray(inp[k], dtype=np.float32)
    maps = []
    for c in range(n_cores):
        m = dict(shared)
        m["xT"] = np.ascontiguousarray(inp["x"][c % 4].T)
        maps.append(m)
    return maps


def kernel(**inputs):
    if "nc" not in _NC_CACHE:
        _NC_CACHE["nc"] = build()
    nc = _NC_CACHE["nc"]
    maps = make_in_maps(inputs, 8)
    res = run_bass_kernel_spmd(nc, maps, core_ids=list(range(8)))
    out = np.stack([np.ascontiguousarray(res.results[b]["outT"].T) for b in range(4)], 0)
    return out.astype(np.float32)
```

```python
import math
import os
from contextlib import ExitStack

import numpy as np
import concourse.bass as bass
import concourse.mybir as mybir
from concourse.bass_utils import run_bass_kernel_spmd

F32 = mybir.dt.float32
F32R = mybir.dt.float32r
BF16 = mybir.dt.bfloat16
AF = mybir.ActivationFunctionType
ALU = mybir.AluOpType
AX = mybir.AxisListType

S_LEN = 4096
NCH = 8
CH = 512
ALPHA = 8.0 ** 0.25
LN_EPS = 1e-5
NEGB = -30000.0
N_JUNK = int(os.environ.get('N_JUNK', '0'))

PV_CA, PV_SL, PV_G1, PV_B1, PV_G2, PV_B2, PV_FW, PV_FB, PV_N = 0, 12, 13, 21, 29, 37, 45, 177, 221


class Buf:
    __slots__ = ("name", "w", "r", "excl")

    def __init__(self, name="", excl=False):
        self.name = name
        self.w = None
        self.r = {}
        self.excl = excl


class Sched:
    def __init__(self, nc, es):
        self.nc = nc
        self.eng = {"pe": nc.tensor, "act": nc.scalar, "dve": nc.vector,
                    "pool": nc.gpsimd, "sp": nc.sync}
        self.sem = {}
        self.cnt = {}
        for k in ("pe", "act", "dve", "pool"):
            self.sem[k] = es.enter_context(nc.semaphore("s_" + k))
            self.cnt[k] = 0
        self.dsems = {}
        self.dnext = {}
        for q, n in {"sp": 16, "pool": 8}.items():
            self.dsems[q] = []
            for i in range(n):
                key = "d_%s_%d" % (q, i)
                self.sem[key] = es.enter_context(nc.semaphore(key))
                self.cnt[key] = 0
                self.dsems[q].append(key)
            self.dnext[q] = 0
        self.seen = {k: {} for k in self.eng}
        self.n_inst = 0
        self.n_wait = 0

    def _wait(self, ek, deps):
        e = self.eng[ek]
        seen = self.seen[ek]
        for key, val in deps.items():
            if val <= 0:
                continue
            if key == ek and ek == "pe":
                continue
            if seen.get(key, 0) >= val:
                continue
            e.wait_ge(self.sem[key], val)
            seen[key] = val
            self.n_wait += 1

    @staticmethod
    def _collect(reads, writes):
        deps = {}
        for b in reads:
            if b.w is not None and deps.get(b.w[0], 0) < b.w[1]:
                deps[b.w[0]] = b.w[1]
        for b in writes:
            if b.w is not None and deps.get(b.w[0], 0) < b.w[1]:
                deps[b.w[0]] = b.w[1]
            for k, v in b.r.items():
                if deps.get(k, 0) < v:
                    deps[k] = v
        return deps

    @staticmethod
    def _update(key, val, reads, writes):
        for b in reads:
            if b.r.get(key, 0) < val:
                b.r[key] = val
        for b in writes:
            b.w = (key, val)
            b.r = {}

    def op(self, ek, fn, reads=(), writes=()):
        xr = [b for b in reads if b.excl]
        self._wait(ek, self._collect(reads, list(writes) + xr))
        inst = fn(self.eng[ek])
        self.cnt[ek] += 1
        inst.then_inc(self.sem[ek], 1)
        self._update(ek, self.cnt[ek], reads, writes)
        self.n_inst += 1

    def dma(self, q, out, in_, reads=(), writes=()):
        lst = self.dsems[q]
        key = lst[self.dnext[q] % len(lst)]
        self.dnext[q] += 1
        deps = self._collect(reads, writes)
        if self.cnt[key] > 0:
            deps[key] = max(deps.get(key, 0), self.cnt[key])
        self._wait(q, deps)
        inst = self.eng[q].dma_start(out=out, in_=in_)
        self.cnt[key] += 16
        inst.then_inc(self.sem[key], 16)
        self._update(key, self.cnt[key], reads, writes)
        self.n_inst += 1

    def cc(self, kind, ins, outs, groups, reads=(), writes=()):
        q = "pool"
        lst = self.dsems[q]
        key = lst[self.dnext[q] % len(lst)]
        self.dnext[q] += 1
        deps = self._collect(reads, writes)
        if self.cnt[key] > 0:
            deps[key] = max(deps.get(key, 0), self.cnt[key])
        self._wait(q, deps)
        inst = self.eng[q].collective_compute(kind, ALU.bypass, replica_groups=groups, ins=ins, outs=outs)
        self.cnt[key] += 16
        inst.then_inc(self.sem[key], 16)
        self._update(key, self.cnt[key], reads, writes)
        self.n_inst += 1

    def barrier(self):
        deps = {k: v for k, v in self.cnt.items() if v > 0}
        for ek in self.eng:
            self._wait(ek, dict(deps))


class Stream:
    def __init__(self, lookahead=2):
        self.pend = []
        self.la = lookahead
        self.count = 0
        self.deferred = []

    def push(self, s_fn, post_fn):
        s_fn()
        self.pend.append(post_fn)
        if len(self.pend) > self.la:
            self._pop()

    def _pop(self):
        fn = self.pend.pop(0)
        fn()
        self.count += 1
        while self.deferred and self.deferred[0][0] <= self.count:
            self.deferred.pop(0)[1]()

    def defer(self, n, fn):
        self.deferred.append((self.count + 1 + n, fn))

    def flush(self):
        while self.pend:
            self._pop()
        while self.deferred:
            self.deferred.pop(0)[1]()


def build(n_layers=4, debug=False, stop_phase=99, max_tiles=999, do_v=True, tile_lo=0):
    nc = bass.Bass("TRN2", target_bir_lowering=False)

    def DI(name, shape, dt=F32):
        return nc.dram_tensor(name, list(shape), dt, kind="ExternalInput").ap()

    def DS(name, shape, dt=BF16):
        kind = "ExternalOutput" if debug else "Internal"
        return nc.dram_tensor(name, list(shape), dt, kind=kind).ap()

    xT_in = DI("xT", [1024, S_LEN])
    w_in = DI("w_in", [4, 1024, 7448])
    w_branch = DI("w_branch", [4, 3, 512, 1024])
    w_o = DI("w_o", [4, 1024, 1024])
    w_up = DI("ffn_w_up", [4, 1024, 5632])
    w_down = DI("ffn_w_down", [4, 2816, 1024])
    phi_w1 = DI("nsa_phi_w1", [4, 2, 2048, 128])
    phi_w2 = DI("nsa_phi_w2", [4, 2, 128, 64])
    pvec = DI("pvec", [4, 128, PV_N])
    posT = DI("posT", [4, 2, 128, 32])
    lamrep = DI("lamrep", [4, 128, 256])
    ropeC_d = DI("ropeC", [128, S_LEN])
    ropeS_d = DI("ropeS", [128, S_LEN])
    permT_d = DI("permT", [128, 128])
    ident_d = DI("ident", [128, 128])
    expand_d = DI("expand", [64, S_LEN])
    ovl1_d = DI("ovl1", [128, 2, 65])
    cbig_d = DI("cbig", [128, 32, 64])
    cfut_d = DI("cfut", [128, 32, 64])
    gsel_d = DI("gsel", [24, 24 * 64])

    outT = nc.dram_tensor("outT", [1024, S_LEN], F32, kind="ExternalOutput").ap()

    yT = DS("yT", [1536, S_LEN])
    qdT = DS("qdT", [512, S_LEN])
    kdT = DS("kdT", [512, S_LEN])
    vd = DS("vd", [S_LEN, 512])
    qnT = DS("qnT", [512, S_LEN])
    kcT = DS("kcT", [128, S_LEN])
    vcT = DS("vcT", [128, S_LEN])
    ksT = DS("ksT", [128, S_LEN])
    kwT = DS("kwT", [128, S_LEN])
    vaug = DS("vaug", [4, S_LEN, 128])
    ngT = DS("ngT", [24, S_LEN])
    mgT = DS("mgT", [3072, S_LEN])
    gT = DS("gT", [2816, S_LEN])
    xres = [DS("xresA", [1024, S_LEN], F32), DS("xresB", [1024, S_LEN], F32)]

    with ExitStack() as es:
        S = Sched(nc, es)

        uid = [0]

        def T(stack, name, shape, dt):
            uid[0] += 1
            return stack.enter_context(nc.sbuf_tensor("%s_u%d" % (name, uid[0]), list(shape), dt)), Buf(name)

        PSP = [es.enter_context(nc.psum_tensor("psp%d" % i, [128, 2 * 512], F32)) for i in range(2)]
        PS = []
        PB = []
        for i in range(8):
            if i < 4:
                PS.append(PSP[i // 2][:, (i % 2) * 512:(i % 2 + 1) * 512])
            else:
                PS.append(es.enter_context(nc.psum_tensor("ps%d" % i, [128, 512], F32)))
            PB.append(Buf("ps%d" % i, excl=True))

        xT_bf = es.enter_context(nc.sbuf_tensor("xT_bf", [128, 8, S_LEN], BF16))
        XB = [Buf("xb%d" % c) for c in range(NCH)]
        pv, PVB = T(es, "pv", [128, PV_N], F32)
        ones_bf, ONB = T(es, "ones_bf", [128, 128], BF16)
        ones_f, ONF = T(es, "ones_f", [128, 128], F32)
        permT, PMB = T(es, "permT_sb", [128, 128], BF16)
        ident, IDB = T(es, "ident_sb", [128, 128], BF16)
        lamt, LAMB = T(es, "lamt", [128, 256], F32)
        lamw, LAMW = T(es, "lamw", [128, 64], F32)
        lams, LAMS = T(es, "lams", [128, 8], F32)

        S.op("dve", lambda e: e.memset(ones_bf[:], 1.0), writes=[ONB])
        S.op("dve", lambda e: e.memset(ones_f[:], 1.0), writes=[ONF])
        S.dma("pool", permT[:], permT_d[:], writes=[PMB])
        S.dma("pool", ident[:], ident_d[:], writes=[IDB])

        for c in range(NCH):
            S.dma("pool", xT_bf[:, :, c * CH:(c + 1) * CH],
                  xT_in[:, c * CH:(c + 1) * CH].rearrange("(kc p) t -> p kc t", p=128), writes=[XB[c]])

        psrot = [0]

        def next_ps(lo=0, hi=4):
            i = lo + psrot[0] % (hi - lo)
            psrot[0] += 1
            return i

        rot6 = [0]

        def next_ps6(banks=(0, 1, 2, 3, 6, 7)):
            i = banks[rot6[0] % len(banks)]
            rot6[0] += 1
            return i

        for l in range(n_layers):
            lam_init = 0.8 - 0.6 * math.exp(-0.3 * l)
            xr_in = xT_in if l == 0 else xres[(l + 1) % 2]
            xr_mid = xres[l % 2]
            xr_out = outT if l == n_layers - 1 else xres[(l + 1) % 2]
            if l > 0:
                xr_in = xres[1]
                xr_mid = xres[0]
                xr_out = outT if l == n_layers - 1 else xres[1]
            else:
                xr_in = xT_in
                xr_mid = xres[0]
                xr_out = outT if l == n_layers - 1 else xres[1]

            S.dma("sp", pv[:], pvec[l], writes=[PVB])
            S.dma("sp", lamt[:], lamrep[l], writes=[LAMB])
            for j in range(2):
                S.op("dve", lambda e, j=j: e.tensor_tensor(out=lamw[:], in0=lamt[:, j * 128:j * 128 + 64],
                                                           in1=lamt[:, j * 128 + 64:j * 128 + 128], op=ALU.mult),
                     reads=[LAMB], writes=[LAMW])
                S.op("dve", lambda e, j=j: e.reduce_sum(out=lams[:, j:j + 1], in_=lamw[:], axis=AX.X),
                     reads=[LAMW], writes=[LAMS])
                S.op("act", lambda e, j=j: e.activation(out=lams[:, 2 + j:3 + j], in_=lams[:, j:j + 1], func=AF.Exp),
                     reads=[LAMS], writes=[LAMS])
            S.op("dve", lambda e: e.tensor_tensor(out=lams[:, 4:5], in0=lams[:, 2:3], in1=lams[:, 3:4], op=ALU.subtract),
                 reads=[LAMS], writes=[LAMS])
            S.op("dve", lambda e: e.tensor_scalar(out=lams[:, 5:6], in0=lams[:, 4:5], scalar1=lam_init, scalar2=-1.0,
                                                  op0=ALU.add, op1=ALU.mult), reads=[LAMS], writes=[LAMS])
            S.op("dve", lambda e: e.tensor_scalar(out=lams[:, 6:7], in0=pv[:, PV_SL:PV_SL + 1], scalar1=1.0 - lam_init,
                                                  scalar2=None, op0=ALU.mult), reads=[PVB, LAMS], writes=[LAMS])

            with ExitStack() as ph:
                ropeC, RCB = T(ph, "ropeC_sb", [128, S_LEN], F32)
                ropeS, RSB = T(ph, "ropeS_sb", [128, S_LEN], F32)
                S.dma("sp", ropeC[:], ropeC_d[:], writes=[RCB])
                S.dma("sp", ropeS[:], ropeS_d[:], writes=[RSB])
                wb = []
                for i in range(2):
                    wb.append(T(ph, "wb%d" % i, [128, 8, 128], BF16))
                ac_buf = ph.enter_context(nc.sbuf_tensor("ac_buf%d" % l, [128, S_LEN], F32))
                ACB = [Buf() for _ in range(NCH)]
                cvp = ph.enter_context(nc.sbuf_tensor("cvp%d" % l, [128, 2 + S_LEN], F32))
                CVB = [Buf() for _ in range(NCH)]
                CVH = Buf()
                S.op("pool", lambda e: e.memset(cvp[:, 0:2], 0.0), writes=[CVH])
                stg = [T(ph, "stg%d" % i, [128, CH], BF16) for i in range(4)]
                zbs = [T(ph, "zb%d" % i, [128, CH], BF16) for i in range(2)]
                t1s = [T(ph, "t1_%d" % i, [128, CH], F32) for i in range(2)]
                t2s = [T(ph, "t2_%d" % i, [128, CH], F32) for i in range(2)]
                rot = {"stg": 0, "zb": 0, "t": 0}

                def nxt(key, lst):
                    i = rot[key] % len(lst)
                    rot[key] += 1
                    return lst[i]

                wv, WVB = T(ph, "wv", [128, 8, 512], BF16)
                wv2, WV2B = T(ph, "wv2", [128, 8, 256], BF16)
                vst = [T(ph, "vst%d" % i, [128, 4, 128], BF16) for i in range(2)]
                for (vs_t, vs_b) in vst:
                    S.op("pool", lambda e, vs_t=vs_t: e.memset(vs_t[:], 1.0), writes=[vs_b])
                S.dma("pool", wv[:], w_in[l, :, 2560:3072].rearrange("(kc p) m -> p kc m", p=128), writes=[WVB])
                S.dma("pool", wv2[:, :, 0:128], w_in[l, :, 3968:4096].rearrange("(kc p) m -> p kc m", p=128), writes=[WV2B])
                S.dma("pool", wv2[:, :, 128:256], w_in[l, :, 4224:4352].rearrange("(kc p) m -> p kc m", p=128), writes=[WV2B])
                tiles = []
                for i in range(4):
                    tiles.append((512 + 128 * i, 128, "ac", None))
                    tiles.append((1024 + 128 * i, 128, "av", None))
                    tiles.append((128 * i, 128, "ab", i))
                for i in range(4):
                    tiles.append((1536 + 128 * i, 128, "rope", qdT[128 * i:128 * (i + 1), :]))
                for i in range(4):
                    tiles.append((2048 + 128 * i, 128, "rope", kdT[128 * i:128 * (i + 1), :]))
                for h in range(8):
                    tiles.append((3072 + 64 * h, 64, "rope", qnT[64 * h:64 * (h + 1), :]))
                tiles.append((3584 + 0 * 128, 128, "rope", kcT))
                tiles.append((3584 + 1 * 128, 128, "copy", vcT))
                tiles.append((3584 + 2 * 128, 128, "rope", ksT))
                tiles.append((3584 + 4 * 128, 128, "rope", kwT))
                tiles.append((4352, 24, "sig", ngT))
                for i in range(24):
                    tiles.append((4376 + 128 * i, 128, "sig", mgT[128 * i:128 * (i + 1), :]))

                def load_w(n):
                    c0, M, _, _ = tiles[n]
                    wt, wbuf = wb[n % 2]
                    S.dma("pool", wt[:, :, 0:M],
                          w_in[l, :, c0:c0 + M].rearrange("(kc p) m -> p kc m", p=128), writes=[wbuf])

                tiles = tiles[tile_lo:max_tiles]
                load_w(0)
                for n, (c0, M, kind, dst) in enumerate(tiles):
                    if n + 1 < len(tiles):
                        load_w(n + 1)
                    wt, wbuf = wb[n % 2]
                    for c in range(NCH):
                        tok = slice(c * CH, (c + 1) * CH)
                        pb = next_ps6()
                        for kc in range(8):
                            S.op("pe", lambda e, kc=kc, pb=pb: e.matmul(PS[pb][0:M, :], lhsT=wt[:, kc, 0:M],
                                                                      rhs=xT_bf[:, kc, tok], start=(kc == 0), stop=(kc == 7)),
                                 reads=[wbuf, XB[c]], writes=[PB[pb]])
                        if kind == "ac":
                            S.op("act", lambda e: e.activation(out=ac_buf[:, tok], in_=PS[pb][:, :], func=AF.Copy),
                                 reads=[PB[pb]], writes=[ACB[c]])
                        elif kind == "av":
                            S.op("dve", lambda e: e.tensor_tensor(out=cvp[:, 2 + c * CH:2 + (c + 1) * CH], in0=PS[pb][:, :],
                                                                  in1=ac_buf[:, tok], op=ALU.mult),
                                 reads=[PB[pb], ACB[c]], writes=[CVB[c]])
                        elif kind == "ab":
                            i = dst
                            t1, t1b = nxt("t", t1s)
                            sg, sgb = nxt("stg", stg)
                            halo = [CVB[c - 1]] if c > 0 else [CVH]
                            cw = lambda k: pv[:, PV_CA + i * 3 + k:PV_CA + i * 3 + k + 1]
                            S.op("dve", lambda e: e.tensor_scalar(out=t1[:], in0=cvp[:, c * CH:c * CH + CH], scalar1=cw(0),
                                                                  scalar2=None, op0=ALU.mult),
                                 reads=[CVB[c], PVB] + halo, writes=[t1b])
                            S.op("dve", lambda e: e.scalar_tensor_tensor(out=t1[:], in0=cvp[:, c * CH + 1:c * CH + CH + 1],
                                                                         scalar=cw(1), in1=t1[:], op0=ALU.mult, op1=ALU.add),
                                 reads=[CVB[c], PVB, t1b] + halo, writes=[t1b])
                            S.op("dve", lambda e: e.scalar_tensor_tensor(out=t1[:], in0=cvp[:, c * CH + 2:c * CH + CH + 2],
                                                                         scalar=cw(2), in1=t1[:], op0=ALU.mult, op1=ALU.add),
                                 reads=[CVB[c], PVB, t1b], writes=[t1b])
                            S.op("dve", lambda e: e.tensor_tensor(out=sg[:], in0=PS[pb][:, :], in1=t1[:], op=ALU.mult),
                                 reads=[PB[pb], t1b], writes=[sgb])
                            S.dma("sp", yT[128 * i:128 * (i + 1), tok], sg[:], reads=[sgb])
                        elif kind == "rope":
                            zb, zbb = nxt("zb", zbs)
                            t1, t1b = t1s[rot["t"] % 2]
                            t2, t2b = t2s[rot["t"] % 2]
                            rot["t"] += 1
                            sg, sgb = nxt("stg", stg)
                            pw = next_ps(4, 6)
                            S.op("act", lambda e: e.activation(out=zb[0:M, :], in_=PS[pb][0:M, :], func=AF.Copy),
                                 reads=[PB[pb]], writes=[zbb])
                            if os.environ.get("ROPE_DBG") != "nope":
                                S.op("pe", lambda e: e.matmul(PS[pw][0:M, :], lhsT=permT[0:M, 0:M], rhs=zb[0:M, :],
                                                              start=True, stop=True),
                                     reads=[PMB, zbb], writes=[PB[pw]])
                            dbg = os.environ.get("ROPE_DBG", "")
                            if dbg == "b":
                                S.dma("sp", dst[0:M, tok], zb[0:M, :], reads=[zbb])
                                continue
                            if dbg == "c":
                                S.op("dve", lambda e: e.tensor_tensor(out=t1[0:M, :], in0=PS[pb][0:M, :], in1=ac_buf[0:M, tok],
                                                                      op=ALU.mult), reads=[PB[pb]], writes=[t1b])
                            else:
                                S.op("dve", lambda e: e.tensor_tensor(out=t1[0:M, :], in0=PS[pb][0:M, :], in1=ropeC[0:M, tok],
                                                                      op=ALU.mult), reads=[PB[pb], RCB, zbb], writes=[t1b])
                            if dbg == "c":
                                S.op("dve", lambda e: e.tensor_copy(out=sg[0:M, :], in_=t1[0:M, :]), reads=[t1b], writes=[sgb])
                                S.dma("sp", dst[0:M, tok], sg[0:M, :], reads=[sgb])
                                continue
                            S.op("dve", lambda e: e.tensor_tensor(out=t2[0:M, :], in0=PS[pw][0:M, :], in1=ropeS[0:M, tok],
                                                                  op=ALU.mult), reads=[PB[pw], RSB], writes=[t2b])
                            S.op("pool", lambda e: e.tensor_tensor(out=sg[0:M, :], in0=t1[0:M, :], in1=t2[0:M, :], op=ALU.add),
                                 reads=[t1b, t2b], writes=[sgb])
                            S.dma("sp", dst[0:M, tok], sg[0:M, :], reads=[sgb])
                        elif kind in ("sig", "copy"):
                            sg, sgb = nxt("stg", stg)
                            fn = AF.Sigmoid if kind == "sig" else AF.Copy
                            S.op("act", lambda e: e.activation(out=sg[0:M, :], in_=PS[pb][0:M, :], func=fn),
                                 reads=[PB[pb]], writes=[sgb])
                            S.dma("sp", dst[0:M, tok], sg[0:M, :], reads=[sgb])

                for tt in range(32 if do_v else 0):
                    c = tt // 4
                    tk = slice(tt * 128, (tt + 1) * 128)
                    pb = next_ps6()
                    for kc in range(8):
                        S.op("pe", lambda e, kc=kc: e.matmul(PS[pb][:, :], lhsT=xT_bf[:, kc, tk], rhs=wv[:, kc, :],
                                                             start=(kc == 0), stop=(kc == 7)),
                             reads=[WVB, XB[c]], writes=[PB[pb]])
                    sg, sgb = nxt("stg", stg)
                    S.op("act", lambda e: e.activation(out=sg[:], in_=PS[pb][:, :], func=AF.Copy), reads=[PB[pb]], writes=[sgb])
                    S.dma("sp", vd[tk, :], sg[:], reads=[sgb])
                    pb2 = next_ps6()
                    for kc in range(8):
                        S.op("pe", lambda e, kc=kc: e.matmul(PS[pb2][:, 0:256], lhsT=xT_bf[:, kc, tk], rhs=wv2[:, kc, :],
                                                             start=(kc == 0), stop=(kc == 7)),
                             reads=[WV2B, XB[c]], writes=[PB[pb2]])
                    vs_t, vs_b = vst[tt % 2]
                    S.op("act", lambda e: e.activation(out=vs_t[:, :, 0:64],
                                                       in_=PS[pb2][:, 0:256].rearrange("p (i d) -> p i d", d=64), func=AF.Copy),
                         reads=[PB[pb2]], writes=[vs_b])
                    S.dma("sp", vaug[:, tk, :].rearrange("i p c -> p i c"), vs_t[:], reads=[vs_b])
                S.barrier()
            if stop_phase <= 1:
                break

            with ExitStack() as ph:
                vall, VAB = T(ph, "vall", [128, 32, 512], BF16)
                S.dma("sp", vall[:], vd.rearrange("(kt p) c -> p kt c", p=128), writes=[VAB])
                qk = [(T(ph, "dq%d" % i, [128, S_LEN], BF16), T(ph, "dk%d" % i, [128, S_LEN], BF16)) for i in range(2)]
                pbufs = [T(ph, "pd%d" % i, [128, 2 * CH], BF16) for i in range(3)]
                osets = []
                for i in range(2):
                    osets.append(dict(r1=T(ph, "r1_%d" % i, [128, CH], F32), o1=T(ph, "o1_%d" % i, [128, CH], F32),
                                      o2=T(ph, "o2_%d" % i, [128, CH], F32), sq=T(ph, "sq_%d" % i, [128, CH], F32)))
                ysg = [T(ph, "ysg%d" % i, [128, CH], BF16) for i in range(2)]
                prot = [0]
                st = Stream(1)
                prot2 = [0]

                def load_qk(h):
                    (qt, qb), (kt_, kb) = qk[h % 2]
                    S.dma("sp", qt[:], qdT[128 * h:128 * (h + 1), :], writes=[qb])
                    S.dma("sp", kt_[:], kdT[128 * h:128 * (h + 1), :], writes=[kb])

                def mk_step(h, qc, m, kts, nkt, oset, gi):
                    (qt, qb), (kt_, kb) = qk[h % 2]
                    po, pl = 4 + m, 6 + m
                    rows = slice(m * 64, (m + 1) * 64)
                    kt = kts[-1]
                    j = kts[0] - 4 * qc
                    c0 = 128 * j if j > 0 else 0
                    box = {}

                    def s_fn():
                        pr = prot2[0] % 2
                        prot2[0] += 1
                        box["pr"] = pr
                        for idx, k in enumerate(kts):
                            bk = 2 * pr + idx
                            S.op("pe", lambda e: e.matmul(PS[bk][:, c0:CH], lhsT=kt_[rows, k * 128:(k + 1) * 128],
                                                          rhs=qt[rows, qc * CH + c0:(qc + 1) * CH], start=True, stop=True),
                                 reads=[qb, kb], writes=[PB[bk]])

                    def post_fn():
                        pr = box["pr"]
                        pt, ptb = pbufs[prot[0] % 3]
                        prot[0] += 1
                        nk = len(kts)
                        banks = [PB[2 * pr + idx] for idx in range(nk)]
                        if nk == 2 and os.environ.get("PAIR_EXP", "1") == "1":
                            S.op("act", lambda e: e.activation(out=pt[:, 0:2 * CH], in_=PSP[pr][:, :], func=AF.Exp,
                                                               scale=0.125), reads=banks, writes=[ptb])
                        elif nk == 2:
                            for idx in range(2):
                                S.op("act", lambda e: e.activation(out=pt[:, idx * CH:(idx + 1) * CH], in_=PS[2 * pr + idx][:, :], func=AF.Exp,
                                                                   scale=0.125), reads=banks, writes=[ptb])
                        else:
                            S.op("act", lambda e: e.activation(out=pt[:, c0:CH], in_=PS[2 * pr][:, c0:CH], func=AF.Exp,
                                                               scale=0.125), reads=banks, writes=[ptb])
                            if j >= 0:
                                S.op("pool", lambda e: e.affine_select(out=pt[:, c0:c0 + 128], in_=pt[:, c0:c0 + 128],
                                                                       pattern=[[1, 128]], compare_op=ALU.is_ge, fill=0.0,
                                                                       base=0, channel_multiplier=-1),
                                     reads=[ptb], writes=[ptb])
                        for idx, k in enumerate(kts):
                            o0 = idx * CH
                            S.op("pe", lambda e: e.matmul(PS[po][:, c0:CH], lhsT=vall[:, k, h * 128:(h + 1) * 128],
                                                          rhs=pt[:, o0 + c0:o0 + CH], start=(k == 0), stop=(k == nkt - 1)),
                                 reads=[VAB, ptb], writes=[PB[po]])
                            S.op("pe", lambda e: e.matmul(PS[pl][:, c0:CH], lhsT=ones_bf[:, :],
                                                          rhs=pt[:, o0 + c0:o0 + CH], start=(k == 0), stop=(k == nkt - 1)),
                                 reads=[ONB, ptb], writes=[PB[pl]])
                        if kt != nkt - 1:
                            return
                        r1, R1B = oset["r1"]
                        o1, O1B = oset["o1"]
                        o2, O2B = oset["o2"]
                        sq, SQB = oset["sq"]
                        om, OMB = (o1, O1B) if m == 0 else (o2, O2B)
                        S.op("dve", lambda e: e.reciprocal(out=r1[:], in_=PS[pl][:, :]), reads=[PB[pl], R1B], writes=[R1B])
                        S.op("dve", lambda e: e.tensor_tensor(out=om[:], in0=PS[po][:, :], in1=r1[:], op=ALU.mult),
                             reads=[PB[po], R1B, OMB], writes=[OMB])
                        if m == 0:
                            return
                        S.op("dve", lambda e: e.scalar_tensor_tensor(out=o1[:], in0=o2[:], scalar=lams[:, 5:6], in1=o1[:],
                                                                     op0=ALU.mult, op1=ALU.add),
                             reads=[O2B, O1B, LAMS], writes=[O1B])
                        S.op("dve", lambda e: e.tensor_tensor(out=sq[:], in0=o1[:], in1=o1[:], op=ALU.mult),
                             reads=[O1B, SQB], writes=[SQB])

                        def final():
                            tok = slice(qc * CH, (qc + 1) * CH)
                            S.op("pe", lambda e: e.matmul(PS[7][:, :], lhsT=ones_f[:, :], rhs=sq[:], start=True, stop=True),
                                 reads=[ONF, SQB], writes=[PB[7]])
                            S.op("dve", lambda e: e.tensor_scalar(out=o2[:], in0=PS[7][:, :], scalar1=1.0 / 128.0, scalar2=LN_EPS,
                                                                  op0=ALU.mult, op1=ALU.add), reads=[PB[7], O2B], writes=[O2B])
                            S.op("act", lambda e: e.activation(out=o2[:], in_=o2[:], func=AF.Ln), reads=[O2B], writes=[O2B])
                            S.op("act", lambda e: e.activation(out=o2[:], in_=o2[:], func=AF.Exp, scale=-0.5), reads=[O2B], writes=[O2B])
                            yt, ytb = ysg[gi % 2]
                            S.op("dve", lambda e: e.scalar_tensor_tensor(out=yt[:], in0=o1[:], scalar=lams[:, 6:7], in1=o2[:],
                                                                         op0=ALU.mult, op1=ALU.mult),
                                 reads=[O1B, O2B, LAMS], writes=[ytb])
                            S.dma("sp", yT[512 + 128 * h:512 + 128 * (h + 1), tok], yt[:], reads=[ytb])
                        st.defer(3, final)

                    return s_fn, post_fn

                load_qk(0)
                gi = 0
                for h in range(4):
                    if h + 1 < 4:
                        load_qk(h + 1)
                    for qc in range(NCH):
                        nkt = 4 * qc + 4
                        oset = osets[gi % 2]
                        for m in range(2):
                            for kt in range(0, 4 * qc, 2):
                                st.push(*mk_step(h, qc, m, (kt, kt + 1), nkt, oset, gi))
                            for kt in range(4 * qc, nkt):
                                st.push(*mk_step(h, qc, m, (kt,), nkt, oset, gi))
                        gi += 1
                st.flush()
                S.barrier()
            if stop_phase <= 2:
                break

            with ExitStack() as ph:
                hT, HTB = T(ph, "hT", [128, 256], BF16)
                kccT = [T(ph, "kccT%d" % g, [64, 256], BF16) for g in range(2)]
                vcE = [T(ph, "vcE%d" % g, [128, 2, 128], BF16) for g in range(2)]
                ovl1, OVB = T(ph, "ovl1_sb", [128, 2, 65], BF16)
                S.dma("pool", ovl1[:], ovl1_d[:], writes=[OVB])
                gsel, GSB = T(ph, "gsel_sb", [24, 24 * 64], BF16)
                S.dma("pool", gsel[:], gsel_d[:], writes=[GSB])
                ng_sb, NGB = T(ph, "ng_sb", [24, S_LEN], BF16)
                S.dma("sp", ng_sb[:], ngT[:], writes=[NGB])
                S.op("pool", lambda e: e.memset(hT[:, 255:256], 0.0), writes=[HTB])
                for g in range(2):
                    S.op("pool", lambda e, g=g: e.memset(vcE[g][0][:], 1.0), writes=[vcE[g][1]])
                with ExitStack() as sub:
                    kc_sb, KCB = T(sub, "kc_sb", [128, S_LEN], BF16)
                    vc_sb, VCB = T(sub, "vc_sb", [128, S_LEN], BF16)
                    S.dma("sp", kc_sb[:], kcT[:], writes=[KCB])
                    S.dma("sp", vc_sb[:], vcT[:], writes=[VCB])
                    w1s, W1B = T(sub, "w1s", [128, 32, 128], BF16)
                    w2s, W2B = T(sub, "w2s", [128, 64], BF16)
                    pos_sb, POSB = T(sub, "pos_sb", [128, 32], BF16)
                    bcol, BCB = T(sub, "bcol", [128, 1], F32)
                    for which, src, srcb in ((0, kc_sb, KCB), (1, vc_sb, VCB)):
                        for half in range(2):
                            S.dma("pool", w1s[half * 64:(half + 1) * 64, :, :],
                                  phi_w1[l, which].rearrange("(l d) h -> d l h", d=64), writes=[W1B])
                        S.dma("pool", w2s[:], phi_w2[l, which], writes=[W2B])
                        S.dma("pool", pos_sb[:], posT[l, which], writes=[POSB])
                        for g in range(2):
                            rows = slice(g * 64, (g + 1) * 64)
                            pbias = next_ps(0, 4)
                            for li in range(32):
                                S.op("pe", lambda e, li=li: e.matmul(PS[pbias][:, 0:1], lhsT=w1s[rows, li, :], rhs=pos_sb[rows, li:li + 1],
                                                                     start=(li == 0), stop=(li == 31)),
                                     reads=[W1B, POSB], writes=[PB[pbias]])
                            S.op("act", lambda e: e.activation(out=bcol[:], in_=PS[pbias][:, 0:1], func=AF.Copy),
                                 reads=[PB[pbias]], writes=[BCB])
                            ph_ = next_ps(0, 4)
                            for li in range(32):
                                S.op("pe", lambda e, li=li: e.matmul(PS[ph_][:, 0:255], lhsT=w1s[rows, li, :],
                                                                     rhs=src[rows, li:li + 16 * 254 + 1:16],
                                                                     start=(li == 0), stop=(li == 31)),
                                     reads=[W1B, srcb], writes=[PB[ph_]])
                            S.op("act", lambda e: e.activation(out=hT[:, 0:255], in_=PS[ph_][:, 0:255], func=AF.Gelu, bias=bcol[:, 0:1]),
                                 reads=[PB[ph_], BCB], writes=[HTB])
                            if which == 0:
                                po_ = next_ps(0, 4)
                                S.op("pe", lambda e: e.matmul(PS[po_][0:64, 0:256], lhsT=w2s[:, :], rhs=hT[:, :], start=True, stop=True),
                                     reads=[W2B, HTB], writes=[PB[po_]])
                                S.op("act", lambda e: e.activation(out=kccT[g][0][:], in_=PS[po_][0:64, 0:256], func=AF.Copy),
                                     reads=[PB[po_]], writes=[kccT[g][1]])
                            else:
                                for nt in range(2):
                                    po_ = next_ps(0, 4)
                                    S.op("pe", lambda e: e.matmul(PS[po_][:, 0:64], lhsT=hT[:, nt * 128:(nt + 1) * 128], rhs=w2s[:, :],
                                                                  start=True, stop=True),
                                         reads=[W2B, HTB], writes=[PB[po_]])
                                    S.op("act", lambda e: e.activation(out=vcE[g][0][:, nt, 0:64], in_=PS[po_][:, 0:64], func=AF.Copy),
                                         reads=[PB[po_]], writes=[vcE[g][1]])
                    S.barrier()
                cbig, CBB = T(ph, "cbig_sb", [128, 32, 64], BF16)
                cfut, CFB = T(ph, "cfut_sb", [128, 32, 64], BF16)
                S.dma("pool", cbig[:], cbig_d[:], writes=[CBB])
                S.dma("pool", cfut[:], cfut_d[:], writes=[CFB])
                impacc = ph.enter_context(nc.sbuf_tensor("impacc%d" % l, [128, 32, 64], F32))
                IMB = [Buf() for _ in range(NCH)]
                impm, IPMB = T(ph, "impm", [128, 64], F32)
                imp2, IP2B = T(ph, "imp2", [128, 64], F32)
                v8a, V8AB = T(ph, "v8a", [128, 8], F32)
                v8b, V8BB = T(ph, "v8b", [128, 8], F32)
                rl4, RL4B = T(ph, "rl4", [128, 4], F32)
                IMQ = [Buf() for _ in range(32)]
                biasq, BQB = T(ph, "biasq", [128, 4, 64], BF16)
                biasT, BTB = T(ph, "biasT", [64, S_LEN], BF16)
                ksE, KSB = T(ph, "ksE", [128, S_LEN], BF16)
                kwg, KWB = T(ph, "kwg", [64, S_LEN], BF16)
                vsE, VSB = T(ph, "vsE", [128, 32, 128], BF16)
                vwE, VWB = T(ph, "vwE", [128, 32, 128], BF16)
                qaug = [T(ph, "qaug%d" % i, [128, S_LEN], BF16) for i in range(2)]
                pnset = [[T(ph, "pn%d_%d" % (s_, i), [128, CH], BF16) for i in range(2)] for s_ in range(2)]
                pt3 = [T(ph, "pt3_%d" % i, [128, CH], BF16) for i in range(5)]
                rLt, RLTB = T(ph, "rLt", [64, CH], F32)
                tA, TAB = T(ph, "tA", [64, CH], F32)
                acc, ACCB = T(ph, "acc", [64, CH], F32)
                ycs = [T(ph, "ycs%d" % i, [64, CH], BF16) for i in range(2)]
                prot = [0]
                qload = [0]
                urot = [0]
                grot = [0]
                st = Stream(3)

                def mk_cmp(g, qa, qc, nt, pn, last_fn):
                    box = {}

                    def s_fn():
                        ps_s = next_ps(0, 4)
                        box["ps"] = ps_s
                        S.op("pe", lambda e: e.matmul(PS[ps_s][:, :], lhsT=kccT[g][0][:, nt * 128:(nt + 1) * 128],
                                                      rhs=qa[0][0:64, qc * CH:(qc + 1) * CH], start=True, stop=True),
                             reads=[kccT[g][1], qa[1]], writes=[PB[ps_s]])

                    def post_fn():
                        ps_s = box["ps"]
                        p_t, p_b = pn[nt]
                        S.op("act", lambda e: e.activation(out=p_t[:], in_=PS[ps_s][:, :], func=AF.Exp, scale=0.125),
                             reads=[PB[ps_s]], writes=[p_b])
                        S.op("pool", lambda e: e.affine_select(out=p_t[:], in_=p_t[:], pattern=[[1, CH]], compare_op=ALU.is_ge,
                                                               fill=0.0, base=qc * CH - 2048 * nt - 31, channel_multiplier=-16),
                             reads=[p_b], writes=[p_b])
                        if last_fn is not None:
                            last_fn()

                    return s_fn, post_fn

                def mk_att(qa, qc, kt, c0, c1, klhs, krows, kbuf, vE, vbuf, obank, first, last, mask, after):
                    box = {}

                    def s_fn():
                        ps_s = next_ps(0, 4)
                        box["ps"] = ps_s
                        S.op("pe", lambda e: e.matmul(PS[ps_s][:, c0:c1], lhsT=klhs[krows, kt * 128:(kt + 1) * 128],
                                                      rhs=qa[0][krows, qc * CH + c0:qc * CH + c1], start=True, stop=True),
                             reads=[kbuf, qa[1]], writes=[PB[ps_s]])

                    def post_fn():
                        ps_s = box["ps"]
                        p_t, p_b = pt3[prot[0] % 5]
                        prot[0] += 1
                        S.op("act", lambda e: e.activation(out=p_t[:, c0:c1], in_=PS[ps_s][:, c0:c1], func=AF.Exp, scale=0.125),
                             reads=[PB[ps_s]], writes=[p_b])
                        if mask == "lo":
                            S.op("pool", lambda e: e.affine_select(out=p_t[:, c0:c0 + 128], in_=p_t[:, c0:c0 + 128],
                                                                   pattern=[[1, 128]], compare_op=ALU.is_ge, fill=0.0,
                                                                   base=0, channel_multiplier=-1),
                                 reads=[p_b], writes=[p_b])
                        elif mask == "hi":
                            S.op("pool", lambda e: e.affine_select(out=p_t[:, c1 - 128:c1], in_=p_t[:, c1 - 128:c1],
                                                                   pattern=[[-1, 128]], compare_op=ALU.is_gt, fill=0.0,
                                                                   base=0, channel_multiplier=1),
                                 reads=[p_b], writes=[p_b])
                        S.op("pe", lambda e: e.matmul(PS[obank][:, c0:c1], lhsT=vE[:, kt, :], rhs=p_t[:, c0:c1],
                                                      start=first, stop=last),
                             reads=[vbuf, p_b], writes=[PB[obank]])
                        if after is not None:
                            after()

                    return s_fn, post_fn

                for g in range(2):
                    S.dma("sp", ksE[0:64, :], ksT[g * 64:(g + 1) * 64, :], writes=[KSB])
                    S.dma("pool", ksE[64:128, :], expand_d[:], writes=[KSB])
                    S.dma("sp", kwg[:], kwT[g * 64:(g + 1) * 64, :], writes=[KWB])
                    S.dma("sp", vsE[:], vaug[g].rearrange("(kt p) c -> p kt c", p=128), writes=[VSB])
                    S.dma("sp", vwE[:], vaug[2 + g].rearrange("(kt p) c -> p kt c", p=128), writes=[VWB])
                    for hp in range(4):
                        h = 4 * g + hp
                        qa = qaug[qload[0] % 2]
                        qload[0] += 1
                        S.dma("sp", qa[0][0:64, :], qnT[h * 64:(h + 1) * 64, :], writes=[qa[1]])
                        for qc in range(NCH):
                            nts = [0] if qc < 4 else [0, 1]
                            pn = pnset[(hp * NCH + qc) % 2]

                            def imp_fn(qc=qc, nts=nts, pn=pn, hp=hp):
                                ub = 4 + urot[0] % 4
                                urot[0] += 1
                                for qs in range(4):
                                    for nt in nts:
                                        S.op("pe", lambda e: e.matmul(PS[ub][:, qs * 128:qs * 128 + 65], lhsT=pn[nt][0][:, qs * 128:(qs + 1) * 128],
                                                                      rhs=ovl1[:, nt, :], start=(nt == nts[0]), stop=(nt == nts[-1])),
                                             reads=[pn[nt][1], OVB], writes=[PB[ub]])
                                S.op("dve", lambda e: e.tensor_scalar(out=rl4[:], in0=PS[ub][:, 64:512:128], scalar1=1e-30,
                                                                      scalar2=None, op0=ALU.add), reads=[PB[ub], RL4B], writes=[RL4B])
                                S.op("dve", lambda e: e.reciprocal(out=rl4[:], in_=rl4[:]), reads=[RL4B], writes=[RL4B])
                                for qs in range(4):
                                    qt = qc * 4 + qs
                                    if hp == 0:
                                        S.op("dve", lambda e: e.tensor_scalar(out=impacc[:, qt, :], in0=PS[ub][:, qs * 128:qs * 128 + 64],
                                                                              scalar1=rl4[:, qs:qs + 1], scalar2=None, op0=ALU.mult),
                                             reads=[PB[ub], RL4B], writes=[IMQ[qt]])
                                    else:
                                        S.op("dve", lambda e: e.scalar_tensor_tensor(out=impacc[:, qt, :], in0=PS[ub][:, qs * 128:qs * 128 + 64],
                                                                                     scalar=rl4[:, qs:qs + 1], in1=impacc[:, qt, :],
                                                                                     op0=ALU.mult, op1=ALU.add),
                                             reads=[PB[ub], RL4B, IMQ[qt]], writes=[IMQ[qt]])

                            for nt in nts:
                                st.push(*mk_cmp(g, qa, qc, nt, pn, imp_fn if nt == nts[-1] else None))
                    st.flush()
                    for qc in range(NCH):
                        for qs in range(4):
                            qt = qc * 4 + qs
                            S.op("dve", lambda e: e.tensor_tensor(out=impm[:], in0=impacc[:, qt, :], in1=cbig[:, qt, :], op=ALU.max),
                                 reads=[IMQ[qt], CBB, IPMB], writes=[IPMB])
                            S.op("dve", lambda e: e.tensor_tensor(out=impm[:], in0=impm[:], in1=cfut[:, qt, :], op=ALU.min),
                                 reads=[IPMB, CFB], writes=[IPMB])
                            S.op("dve", lambda e: e.max(out=v8a[:], in_=impm[:]), reads=[IPMB], writes=[V8AB])
                            S.op("dve", lambda e: e.match_replace(out=imp2[:], in_to_replace=v8a[:], in_values=impm[:], imm_value=-3.0e38),
                                 reads=[IPMB, V8AB], writes=[IP2B])
                            S.op("dve", lambda e: e.max(out=v8b[:], in_=imp2[:]), reads=[IP2B], writes=[V8BB])
                            S.op("dve", lambda e: e.tensor_scalar(out=biasq[:, qs, :], in0=impm[:], scalar1=v8b[:, 7:8], scalar2=NEGB,
                                                                  op0=ALU.is_lt, op1=ALU.mult),
                                 reads=[IPMB, V8BB], writes=[BQB])
                        for qs in range(4):
                            S.op("pe", lambda e: e.matmul(PS[4][0:64, qs * 128:(qs + 1) * 128], lhsT=biasq[:, qs, :], rhs=ident[:, :],
                                                          start=True, stop=True), reads=[BQB, IDB], writes=[PB[4]])
                        S.op("act", lambda e: e.activation(out=biasT[:, qc * CH:(qc + 1) * CH], in_=PS[4][0:64, :], func=AF.Copy),
                             reads=[PB[4]], writes=[BTB])
                    for hp in range(4):
                        h = 4 * g + hp
                        qa = qaug[qload[0] % 2]
                        qload[0] += 1
                        S.dma("sp", qa[0][0:64, :], qnT[h * 64:(h + 1) * 64, :], writes=[qa[1]])
                        S.dma("sp", qa[0][64:128, :], biasT[:, :], reads=[BTB], writes=[qa[1]])
                        for qc in range(NCH):
                            nts = [0] if qc < 4 else [0, 1]
                            pn = pnset[(hp * NCH + qc) % 2]
                            ci = hp * NCH + qc

                            def epi(br, pbk, which, h=h, qc=qc, ci=ci):
                                def fn():
                                    tok = slice(qc * CH, (qc + 1) * CH)
                                    col = h * 3 + br
                                    gb = 7
                                    grot[0] += 1
                                    yc, ycb = ycs[ci % 2]
                                    S.op("pe", lambda e: e.matmul(PS[gb][0:64, :], lhsT=gsel[0:24, col * 64:(col + 1) * 64], rhs=ng_sb[0:24, tok],
                                                                  start=True, stop=True), reads=[GSB, NGB], writes=[PB[gb]])
                                    if br == 0:
                                        S.op("dve", lambda e: e.tensor_scalar(out=rLt[:], in0=PS[pbk][64:128, :], scalar1=1e-30, scalar2=None,
                                                                              op0=ALU.add), reads=[PB[pbk], RLTB], writes=[RLTB])
                                        S.op("dve", lambda e: e.reciprocal(out=rLt[:], in_=rLt[:]), reads=[RLTB], writes=[RLTB])
                                    else:
                                        S.op("dve", lambda e: e.reciprocal(out=rLt[:], in_=PS[pbk][64:128, :]), reads=[PB[pbk], RLTB], writes=[RLTB])
                                    S.op("dve", lambda e: e.tensor_tensor(out=tA[:], in0=PS[pbk][0:64, :], in1=rLt[:], op=ALU.mult),
                                         reads=[PB[pbk], RLTB, TAB], writes=[TAB])
                                    if which == 0:
                                        S.op("dve", lambda e: e.tensor_tensor(out=acc[:], in0=PS[gb][0:64, :], in1=tA[:], op=ALU.mult),
                                             reads=[PB[gb], TAB, ACCB], writes=[ACCB])
                                    else:
                                        S.op("dve", lambda e: e.tensor_tensor(out=tA[:], in0=PS[gb][0:64, :], in1=tA[:], op=ALU.mult),
                                             reads=[PB[gb], TAB], writes=[TAB])
                                        if which == 1:
                                            S.op("pool", lambda e: e.tensor_tensor(out=acc[:], in0=acc[:], in1=tA[:], op=ALU.add),
                                                 reads=[TAB, ACCB], writes=[ACCB])
                                        else:
                                            S.op("pool", lambda e: e.tensor_tensor(out=yc[:], in0=acc[:], in1=tA[:], op=ALU.add),
                                                 reads=[TAB, ACCB], writes=[ycb])
                                            S.dma("sp", yT[1024 + 64 * h:1024 + 64 * (h + 1), tok], yc[:], reads=[ycb])
                                return lambda: st.defer(2, fn)

                            def cmp_o(nts=nts, pn=pn, g=g):
                                for nt in nts:
                                    S.op("pe", lambda e: e.matmul(PS[6][:, :], lhsT=vcE[g][0][:, nt, :], rhs=pn[nt][0][:, :],
                                                                  start=(nt == nts[0]), stop=(nt == nts[-1])),
                                         reads=[vcE[g][1], pn[nt][1]], writes=[PB[6]])
                                epi(0, 6, 0)()
                            for nt in nts:
                                st.push(*mk_cmp(g, qa, qc, nt, pn, cmp_o if nt == nts[-1] else None))
                            jls = [jl for jl in (-1, -4, -3, -2, 0, 1, 2, 3) if 4 * qc + jl >= 0]
                            for wi, jl in enumerate(jls):
                                kt = 4 * qc + jl
                                if jl >= 0:
                                    c0, c1, mk = 128 * jl, CH, "lo"
                                else:
                                    c0, c1, mk = 0, 128 * (jl + 5), "hi"
                                lastw = (wi == len(jls) - 1)
                                st.push(*mk_att(qa, qc, kt, c0, c1, kwg, slice(0, 64), KWB, vwE, VWB, 5, wi == 0, lastw, mk,
                                                epi(2, 5, 1) if lastw else None))
                            nkt = 4 * qc + 4
                            for kt in range(nkt):
                                j = kt - 4 * qc
                                c0 = 128 * j if j > 0 else 0
                                lasts = (kt == nkt - 1)
                                st.push(*mk_att(qa, qc, kt, c0, CH, ksE, slice(0, 128), KSB, vsE, VSB, 4, kt == 0, lasts,
                                                "lo" if j >= 0 else None, epi(1, 4, 2) if lasts else None))
                    st.flush()
                S.barrier()
            if stop_phase <= 3:
                break

            def layer_norm(ph, r, RB, c, goff, boff, dst, tmps):
                (sqt, mean, MEB, msq, MSB, tt, outf, rbs) = tmps
                tok = slice(c * CH, (c + 1) * CH)
                for ot in range(8):
                    rb_t, rb_b = rbs[ot % 2]
                    S.op("pool", lambda e, ot=ot: e.tensor_copy(out=rb_t[:], in_=r[:, ot, :]), reads=[RB], writes=[rb_b])
                    S.op("pe", lambda e, ot=ot: e.matmul(PS[4][:, :], lhsT=ones_bf[:, :], rhs=rb_t[:], start=(ot == 0), stop=(ot == 7)),
                         reads=[ONB, rb_b], writes=[PB[4]])
                    sq_t, sq_b = sqt[ot % 2]
                    S.op("act", lambda e, ot=ot: e.activation(out=sq_t[:], in_=r[:, ot, :], func=AF.Square), reads=[RB], writes=[sq_b])
                    S.op("pe", lambda e, ot=ot: e.matmul(PS[5][:, :], lhsT=ones_bf[:, :], rhs=sq_t[:], start=(ot == 0), stop=(ot == 7)),
                         reads=[ONB, sq_b], writes=[PB[5]])
                S.op("act", lambda e: e.activation(out=mean[:], in_=PS[4][:, :], func=AF.Copy, scale=1.0 / 1024.0),
                     reads=[PB[4]], writes=[MEB])
                S.op("act", lambda e: e.activation(out=msq[:], in_=mean[:], func=AF.Square), reads=[MEB], writes=[MSB])
                S.op("dve", lambda e: e.scalar_tensor_tensor(out=msq[:], in0=PS[5][:, :], scalar=1.0 / 1024.0, in1=msq[:],
                                                             op0=ALU.mult, op1=ALU.subtract), reads=[PB[5], MSB], writes=[MSB])
                S.op("dve", lambda e: e.tensor_scalar(out=msq[:], in0=msq[:], scalar1=LN_EPS, scalar2=None, op0=ALU.add),
                     reads=[MSB], writes=[MSB])
                S.op("act", lambda e: e.activation(out=msq[:], in_=msq[:], func=AF.Ln), reads=[MSB], writes=[MSB])
                S.op("act", lambda e: e.activation(out=msq[:], in_=msq[:], func=AF.Exp, scale=-0.5), reads=[MSB], writes=[MSB])
                for ot in range(8):
                    t_t, t_b = tt[ot % 2]
                    o_t, o_b = outf[ot % 2]
                    S.op("dve", lambda e, ot=ot: e.tensor_tensor(out=t_t[:], in0=r[:, ot, :], in1=mean[:], op=ALU.subtract),
                         reads=[RB, MEB], writes=[t_b])
                    S.op("dve", lambda e: e.tensor_tensor(out=t_t[:], in0=t_t[:], in1=msq[:], op=ALU.mult),
                         reads=[t_b, MSB], writes=[t_b])
                    S.op("act", lambda e, ot=ot: e.activation(out=o_t[:], in_=t_t[:], func=AF.Identity,
                                                              scale=pv[:, goff + ot:goff + ot + 1], bias=pv[:, boff + ot:boff + ot + 1]),
                         reads=[t_b, PVB], writes=[o_b])
                    S.dma("sp", dst[ot * 128:(ot + 1) * 128, tok], o_t[:], reads=[o_b])
                    S.op("pool", lambda e, ot=ot: e.tensor_copy(out=xT_bf[:, ot, tok], in_=o_t[:]),
                         reads=[o_b], writes=[XB[c]])

            def ln_tmps(ph):
                sqt = [T(ph, "sqt%d" % i, [128, CH], BF16) for i in range(2)]
                rbs = [T(ph, "rbs%d" % i, [128, CH], BF16) for i in range(2)]
                mean, MEB = T(ph, "mean", [128, CH], F32)
                msq, MSB = T(ph, "msq", [128, CH], F32)
                tt = [T(ph, "lt%d" % i, [128, CH], F32) for i in range(2)]
                outf = [T(ph, "lo%d" % i, [128, CH], F32) for i in range(2)]
                return (sqt, mean, MEB, msq, MSB, tt, outf, rbs)

            with ExitStack() as ph:
                wbr, WBRB = T(ph, "wbr", [128, 12, 1024], BF16)
                wo, WOB = T(ph, "wo", [128, 8, 1024], BF16)
                WBRL = [Buf() for _ in range(3)]
                for br in range(3):
                    S.dma("pool", wbr[:, br * 4:(br + 1) * 4, :], w_branch[l, br].rearrange("(kc p) m -> p kc m", p=128), writes=[WBRL[br]])
                S.dma("pool", wo[:], w_o[l].rearrange("(kc p) m -> p kc m", p=128), writes=[WOB])
                ych = [T(ph, "ych%d" % i, [128, 12, CH], BF16) for i in range(2)]
                gtl = [T(ph, "gtl%d" % i, [128, 3, CH], BF16) for i in range(4)]
                xo = [T(ph, "xo%d" % i, [128, CH], F32) for i in range(4)]
                mg_v = mgT.rearrange("(br ot p) t -> ot p br t", br=3, ot=8)
                merged, MGB = T(ph, "merged", [128, 8, CH], BF16)
                r, RB = T(ph, "r", [128, 8, CH], F32)
                tmp, TMB = T(ph, "tmp", [128, CH], F32)
                macc, MAB = T(ph, "macc", [128, CH], F32)
                tmps = ln_tmps(ph)

                def load_y(c):
                    tok = slice(c * CH, (c + 1) * CH)
                    S.dma("sp", ych[c % 2][0][:], yT[:, tok].rearrange("(kc p) t -> p kc t", p=128), writes=[ych[c % 2][1]])

                def load_g(idx):
                    if idx >= NCH * 8:
                        return
                    c_, ot_ = idx // 8, idx % 8
                    g_t, g_b = gtl[idx % 4]
                    S.dma("sp", g_t[:], mg_v[ot_][:, :, c_ * CH:(c_ + 1) * CH], writes=[g_b])

                def load_x(c_, ot_):
                    x_t, x_b = xo[ot_ % 4]
                    S.dma("sp", x_t[:], xr_in[ot_ * 128:(ot_ + 1) * 128, c_ * CH:(c_ + 1) * CH], writes=[x_b])

                load_y(0)
                load_g(0)
                load_g(1)
                for c in range(NCH):
                    if c + 1 < NCH:
                        load_y(c + 1)
                    y_t, y_b = ych[c % 2]
                    for ot in range(8):
                        load_g(c * 8 + ot + 2)
                        if ot == 5:
                            load_x(c, 0)
                            load_x(c, 1)
                        g_t, g_b = gtl[(c * 8 + ot) % 4]
                        for br in range(3):
                            pb = next_ps6()
                            for kc in range(4):
                                S.op("pe", lambda e, kc=kc: e.matmul(PS[pb][:, :], lhsT=wbr[:, br * 4 + kc, ot * 128:(ot + 1) * 128],
                                                                     rhs=y_t[:, br * 4 + kc, :], start=(kc == 0), stop=(kc == 3)),
                                     reads=[WBRL[br], y_b], writes=[PB[pb]])
                            if br == 0:
                                S.op("dve", lambda e: e.tensor_tensor(out=macc[:], in0=PS[pb][:, :], in1=g_t[:, br, :], op=ALU.mult),
                                     reads=[PB[pb], g_b, MAB], writes=[MAB])
                            else:
                                S.op("dve", lambda e: e.tensor_tensor(out=tmp[:], in0=PS[pb][:, :], in1=g_t[:, br, :], op=ALU.mult),
                                     reads=[PB[pb], g_b, TMB], writes=[TMB])
                                if br == 1:
                                    S.op("pool", lambda e: e.tensor_tensor(out=macc[:], in0=macc[:], in1=tmp[:], op=ALU.add),
                                         reads=[TMB, MAB], writes=[MAB])
                                else:
                                    S.op("pool", lambda e: e.tensor_tensor(out=merged[:, ot, :], in0=macc[:], in1=tmp[:], op=ALU.add),
                                         reads=[TMB, MAB], writes=[MGB])
                    for ot in range(8):
                        if ot + 2 < 8:
                            load_x(c, ot + 2)
                        x_t, x_b = xo[ot % 4]
                        pb = next_ps6()
                        for kc in range(8):
                            S.op("pe", lambda e, kc=kc: e.matmul(PS[pb][:, :], lhsT=wo[:, kc, ot * 128:(ot + 1) * 128], rhs=merged[:, kc, :],
                                                                 start=(kc == 0), stop=(kc == 7)),
                                 reads=[WOB, MGB], writes=[PB[pb]])
                        S.op("dve", lambda e: e.scalar_tensor_tensor(out=r[:, ot, :], in0=x_t[:], scalar=ALPHA, in1=PS[pb][:, :],
                                                                     op0=ALU.mult, op1=ALU.add),
                             reads=[x_b, PB[pb]], writes=[RB])
                    layer_norm(ph, r, RB, c, PV_G1, PV_B1, xr_mid, tmps)
                S.barrier()
            if stop_phase <= 4:
                break

            p5o = ExitStack()
            wd, WDB = T(p5o, "wd", [128, 22, 1024], BF16)
            for q4 in range(2):
                S.dma("pool", wd[:, q4 * 11:(q4 + 1) * 11, :],
                      w_down[l, q4 * 1408:(q4 + 1) * 1408, :].rearrange("(kc p) m -> p kc m", p=128), writes=[WDB])
            with ExitStack() as ph:
                wu = [T(ph, "wu%d" % i, [128, 8, 256], BF16) for i in range(2)]
                hb = [[T(ph, "hb%d_%d" % (w_, i), [128, 2 + CH], F32) for i in range(2)] for w_ in range(2)]
                tcv = [[T(ph, "tcv%d_%d" % (w_, i), [128, CH], F32) for i in range(2)] for w_ in range(2)]
                gls = [T(ph, "gl%d" % i, [128, CH], F32) for i in range(2)]
                gsg = [T(ph, "gsg%d" % i, [128, CH], BF16) for i in range(2)]

                def load_wu(ft):
                    w_t, w_b = wu[ft % 2]
                    S.dma("pool", w_t[:, :, 0:128], w_up[l, :, ft * 128:(ft + 1) * 128].rearrange("(kc p) m -> p kc m", p=128), writes=[w_b])
                    S.dma("pool", w_t[:, :, 128:256], w_up[l, :, 2816 + ft * 128:2816 + (ft + 1) * 128].rearrange("(kc p) m -> p kc m", p=128),
                          writes=[w_b])

                def tail5(ft, c, par):
                    tok = slice(c * CH, (c + 1) * CH)
                    gl, GLB = gls[par]
                    g_t, g_b = gsg[par]
                    S.op("act", lambda e: e.activation(out=gl[:], in_=tcv[0][par][0][:], func=AF.Gelu), reads=[tcv[0][par][1], GLB], writes=[GLB])
                    S.op("pool", lambda e: e.tensor_tensor(out=g_t[:], in0=gl[:], in1=tcv[1][par][0][:], op=ALU.mult),
                         reads=[GLB, tcv[1][par][1]], writes=[g_b])
                    S.dma("sp", gT[ft * 128:(ft + 1) * 128, tok], g_t[:], reads=[g_b])

                load_wu(0)
                pend5 = None
                it5 = 0
                for ft in range(22):
                    if ft + 1 < 22:
                        load_wu(ft + 1)
                    w_t, w_b = wu[ft % 2]
                    for c in range(NCH):
                        tok = slice(c * CH, (c + 1) * CH)
                        par = it5 % 2
                        it5 += 1
                        for which in range(2):
                            pb = next_ps6((0, 1, 2, 3, 4, 5, 6, 7))
                            for kc in range(8):
                                S.op("pe", lambda e, kc=kc: e.matmul(PS[pb][:, :], lhsT=w_t[:, kc, which * 128:(which + 1) * 128],
                                                                     rhs=xT_bf[:, kc, tok], start=(kc == 0), stop=(kc == 7)),
                                     reads=[w_b, XB[c]], writes=[PB[pb]])
                            h_t, h_b = hb[which][c % 2]
                            p_t, p_b = hb[which][(c + 1) % 2]
                            col = ft + 22 * which
                            fw = lambda k: pv[:, PV_FW + col * 3 + k:PV_FW + col * 3 + k + 1]
                            t_t, t_b = tcv[which][par]
                            S.op("act", lambda e: e.activation(out=h_t[:, 2:2 + CH], in_=PS[pb][:, :], func=AF.Copy),
                                 reads=[PB[pb]], writes=[h_b])
                            S.op("act", lambda e: e.activation(out=t_t[:], in_=h_t[:, 2:2 + CH], func=AF.Identity, scale=fw(2),
                                                               bias=pv[:, PV_FB + col:PV_FB + col + 1]),
                                 reads=[h_b, PVB, t_b], writes=[t_b])
                            if c == 0:
                                S.op("pool", lambda e: e.memset(h_t[:, 0:2], 0.0), writes=[h_b])
                            else:
                                S.op("pool", lambda e: e.tensor_copy(out=h_t[:, 0:2], in_=p_t[:, CH:CH + 2]), reads=[p_b], writes=[h_b])
                            S.op("dve", lambda e: e.scalar_tensor_tensor(out=t_t[:], in0=h_t[:, 0:CH], scalar=fw(0), in1=t_t[:],
                                                                         op0=ALU.mult, op1=ALU.add), reads=[h_b, PVB, t_b], writes=[t_b])
                            S.op("dve", lambda e: e.scalar_tensor_tensor(out=t_t[:], in0=h_t[:, 1:CH + 1], scalar=fw(1), in1=t_t[:],
                                                                         op0=ALU.mult, op1=ALU.add), reads=[h_b, PVB, t_b], writes=[t_b])
                        if pend5 is not None:
                            tail5(*pend5)
                        pend5 = (ft, c, par)
                tail5(*pend5)
                S.barrier()

            with ExitStack() as ph:
                gch = [T(ph, "gdch%d" % i, [128, 22, CH], BF16) for i in range(2)]
                xo = [T(ph, "xdo%d" % i, [128, CH], F32) for i in range(4)]
                r, RB = T(ph, "r2", [128, 8, CH], F32)
                tmps = ln_tmps(ph)

                def load_gc(c):
                    tok = slice(c * CH, (c + 1) * CH)
                    S.dma("sp", gch[c % 2][0][:], gT[:, tok].rearrange("(kc p) t -> p kc t", p=128), writes=[gch[c % 2][1]])

                def load_x5(c_, ot_):
                    x_t, x_b = xo[ot_ % 4]
                    S.dma("sp", x_t[:], xr_mid[ot_ * 128:(ot_ + 1) * 128, c_ * CH:(c_ + 1) * CH], writes=[x_b])

                load_gc(0)
                for c in range(NCH):
                    if c + 1 < NCH:
                        load_gc(c + 1)
                    load_x5(c, 0)
                    load_x5(c, 1)
                    g_t, g_b = gch[c % 2]
                    for ot in range(8):
                        if ot + 2 < 8:
                            load_x5(c, ot + 2)
                        x_t, x_b = xo[ot % 4]
                        pb = next_ps6()
                        for kc in range(22):
                            S.op("pe", lambda e, kc=kc: e.matmul(PS[pb][:, :], lhsT=wd[:, kc, ot * 128:(ot + 1) * 128], rhs=g_t[:, kc, :],
                                                                 start=(kc == 0), stop=(kc == 21)),
                                 reads=[WDB, g_b], writes=[PB[pb]])
                        S.op("dve", lambda e: e.scalar_tensor_tensor(out=r[:, ot, :], in0=x_t[:], scalar=ALPHA, in1=PS[pb][:, :],
                                                                     op0=ALU.mult, op1=ALU.add),
                             reads=[x_b, PB[pb]], writes=[RB])
                    layer_norm(ph, r, RB, c, PV_G2, PV_B2, xr_out, tmps)
                S.barrier()
            p5o.close()
        S.barrier()
    return nc


def host_constants():
    f32 = np.float32
    pos = np.arange(S_LEN, dtype=f32)
    inv_freq = (f32(500000.0) ** (-np.arange(0, 16, 2, dtype=f32) / f32(16))).astype(f32)
    ang = (pos[:, None] * inv_freq[None, :]).astype(f32)
    cos = np.cos(ang.astype(np.float64)).astype(f32).T
    sin = np.sin(ang.astype(np.float64)).astype(f32).T
    C64 = np.ones((64, S_LEN), f32)
    S64 = np.zeros((64, S_LEN), f32)
    C64[0:8] = cos
    C64[8:16] = cos
    S64[0:8] = -sin
    S64[8:16] = sin
    ropeC = np.concatenate([C64, C64], 0)
    ropeS = np.concatenate([S64, S64], 0)
    permT = np.zeros((128, 128), f32)
    for m in range(128):
        r = m % 64
        if r < 8:
            permT[m + 8, m] = 1.0
        elif r < 16:
            permT[m - 8, m] = 1.0
    ident = np.eye(128, dtype=f32)
    expand = np.zeros((64, S_LEN), f32)
    for j in range(64):
        expand[j, 64 * j:64 * (j + 1)] = 1.0
    n = np.arange(256)
    cstart = n * 16
    sstart = np.arange(64) * 64
    ov = np.maximum(np.minimum((cstart + 32)[:, None], (sstart + 64)[None, :])
                    - np.maximum(cstart[:, None], sstart[None, :]), 0).astype(f32) / f32(32)
    ov[255] = 0.0
    ovl = np.concatenate([ov, np.ones((256, 1), f32)], 1)
    ovl1 = np.ascontiguousarray(ovl.reshape(2, 128, 65).transpose(1, 0, 2))
    t = np.arange(S_LEN)
    cur = t // 64
    jj = np.arange(64)
    forced = (jj[None, :] == 0) | ((jj[None, :] <= cur[:, None]) & (jj[None, :] > cur[:, None] - 2))
    future = jj[None, :] > cur[:, None]
    cbig = np.where(forced, f32(1e30), f32(0.0)).astype(f32)
    cfut = np.where(future, f32(-1e30), f32(1e30)).astype(f32)
    cbig = np.ascontiguousarray(cbig.reshape(32, 128, 64).transpose(1, 0, 2))
    cfut = np.ascontiguousarray(cfut.reshape(32, 128, 64).transpose(1, 0, 2))
    gsel = np.zeros((24, 24 * 64), f32)
    for j in range(24):
        gsel[j, j * 64:(j + 1) * 64] = 1.0
    return dict(ropeC=ropeC, ropeS=ropeS, permT=permT, ident=ident, expand=expand, ovl1=ovl1,
                cbig=cbig, cfut=cfut, gsel=gsel)


def host_params(inp):
    f32 = np.float32
    pvec = np.zeros((4, 128, PV_N), f32)
    for l in range(4):
        ca = inp["conv_a_w"][l]
        pvec[l, :, PV_CA:PV_CA + 12] = ca.reshape(3, 4, 128).transpose(2, 1, 0).reshape(128, 12)
        pvec[l, :, PV_SL] = inp["diff_subln"][l]
        pvec[l, :, PV_G1:PV_G1 + 8] = inp["ln1_g"][l].reshape(8, 128).T
        pvec[l, :, PV_B1:PV_B1 + 8] = inp["ln1_b"][l].reshape(8, 128).T
        pvec[l, :, PV_G2:PV_G2 + 8] = inp["ln2_g"][l].reshape(8, 128).T
        pvec[l, :, PV_B2:PV_B2 + 8] = inp["ln2_b"][l].reshape(8, 128).T
        fw = inp["ffn_conv_w"][l]
        pvec[l, :, PV_FW:PV_FW + 132] = fw.reshape(3, 44, 128).transpose(2, 1, 0).reshape(128, 132)
        pvec[l, :, PV_FB:PV_FB + 44] = inp["ffn_conv_b"][l].reshape(44, 128).T
    pos = inp["nsa_cmp_pos"]
    pT = np.ascontiguousarray(pos.transpose(0, 1, 3, 2))
    posT = np.concatenate([pT, pT], axis=2)
    lamrep = np.ascontiguousarray(np.broadcast_to(inp["diff_lambda"].reshape(4, 1, 256), (4, 128, 256))).astype(f32)
    return dict(pvec=pvec, posT=np.ascontiguousarray(posT), lamrep=lamrep)


_NC_CACHE = {}


def make_in_maps(inputs, n_cores=8):
    inp = {k: np.asarray(v) for k, v in inputs.items()}
    shared = dict(host_constants())
    shared.update(host_params(inp))
    for k in ("w_in", "w_branch", "w_o", "ffn_w_up", "ffn_w_down", "nsa_phi_w1", "nsa_phi_w2"):
        shared[k] = np.ascontiguousarray(inp[k], dtype=np.float32)
    maps = []
    for c in range(n_cores):
        m = dict(shared)
        m["xT"] = np.ascontiguousarray(inp["x"][c % 4].T)
        maps.append(m)
    return maps


def kernel(**inputs):
    if "nc" not in _NC_CACHE:
        _NC_CACHE["nc"] = build()
    nc = _NC_CACHE["nc"]
    maps = make_in_maps(inputs, 8)
    res = run_bass_kernel_spmd(nc, maps, core_ids=list(range(8)))
    out = np.stack([np.ascontiguousarray(res.results[b]["outT"].T) for b in range(4)], 0)
    return out.astype(np.float32)
```

```python
import math
import os
from contextlib import ExitStack

import numpy as np
import concourse.bass as bass
import concourse.mybir as mybir
from concourse.bass_utils import run_bass_kernel_spmd

F32 = mybir.dt.float32
F32R = mybir.dt.float32r
BF16 = mybir.dt.bfloat16
AF = mybir.ActivationFunctionType
ALU = mybir.AluOpType
AX = mybir.AxisListType

S_LEN = 4096
NCH = 8
CH = 512
ALPHA = 8.0 ** 0.25
LN_EPS = 1e-5
NEGB = -30000.0
N_JUNK = int(os.environ.get('N_JUNK', '0'))

PV_CA, PV_SL, PV_G1, PV_B1, PV_G2, PV_B2, PV_FW, PV_FB, PV_N = 0, 12, 13, 21, 29, 37, 45, 177, 221


class Buf:
    __slots__ = ("name", "w", "r", "excl")

    def __init__(self, name="", excl=False):
        self.name = name
        self.w = None
        self.r = {}
        self.excl = excl


class Sched:
    def __init__(self, nc, es):
        self.nc = nc
        self.eng = {"pe": nc.tensor, "act": nc.scalar, "dve": nc.vector,
                    "pool": nc.gpsimd, "sp": nc.sync}
        self.sem = {}
        self.cnt = {}
        for k in ("pe", "act", "dve", "pool"):
            self.sem[k] = es.enter_context(nc.semaphore("s_" + k))
            self.cnt[k] = 0
        self.dsems = {}
        self.dnext = {}
        for q, n in {"sp": 16, "pool": 8}.items():
            self.dsems[q] = []
            for i in range(n):
                key = "d_%s_%d" % (q, i)
                self.sem[key] = es.enter_context(nc.semaphore(key))
                self.cnt[key] = 0
                self.dsems[q].append(key)
            self.dnext[q] = 0
        self.seen = {k: {} for k in self.eng}
        self.n_inst = 0
        self.n_wait = 0

    def _wait(self, ek, deps):
        e = self.eng[ek]
        seen = self.seen[ek]
        for key, val in deps.items():
            if val <= 0:
                continue
            if key == ek and ek == "pe":
                continue
            if seen.get(key, 0) >= val:
                continue
            e.wait_ge(self.sem[key], val)
            seen[key] = val
            self.n_wait += 1

    @staticmethod
    def _collect(reads, writes):
        deps = {}
        for b in reads:
            if b.w is not None and deps.get(b.w[0], 0) < b.w[1]:
                deps[b.w[0]] = b.w[1]
        for b in writes:
            if b.w is not None and deps.get(b.w[0], 0) < b.w[1]:
                deps[b.w[0]] = b.w[1]
            for k, v in b.r.items():
                if deps.get(k, 0) < v:
                    deps[k] = v
        return deps

    @staticmethod
    def _update(key, val, reads, writes):
        for b in reads:
            if b.r.get(key, 0) < val:
                b.r[key] = val
        for b in writes:
            b.w = (key, val)
            b.r = {}

    def op(self, ek, fn, reads=(), writes=()):
        xr = [b for b in reads if b.excl]
        self._wait(ek, self._collect(reads, list(writes) + xr))
        inst = fn(self.eng[ek])
        self.cnt[ek] += 1
        inst.then_inc(self.sem[ek], 1)
        self._update(ek, self.cnt[ek], reads, writes)
        self.n_inst += 1

    def dma(self, q, out, in_, reads=(), writes=()):
        lst = self.dsems[q]
        key = lst[self.dnext[q] % len(lst)]
        self.dnext[q] += 1
        deps = self._collect(reads, writes)
        if self.cnt[key] > 0:
            deps[key] = max(deps.get(key, 0), self.cnt[key])
        self._wait(q, deps)
        inst = self.eng[q].dma_start(out=out, in_=in_)
        self.cnt[key] += 16
        inst.then_inc(self.sem[key], 16)
        self._update(key, self.cnt[key], reads, writes)
        self.n_inst += 1

    def cc(self, kind, ins, outs, groups, reads=(), writes=()):
        q = "pool"
        lst = self.dsems[q]
        key = lst[self.dnext[q] % len(lst)]
        self.dnext[q] += 1
        deps = self._collect(reads, writes)
        if self.cnt[key] > 0:
            deps[key] = max(deps.get(key, 0), self.cnt[key])
        self._wait(q, deps)
        inst = self.eng[q].collective_compute(kind, ALU.bypass, replica_groups=groups, ins=ins, outs=outs)
        self.cnt[key] += 16
        inst.then_inc(self.sem[key], 16)
        self._update(key, self.cnt[key], reads, writes)
        self.n_inst += 1

    def barrier(self):
        deps = {k: v for k, v in self.cnt.items() if v > 0}
        for ek in self.eng:
            self._wait(ek, dict(deps))


class Stream:
    def __init__(self, lookahead=2):
        self.pend = []
        self.la = lookahead
        self.count = 0
        self.deferred = []

    def push(self, s_fn, post_fn):
        s_fn()
        self.pend.append(post_fn)
        if len(self.pend) > self.la:
            self._pop()

    def _pop(self):
        fn = self.pend.pop(0)
        fn()
        self.count += 1
        while self.deferred and self.deferred[0][0] <= self.count:
            self.deferred.pop(0)[1]()

    def defer(self, n, fn):
        self.deferred.append((self.count + 1 + n, fn))

    def flush(self):
        while self.pend:
            self._pop()
        while self.deferred:
            self.deferred.pop(0)[1]()


def build(n_layers=4, debug=False, stop_phase=99, max_tiles=999, do_v=True, tile_lo=0):
    nc = bass.Bass("TRN2", target_bir_lowering=False)

    def DI(name, shape, dt=F32):
        return nc.dram_tensor(name, list(shape), dt, kind="ExternalInput").ap()

    def DS(name, shape, dt=BF16):
        kind = "ExternalOutput" if debug else "Internal"
        return nc.dram_tensor(name, list(shape), dt, kind=kind).ap()

    xT_in = DI("xT", [1024, S_LEN])
    w_in = DI("w_in", [4, 1024, 7448])
    w_branch = DI("w_branch", [4, 3, 512, 1024])
    w_o = DI("w_o", [4, 1024, 1024])
    w_up = DI("ffn_w_up", [4, 1024, 5632])
    w_down = DI("ffn_w_down", [4, 2816, 1024])
    phi_w1 = DI("nsa_phi_w1", [4, 2, 2048, 128])
    phi_w2 = DI("nsa_phi_w2", [4, 2, 128, 64])
    pvec = DI("pvec", [4, 128, PV_N])
    posT = DI("posT", [4, 2, 128, 32])
    lamrep = DI("lamrep", [4, 128, 256])
    ropeC_d = DI("ropeC", [128, S_LEN])
    ropeS_d = DI("ropeS", [128, S_LEN])
    permT_d = DI("permT", [128, 128])
    ident_d = DI("ident", [128, 128])
    expand_d = DI("expand", [64, S_LEN])
    ovl1_d = DI("ovl1", [128, 2, 65])
    cbig_d = DI("cbig", [128, 32, 64])
    cfut_d = DI("cfut", [128, 32, 64])
    gsel_d = DI("gsel", [24, 24 * 64])

    outT = nc.dram_tensor("outT", [1024, S_LEN], F32, kind="ExternalOutput").ap()

    yT = DS("yT", [1536, S_LEN])
    qdT = DS("qdT", [512, S_LEN])
    kdT = DS("kdT", [512, S_LEN])
    vd = DS("vd", [S_LEN, 512])
    qnT = DS("qnT", [512, S_LEN])
    kcT = DS("kcT", [128, S_LEN])
    vcT = DS("vcT", [128, S_LEN])
    ksT = DS("ksT", [128, S_LEN])
    kwT = DS("kwT", [128, S_LEN])
    vaug = DS("vaug", [4, S_LEN, 128])
    ngT = DS("ngT", [24, S_LEN])
    mgT = DS("mgT", [3072, S_LEN])
    gT = DS("gT", [2816, S_LEN])
    xres = [DS("xresA", [1024, S_LEN], F32), DS("xresB", [1024, S_LEN], F32)]

    with ExitStack() as es:
        S = Sched(nc, es)

        uid = [0]

        def T(stack, name, shape, dt):
            uid[0] += 1
            return stack.enter_context(nc.sbuf_tensor("%s_u%d" % (name, uid[0]), list(shape), dt)), Buf(name)

        PSP = [es.enter_context(nc.psum_tensor("psp%d" % i, [128, 2 * 512], F32)) for i in range(2)]
        PS = []
        PB = []
        for i in range(8):
            if i < 4:
                PS.append(PSP[i // 2][:, (i % 2) * 512:(i % 2 + 1) * 512])
            else:
                PS.append(es.enter_context(nc.psum_tensor("ps%d" % i, [128, 512], F32)))
            PB.append(Buf("ps%d" % i, excl=True))

        xT_bf = es.enter_context(nc.sbuf_tensor("xT_bf", [128, 8, S_LEN], BF16))
        XB = [Buf("xb%d" % c) for c in range(NCH)]
        pv, PVB = T(es, "pv", [128, PV_N], F32)
        ones_bf, ONB = T(es, "ones_bf", [128, 128], BF16)
        ones_f, ONF = T(es, "ones_f", [128, 128], F32)
        permT, PMB = T(es, "permT_sb", [128, 128], BF16)
        ident, IDB = T(es, "ident_sb", [128, 128], BF16)
        lamt, LAMB = T(es, "lamt", [128, 256], F32)
        lamw, LAMW = T(es, "lamw", [128, 64], F32)
        lams, LAMS = T(es, "lams", [128, 8], F32)

        S.op("dve", lambda e: e.memset(ones_bf[:], 1.0), writes=[ONB])
        S.op("dve", lambda e: e.memset(ones_f[:], 1.0), writes=[ONF])
        S.dma("pool", permT[:], permT_d[:], writes=[PMB])
        S.dma("pool", ident[:], ident_d[:], writes=[IDB])

        for c in range(NCH):
            S.dma("pool", xT_bf[:, :, c * CH:(c + 1) * CH],
                  xT_in[:, c * CH:(c + 1) * CH].rearrange("(kc p) t -> p kc t", p=128), writes=[XB[c]])

        psrot = [0]

        def next_ps(lo=0, hi=4):
            i = lo + psrot[0] % (hi - lo)
            psrot[0] += 1
            return i

        rot6 = [0]

        def next_ps6(banks=(0, 1, 2, 3, 6, 7)):
            i = banks[rot6[0] % len(banks)]
            rot6[0] += 1
            return i

        for l in range(n_layers):
            lam_init = 0.8 - 0.6 * math.exp(-0.3 * l)
            xr_in = xT_in if l == 0 else xres[(l + 1) % 2]
            xr_mid = xres[l % 2]
            xr_out = outT if l == n_layers - 1 else xres[(l + 1) % 2]
            if l > 0:
                xr_in = xres[1]
                xr_mid = xres[0]
                xr_out = outT if l == n_layers - 1 else xres[1]
            else:
                xr_in = xT_in
                xr_mid = xres[0]
                xr_out = outT if l == n_layers - 1 else xres[1]

            S.dma("sp", pv[:], pvec[l], writes=[PVB])
            S.dma("sp", lamt[:], lamrep[l], writes=[LAMB])
            for j in range(2):
                S.op("dve", lambda e, j=j: e.tensor_tensor(out=lamw[:], in0=lamt[:, j * 128:j * 128 + 64],
                                                           in1=lamt[:, j * 128 + 64:j * 128 + 128], op=ALU.mult),
                     reads=[LAMB], writes=[LAMW])
                S.op("dve", lambda e, j=j: e.reduce_sum(out=lams[:, j:j + 1], in_=lamw[:], axis=AX.X),
                     reads=[LAMW], writes=[LAMS])
                S.op("act", lambda e, j=j: e.activation(out=lams[:, 2 + j:3 + j], in_=lams[:, j:j + 1], func=AF.Exp),
                     reads=[LAMS], writes=[LAMS])
            S.op("dve", lambda e: e.tensor_tensor(out=lams[:, 4:5], in0=lams[:, 2:3], in1=lams[:, 3:4], op=ALU.subtract),
                 reads=[LAMS], writes=[LAMS])
            S.op("dve", lambda e: e.tensor_scalar(out=lams[:, 5:6], in0=lams[:, 4:5], scalar1=lam_init, scalar2=-1.0,
                                                  op0=ALU.add, op1=ALU.mult), reads=[LAMS], writes=[LAMS])
            S.op("dve", lambda e: e.tensor_scalar(out=lams[:, 6:7], in0=pv[:, PV_SL:PV_SL + 1], scalar1=1.0 - lam_init,
                                                  scalar2=None, op0=ALU.mult), reads=[PVB, LAMS], writes=[LAMS])

            with ExitStack() as ph:
                ropeC, RCB = T(ph, "ropeC_sb", [128, S_LEN], F32)
                ropeS, RSB = T(ph, "ropeS_sb", [128, S_LEN], F32)
                S.dma("sp", ropeC[:], ropeC_d[:], writes=[RCB])
                S.dma("sp", ropeS[:], ropeS_d[:], writes=[RSB])
                wb = []
                for i in range(2):
                    wb.append(T(ph, "wb%d" % i, [128, 8, 128], BF16))
                ac_buf = ph.enter_context(nc.sbuf_tensor("ac_buf%d" % l, [128, S_LEN], F32))
                ACB = [Buf() for _ in range(NCH)]
                cvp = ph.enter_context(nc.sbuf_tensor("cvp%d" % l, [128, 2 + S_LEN], F32))
                CVB = [Buf() for _ in range(NCH)]
                CVH = Buf()
                S.op("pool", lambda e: e.memset(cvp[:, 0:2], 0.0), writes=[CVH])
                stg = [T(ph, "stg%d" % i, [128, CH], BF16) for i in range(4)]
                zbs = [T(ph, "zb%d" % i, [128, CH], BF16) for i in range(2)]
                t1s = [T(ph, "t1_%d" % i, [128, CH], F32) for i in range(2)]
                t2s = [T(ph, "t2_%d" % i, [128, CH], F32) for i in range(2)]
                rot = {"stg": 0, "zb": 0, "t": 0}

                def nxt(key, lst):
                    i = rot[key] % len(lst)
                    rot[key] += 1
                    return lst[i]

                wv, WVB = T(ph, "wv", [128, 8, 512], BF16)
                wv2, WV2B = T(ph, "wv2", [128, 8, 256], BF16)
                vst = [T(ph, "vst%d" % i, [128, 4, 128], BF16) for i in range(2)]
                for (vs_t, vs_b) in vst:
                    S.op("pool", lambda e, vs_t=vs_t: e.memset(vs_t[:], 1.0), writes=[vs_b])
                S.dma("pool", wv[:], w_in[l, :, 2560:3072].rearrange("(kc p) m -> p kc m", p=128), writes=[WVB])
                S.dma("pool", wv2[:, :, 0:128], w_in[l, :, 3968:4096].rearrange("(kc p) m -> p kc m", p=128), writes=[WV2B])
                S.dma("pool", wv2[:, :, 128:256], w_in[l, :, 4224:4352].rearrange("(kc p) m -> p kc m", p=128), writes=[WV2B])
                tiles = []
                for i in range(4):
                    tiles.append((512 + 128 * i, 128, "ac", None))
                    tiles.append((1024 + 128 * i, 128, "av", None))
                    tiles.append((128 * i, 128, "ab", i))
                for i in range(4):
                    tiles.append((1536 + 128 * i, 128, "rope", qdT[128 * i:128 * (i + 1), :]))
                for i in range(4):
                    tiles.append((2048 + 128 * i, 128, "rope", kdT[128 * i:128 * (i + 1), :]))
                for h in range(8):
                    tiles.append((3072 + 64 * h, 64, "rope", qnT[64 * h:64 * (h + 1), :]))
                tiles.append((3584 + 0 * 128, 128, "rope", kcT))
                tiles.append((3584 + 1 * 128, 128, "copy", vcT))
                tiles.append((3584 + 2 * 128, 128, "rope", ksT))
                tiles.append((3584 + 4 * 128, 128, "rope", kwT))
                tiles.append((4352, 24, "sig", ngT))
                for i in range(24):
                    tiles.append((4376 + 128 * i, 128, "sig", mgT[128 * i:128 * (i + 1), :]))

                def load_w(n):
                    c0, M, _, _ = tiles[n]
                    wt, wbuf = wb[n % 2]
                    S.dma("pool", wt[:, :, 0:M],
                          w_in[l, :, c0:c0 + M].rearrange("(kc p) m -> p kc m", p=128), writes=[wbuf])

                tiles = tiles[tile_lo:max_tiles]
                load_w(0)
                for n, (c0, M, kind, dst) in enumerate(tiles):
                    if n + 1 < len(tiles):
                        load_w(n + 1)
                    wt, wbuf = wb[n % 2]
                    for c in range(NCH):
                        tok = slice(c * CH, (c + 1) * CH)
                        pb = next_ps6()
                        for kc in range(8):
                            S.op("pe", lambda e, kc=kc, pb=pb: e.matmul(PS[pb][0:M, :], lhsT=wt[:, kc, 0:M],
                                                                      rhs=xT_bf[:, kc, tok], start=(kc == 0), stop=(kc == 7)),
                                 reads=[wbuf, XB[c]], writes=[PB[pb]])
                        if kind == "ac":
                            S.op("act", lambda e: e.activation(out=ac_buf[:, tok], in_=PS[pb][:, :], func=AF.Copy),
                                 reads=[PB[pb]], writes=[ACB[c]])
                        elif kind == "av":
                            S.op("dve", lambda e: e.tensor_tensor(out=cvp[:, 2 + c * CH:2 + (c + 1) * CH], in0=PS[pb][:, :],
                                                                  in1=ac_buf[:, tok], op=ALU.mult),
                                 reads=[PB[pb], ACB[c]], writes=[CVB[c]])
                        elif kind == "ab":
                            i = dst
                            t1, t1b = nxt("t", t1s)
                            sg, sgb = nxt("stg", stg)
                            halo = [CVB[c - 1]] if c > 0 else [CVH]
                            cw = lambda k: pv[:, PV_CA + i * 3 + k:PV_CA + i * 3 + k + 1]
                            S.op("dve", lambda e: e.tensor_scalar(out=t1[:], in0=cvp[:, c * CH:c * CH + CH], scalar1=cw(0),
                                                                  scalar2=None, op0=ALU.mult),
                                 reads=[CVB[c], PVB] + halo, writes=[t1b])
                            S.op("dve", lambda e: e.scalar_tensor_tensor(out=t1[:], in0=cvp[:, c * CH + 1:c * CH + CH + 1],
                                                                         scalar=cw(1), in1=t1[:], op0=ALU.mult, op1=ALU.add),
                                 reads=[CVB[c], PVB, t1b] + halo, writes=[t1b])
                            S.op("dve", lambda e: e.scalar_tensor_tensor(out=t1[:], in0=cvp[:, c * CH + 2:c * CH + CH + 2],
                                                                         scalar=cw(2), in1=t1[:], op0=ALU.mult, op1=ALU.add),
                                 reads=[CVB[c], PVB, t1b], writes=[t1b])
                            S.op("dve", lambda e: e.tensor_tensor(out=sg[:], in0=PS[pb][:, :], in1=t1[:], op=ALU.mult),
                                 reads=[PB[pb], t1b], writes=[sgb])
                            S.dma("sp", yT[128 * i:128 * (i + 1), tok], sg[:], reads=[sgb])
                        elif kind == "rope":
                            zb, zbb = nxt("zb", zbs)
                            t1, t1b = t1s[rot["t"] % 2]
                            t2, t2b = t2s[rot["t"] % 2]
                            rot["t"] += 1
                            sg, sgb = nxt("stg", stg)
                            pw = next_ps(4, 6)
                            S.op("act", lambda e: e.activation(out=zb[0:M, :], in_=PS[pb][0:M, :], func=AF.Copy),
                                 reads=[PB[pb]], writes=[zbb])
                            if os.environ.get("ROPE_DBG") != "nope":
                                S.op("pe", lambda e: e.matmul(PS[pw][0:M, :], lhsT=permT[0:M, 0:M], rhs=zb[0:M, :],
                                                              start=True, stop=True),
                                     reads=[PMB, zbb], writes=[PB[pw]])
                            dbg = os.environ.get("ROPE_DBG", "")
                            if dbg == "b":
                                S.dma("sp", dst[0:M, tok], zb[0:M, :], reads=[zbb])
                                continue
                            if dbg == "c":
                                S.op("dve", lambda e: e.tensor_tensor(out=t1[0:M, :], in0=PS[pb][0:M, :], in1=ac_buf[0:M, tok],
                                                                      op=ALU.mult), reads=[PB[pb]], writes=[t1b])
                            else:
                                S.op("dve", lambda e: e.tensor_tensor(out=t1[0:M, :], in0=PS[pb][0:M, :], in1=ropeC[0:M, tok],
                                                                      op=ALU.mult), reads=[PB[pb], RCB, zbb], writes=[t1b])
                            if dbg == "c":
                                S.op("dve", lambda e: e.tensor_copy(out=sg[0:M, :], in_=t1[0:M, :]), reads=[t1b], writes=[sgb])
                                S.dma("sp", dst[0:M, tok], sg[0:M, :], reads=[sgb])
                                continue
                            S.op("dve", lambda e: e.tensor_tensor(out=t2[0:M, :], in0=PS[pw][0:M, :], in1=ropeS[0:M, tok],
                                                                  op=ALU.mult), reads=[PB[pw], RSB], writes=[t2b])
                            S.op("pool", lambda e: e.tensor_tensor(out=sg[0:M, :], in0=t1[0:M, :], in1=t2[0:M, :], op=ALU.add),
                                 reads=[t1b, t2b], writes=[sgb])
                            S.dma("sp", dst[0:M, tok], sg[0:M, :], reads=[sgb])
                        elif kind in ("sig", "copy"):
                            sg, sgb = nxt("stg", stg)
                            fn = AF.Sigmoid if kind == "sig" else AF.Copy
                            S.op("act", lambda e: e.activation(out=sg[0:M, :], in_=PS[pb][0:M, :], func=fn),
                                 reads=[PB[pb]], writes=[sgb])
                            S.dma("sp", dst[0:M, tok], sg[0:M, :], reads=[sgb])

                for tt in range(32 if do_v else 0):
                    c = tt // 4
                    tk = slice(tt * 128, (tt + 1) * 128)
                    pb = next_ps6()
                    for kc in range(8):
                        S.op("pe", lambda e, kc=kc: e.matmul(PS[pb][:, :], lhsT=xT_bf[:, kc, tk], rhs=wv[:, kc, :],
                                                             start=(kc == 0), stop=(kc == 7)),
                             reads=[WVB, XB[c]], writes=[PB[pb]])
                    sg, sgb = nxt("stg", stg)
                    S.op("act", lambda e: e.activation(out=sg[:], in_=PS[pb][:, :], func=AF.Copy), reads=[PB[pb]], writes=[sgb])
                    S.dma("sp", vd[tk, :], sg[:], reads=[sgb])
                    pb2 = next_ps6()
                    for kc in range(8):
                        S.op("pe", lambda e, kc=kc: e.matmul(PS[pb2][:, 0:256], lhsT=xT_bf[:, kc, tk], rhs=wv2[:, kc, :],
                                                             start=(kc == 0), stop=(kc == 7)),
                             reads=[WV2B, XB[c]], writes=[PB[pb2]])
                    vs_t, vs_b = vst[tt % 2]
                    S.op("act", lambda e: e.activation(out=vs_t[:, :, 0:64],
                                                       in_=PS[pb2][:, 0:256].rearrange("p (i d) -> p i d", d=64), func=AF.Copy),
                         reads=[PB[pb2]], writes=[vs_b])
                    S.dma("sp", vaug[:, tk, :].rearrange("i p c -> p i c"), vs_t[:], reads=[vs_b])
                S.barrier()
            if stop_phase <= 1:
                break

            with ExitStack() as ph:
                vall, VAB = T(ph, "vall", [128, 32, 512], BF16)
                S.dma("sp", vall[:], vd.rearrange("(kt p) c -> p kt c", p=128), writes=[VAB])
                qk = [(T(ph, "dq%d" % i, [128, S_LEN], BF16), T(ph, "dk%d" % i, [128, S_LEN], BF16)) for i in range(2)]
                pbufs = [T(ph, "pd%d" % i, [128, 2 * CH], BF16) for i in range(3)]
                osets = []
                for i in range(2):
                    osets.append(dict(r1=T(ph, "r1_%d" % i, [128, CH], F32), o1=T(ph, "o1_%d" % i, [128, CH], F32),
                                      o2=T(ph, "o2_%d" % i, [128, CH], F32), sq=T(ph, "sq_%d" % i, [128, CH], BF16)))
                ysg = [T(ph, "ysg%d" % i, [128, CH], BF16) for i in range(2)]
                prot = [0]
                st = Stream(1)
                prot2 = [0]

                def load_qk(h):
                    (qt, qb), (kt_, kb) = qk[h % 2]
                    S.dma("sp", qt[:], qdT[128 * h:128 * (h + 1), :], writes=[qb])
                    S.dma("sp", kt_[:], kdT[128 * h:128 * (h + 1), :], writes=[kb])

                def mk_step(h, qc, m, kts, nkt, oset, gi):
                    (qt, qb), (kt_, kb) = qk[h % 2]
                    po, pl = 4 + m, 6 + m
                    rows = slice(m * 64, (m + 1) * 64)
                    kt = kts[-1]
                    j = kts[0] - 4 * qc
                    c0 = 128 * j if j > 0 else 0
                    box = {}

                    def s_fn():
                        pr = prot2[0] % 2
                        prot2[0] += 1
                        box["pr"] = pr
                        for idx, k in enumerate(kts):
                            bk = 2 * pr + idx
                            S.op("pe", lambda e: e.matmul(PS[bk][:, c0:CH], lhsT=kt_[rows, k * 128:(k + 1) * 128],
                                                          rhs=qt[rows, qc * CH + c0:(qc + 1) * CH], start=True, stop=True),
                                 reads=[qb, kb], writes=[PB[bk]])

                    def post_fn():
                        pr = box["pr"]
                        pt, ptb = pbufs[prot[0] % 3]
                        prot[0] += 1
                        nk = len(kts)
                        banks = [PB[2 * pr + idx] for idx in range(nk)]
                        if nk == 2 and os.environ.get("PAIR_EXP", "1") == "1":
                            S.op("act", lambda e: e.activation(out=pt[:, 0:2 * CH], in_=PSP[pr][:, :], func=AF.Exp,
                                                               scale=0.125), reads=banks, writes=[ptb])
                        elif nk == 2:
                            for idx in range(2):
                                S.op("act", lambda e: e.activation(out=pt[:, idx * CH:(idx + 1) * CH], in_=PS[2 * pr + idx][:, :], func=AF.Exp,
                                                                   scale=0.125), reads=banks, writes=[ptb])
                        else:
                            S.op("act", lambda e: e.activation(out=pt[:, c0:CH], in_=PS[2 * pr][:, c0:CH], func=AF.Exp,
                                                               scale=0.125), reads=banks, writes=[ptb])
                            if j >= 0:
                                S.op("pool", lambda e: e.affine_select(out=pt[:, c0:c0 + 128], in_=pt[:, c0:c0 + 128],
                                                                       pattern=[[1, 128]], compare_op=ALU.is_ge, fill=0.0,
                                                                       base=0, channel_multiplier=-1),
                                     reads=[ptb], writes=[ptb])
                        for idx, k in enumerate(kts):
                            o0 = idx * CH
                            S.op("pe", lambda e: e.matmul(PS[po][:, c0:CH], lhsT=vall[:, k, h * 128:(h + 1) * 128],
                                                          rhs=pt[:, o0 + c0:o0 + CH], start=(k == 0), stop=(k == nkt - 1)),
                                 reads=[VAB, ptb], writes=[PB[po]])
                            S.op("pe", lambda e: e.matmul(PS[pl][:, c0:CH], lhsT=ones_bf[:, :],
                                                          rhs=pt[:, o0 + c0:o0 + CH], start=(k == 0), stop=(k == nkt - 1)),
                                 reads=[ONB, ptb], writes=[PB[pl]])
                        if kt != nkt - 1:
                            return
                        r1, R1B = oset["r1"]
                        o1, O1B = oset["o1"]
                        o2, O2B = oset["o2"]
                        sq, SQB = oset["sq"]
                        om, OMB = (o1, O1B) if m == 0 else (o2, O2B)
                        S.op("dve", lambda e: e.reciprocal(out=r1[:], in_=PS[pl][:, :]), reads=[PB[pl], R1B], writes=[R1B])
                        S.op("dve", lambda e: e.tensor_tensor(out=om[:], in0=PS[po][:, :], in1=r1[:], op=ALU.mult),
                             reads=[PB[po], R1B, OMB], writes=[OMB])
                        if m == 0:
                            return
                        S.op("dve", lambda e: e.scalar_tensor_tensor(out=o1[:], in0=o2[:], scalar=lams[:, 5:6], in1=o1[:],
                                                                     op0=ALU.mult, op1=ALU.add),
                             reads=[O2B, O1B, LAMS], writes=[O1B])
                        S.op("pool", lambda e: e.tensor_tensor(out=sq[:], in0=o1[:], in1=o1[:], op=ALU.mult),
                             reads=[O1B, SQB], writes=[SQB])

                        def final():
                            tok = slice(qc * CH, (qc + 1) * CH)
                            S.op("pe", lambda e: e.matmul(PS[7][:, :], lhsT=ones_bf[:, :], rhs=sq[:], start=True, stop=True),
                                 reads=[ONB, SQB], writes=[PB[7]])
                            S.op("dve", lambda e: e.tensor_scalar(out=o2[:], in0=PS[7][:, :], scalar1=1.0 / 128.0, scalar2=LN_EPS,
                                                                  op0=ALU.mult, op1=ALU.add), reads=[PB[7], O2B], writes=[O2B])
                            S.op("act", lambda e: e.activation(out=o2[:], in_=o2[:], func=AF.Ln), reads=[O2B], writes=[O2B])
                            S.op("act", lambda e: e.activation(out=o2[:], in_=o2[:], func=AF.Exp, scale=-0.5), reads=[O2B], writes=[O2B])
                            yt, ytb = ysg[gi % 2]
                            S.op("dve", lambda e: e.scalar_tensor_tensor(out=yt[:], in0=o1[:], scalar=lams[:, 6:7], in1=o2[:],
                                                                         op0=ALU.mult, op1=ALU.mult),
                                 reads=[O1B, O2B, LAMS], writes=[ytb])
                            S.dma("sp", yT[512 + 128 * h:512 + 128 * (h + 1), tok], yt[:], reads=[ytb])
                        st.defer(3, final)

                    return s_fn, post_fn

                load_qk(0)
                gi = 0
                for h in range(4):
                    if h + 1 < 4:
                        load_qk(h + 1)
                    for qc in range(NCH):
                        nkt = 4 * qc + 4
                        oset = osets[gi % 2]
                        for m in range(2):
                            for kt in range(0, 4 * qc, 2):
                                st.push(*mk_step(h, qc, m, (kt, kt + 1), nkt, oset, gi))
                            for kt in range(4 * qc, nkt):
                                st.push(*mk_step(h, qc, m, (kt,), nkt, oset, gi))
                        gi += 1
                st.flush()
                S.barrier()
            if stop_phase <= 2:
                break

            with ExitStack() as ph:
                hT, HTB = T(ph, "hT", [128, 256], BF16)
                kccT = [T(ph, "kccT%d" % g, [64, 256], BF16) for g in range(2)]
                vcE = [T(ph, "vcE%d" % g, [128, 2, 128], BF16) for g in range(2)]
                ovl1, OVB = T(ph, "ovl1_sb", [128, 2, 65], BF16)
                S.dma("pool", ovl1[:], ovl1_d[:], writes=[OVB])
                gsel, GSB = T(ph, "gsel_sb", [24, 24 * 64], BF16)
                S.dma("pool", gsel[:], gsel_d[:], writes=[GSB])
                ng_sb, NGB = T(ph, "ng_sb", [24, S_LEN], BF16)
                S.dma("sp", ng_sb[:], ngT[:], writes=[NGB])
                S.op("pool", lambda e: e.memset(hT[:, 255:256], 0.0), writes=[HTB])
                for g in range(2):
                    S.op("pool", lambda e, g=g: e.memset(vcE[g][0][:], 1.0), writes=[vcE[g][1]])
                with ExitStack() as sub:
                    kc_sb, KCB = T(sub, "kc_sb", [128, S_LEN], BF16)
                    vc_sb, VCB = T(sub, "vc_sb", [128, S_LEN], BF16)
                    S.dma("sp", kc_sb[:], kcT[:], writes=[KCB])
                    S.dma("sp", vc_sb[:], vcT[:], writes=[VCB])
                    w1s, W1B = T(sub, "w1s", [128, 32, 128], BF16)
                    w2s, W2B = T(sub, "w2s", [128, 64], BF16)
                    pos_sb, POSB = T(sub, "pos_sb", [128, 32], BF16)
                    bcol, BCB = T(sub, "bcol", [128, 1], F32)
                    for which, src, srcb in ((0, kc_sb, KCB), (1, vc_sb, VCB)):
                        for half in range(2):
                            S.dma("pool", w1s[half * 64:(half + 1) * 64, :, :],
                                  phi_w1[l, which].rearrange("(l d) h -> d l h", d=64), writes=[W1B])
                        S.dma("pool", w2s[:], phi_w2[l, which], writes=[W2B])
                        S.dma("pool", pos_sb[:], posT[l, which], writes=[POSB])
                        for g in range(2):
                            rows = slice(g * 64, (g + 1) * 64)
                            pbias = next_ps(0, 4)
                            for li in range(32):
                                S.op("pe", lambda e, li=li: e.matmul(PS[pbias][:, 0:1], lhsT=w1s[rows, li, :], rhs=pos_sb[rows, li:li + 1],
                                                                     start=(li == 0), stop=(li == 31)),
                                     reads=[W1B, POSB], writes=[PB[pbias]])
                            S.op("act", lambda e: e.activation(out=bcol[:], in_=PS[pbias][:, 0:1], func=AF.Copy),
                                 reads=[PB[pbias]], writes=[BCB])
                            ph_ = next_ps(0, 4)
                            for li in range(32):
                                S.op("pe", lambda e, li=li: e.matmul(PS[ph_][:, 0:255], lhsT=w1s[rows, li, :],
                                                                     rhs=src[rows, li:li + 16 * 254 + 1:16],
                                                                     start=(li == 0), stop=(li == 31)),
                                     reads=[W1B, srcb], writes=[PB[ph_]])
                            S.op("act", lambda e: e.activation(out=hT[:, 0:255], in_=PS[ph_][:, 0:255], func=AF.Gelu, bias=bcol[:, 0:1]),
                                 reads=[PB[ph_], BCB], writes=[HTB])
                            if which == 0:
                                po_ = next_ps(0, 4)
                                S.op("pe", lambda e: e.matmul(PS[po_][0:64, 0:256], lhsT=w2s[:, :], rhs=hT[:, :], start=True, stop=True),
                                     reads=[W2B, HTB], writes=[PB[po_]])
                                S.op("act", lambda e: e.activation(out=kccT[g][0][:], in_=PS[po_][0:64, 0:256], func=AF.Copy),
                                     reads=[PB[po_]], writes=[kccT[g][1]])
                            else:
                                for nt in range(2):
                                    po_ = next_ps(0, 4)
                                    S.op("pe", lambda e: e.matmul(PS[po_][:, 0:64], lhsT=hT[:, nt * 128:(nt + 1) * 128], rhs=w2s[:, :],
                                                                  start=True, stop=True),
                                         reads=[W2B, HTB], writes=[PB[po_]])
                                    S.op("act", lambda e: e.activation(out=vcE[g][0][:, nt, 0:64], in_=PS[po_][:, 0:64], func=AF.Copy),
                                         reads=[PB[po_]], writes=[vcE[g][1]])
                    S.barrier()
                cbig, CBB = T(ph, "cbig_sb", [128, 32, 64], BF16)
                cfut, CFB = T(ph, "cfut_sb", [128, 32, 64], BF16)
                S.dma("pool", cbig[:], cbig_d[:], writes=[CBB])
                S.dma("pool", cfut[:], cfut_d[:], writes=[CFB])
                impacc = ph.enter_context(nc.sbuf_tensor("impacc%d" % l, [128, 32, 64], F32))
                IMB = [Buf() for _ in range(NCH)]
                impm, IPMB = T(ph, "impm", [128, 64], F32)
                imp2, IP2B = T(ph, "imp2", [128, 64], F32)
                v8a, V8AB = T(ph, "v8a", [128, 8], F32)
                v8b, V8BB = T(ph, "v8b", [128, 8], F32)
                rl4, RL4B = T(ph, "rl4", [128, 4], F32)
                IMQ = [Buf() for _ in range(32)]
                biasq, BQB = T(ph, "biasq", [128, 4, 64], BF16)
                biasT, BTB = T(ph, "biasT", [64, S_LEN], BF16)
                ksE, KSB = T(ph, "ksE", [128, S_LEN], BF16)
                kwg, KWB = T(ph, "kwg", [64, S_LEN], BF16)
                vsE, VSB = T(ph, "vsE", [128, 32, 128], BF16)
                vwE, VWB = T(ph, "vwE", [128, 32, 128], BF16)
                qaug = [T(ph, "qaug%d" % i, [128, S_LEN], BF16) for i in range(2)]
                pnset = [[T(ph, "pn%d_%d" % (s_, i), [128, CH], BF16) for i in range(2)] for s_ in range(2)]
                pt3 = [T(ph, "pt3_%d" % i, [128, CH], BF16) for i in range(5)]
                rLt, RLTB = T(ph, "rLt", [64, CH], F32)
                tA, TAB = T(ph, "tA", [64, CH], F32)
                acc, ACCB = T(ph, "acc", [64, CH], F32)
                ycs = [T(ph, "ycs%d" % i, [64, CH], BF16) for i in range(2)]
                prot = [0]
                qload = [0]
                urot = [0]
                grot = [0]
                st = Stream(3)

                def mk_cmp(g, qa, qc, nt, pn, last_fn):
                    box = {}

                    def s_fn():
                        ps_s = next_ps(0, 4)
                        box["ps"] = ps_s
                        S.op("pe", lambda e: e.matmul(PS[ps_s][:, :], lhsT=kccT[g][0][:, nt * 128:(nt + 1) * 128],
                                                      rhs=qa[0][0:64, qc * CH:(qc + 1) * CH], start=True, stop=True),
                             reads=[kccT[g][1], qa[1]], writes=[PB[ps_s]])

                    def post_fn():
                        ps_s = box["ps"]
                        p_t, p_b = pn[nt]
                        S.op("act", lambda e: e.activation(out=p_t[:], in_=PS[ps_s][:, :], func=AF.Exp, scale=0.125),
                             reads=[PB[ps_s]], writes=[p_b])
                        S.op("pool", lambda e: e.affine_select(out=p_t[:], in_=p_t[:], pattern=[[1, CH]], compare_op=ALU.is_ge,
                                                               fill=0.0, base=qc * CH - 2048 * nt - 31, channel_multiplier=-16),
                             reads=[p_b], writes=[p_b])
                        if last_fn is not None:
                            last_fn()

                    return s_fn, post_fn

                def mk_att(qa, qc, kt, c0, c1, klhs, krows, kbuf, vE, vbuf, obank, first, last, mask, after):
                    box = {}

                    def s_fn():
                        ps_s = next_ps(0, 4)
                        box["ps"] = ps_s
                        S.op("pe", lambda e: e.matmul(PS[ps_s][:, c0:c1], lhsT=klhs[krows, kt * 128:(kt + 1) * 128],
                                                      rhs=qa[0][krows, qc * CH + c0:qc * CH + c1], start=True, stop=True),
                             reads=[kbuf, qa[1]], writes=[PB[ps_s]])

                    def post_fn():
                        ps_s = box["ps"]
                        p_t, p_b = pt3[prot[0] % 5]
                        prot[0] += 1
                        S.op("act", lambda e: e.activation(out=p_t[:, c0:c1], in_=PS[ps_s][:, c0:c1], func=AF.Exp, scale=0.125),
                             reads=[PB[ps_s]], writes=[p_b])
                        if mask == "lo":
                            S.op("pool", lambda e: e.affine_select(out=p_t[:, c0:c0 + 128], in_=p_t[:, c0:c0 + 128],
                                                                   pattern=[[1, 128]], compare_op=ALU.is_ge, fill=0.0,
                                                                   base=0, channel_multiplier=-1),
                                 reads=[p_b], writes=[p_b])
                        elif mask == "hi":
                            S.op("pool", lambda e: e.affine_select(out=p_t[:, c1 - 128:c1], in_=p_t[:, c1 - 128:c1],
                                                                   pattern=[[-1, 128]], compare_op=ALU.is_gt, fill=0.0,
                                                                   base=0, channel_multiplier=1),
                                 reads=[p_b], writes=[p_b])
                        S.op("pe", lambda e: e.matmul(PS[obank][:, c0:c1], lhsT=vE[:, kt, :], rhs=p_t[:, c0:c1],
                                                      start=first, stop=last),
                             reads=[vbuf, p_b], writes=[PB[obank]])
                        if after is not None:
                            after()

                    return s_fn, post_fn

                for g in range(2):
                    S.dma("sp", ksE[0:64, :], ksT[g * 64:(g + 1) * 64, :], writes=[KSB])
                    S.dma("pool", ksE[64:128, :], expand_d[:], writes=[KSB])
                    S.dma("sp", kwg[:], kwT[g * 64:(g + 1) * 64, :], writes=[KWB])
                    S.dma("sp", vsE[:], vaug[g].rearrange("(kt p) c -> p kt c", p=128), writes=[VSB])
                    S.dma("sp", vwE[:], vaug[2 + g].rearrange("(kt p) c -> p kt c", p=128), writes=[VWB])
                    for hp in range(4):
                        h = 4 * g + hp
                        qa = qaug[qload[0] % 2]
                        qload[0] += 1
                        S.dma("sp", qa[0][0:64, :], qnT[h * 64:(h + 1) * 64, :], writes=[qa[1]])
                        for qc in range(NCH):
                            nts = [0] if qc < 4 else [0, 1]
                            pn = pnset[(hp * NCH + qc) % 2]

                            def imp_fn(qc=qc, nts=nts, pn=pn, hp=hp):
                                ub = 4 + urot[0] % 4
                                urot[0] += 1
                                for qs in range(4):
                                    for nt in nts:
                                        S.op("pe", lambda e: e.matmul(PS[ub][:, qs * 128:qs * 128 + 65], lhsT=pn[nt][0][:, qs * 128:(qs + 1) * 128],
                                                                      rhs=ovl1[:, nt, :], start=(nt == nts[0]), stop=(nt == nts[-1])),
                                             reads=[pn[nt][1], OVB], writes=[PB[ub]])
                                S.op("dve", lambda e: e.tensor_scalar(out=rl4[:], in0=PS[ub][:, 64:512:128], scalar1=1e-30,
                                                                      scalar2=None, op0=ALU.add), reads=[PB[ub], RL4B], writes=[RL4B])
                                S.op("dve", lambda e: e.reciprocal(out=rl4[:], in_=rl4[:]), reads=[RL4B], writes=[RL4B])
                                for qs in range(4):
                                    qt = qc * 4 + qs
                                    if hp == 0:
                                        S.op("dve", lambda e: e.tensor_scalar(out=impacc[:, qt, :], in0=PS[ub][:, qs * 128:qs * 128 + 64],
                                                                              scalar1=rl4[:, qs:qs + 1], scalar2=None, op0=ALU.mult),
                                             reads=[PB[ub], RL4B], writes=[IMQ[qt]])
                                    else:
                                        S.op("dve", lambda e: e.scalar_tensor_tensor(out=impacc[:, qt, :], in0=PS[ub][:, qs * 128:qs * 128 + 64],
                                                                                     scalar=rl4[:, qs:qs + 1], in1=impacc[:, qt, :],
                                                                                     op0=ALU.mult, op1=ALU.add),
                                             reads=[PB[ub], RL4B, IMQ[qt]], writes=[IMQ[qt]])

                            for nt in nts:
                                st.push(*mk_cmp(g, qa, qc, nt, pn, imp_fn if nt == nts[-1] else None))
                    st.flush()
                    for qc in range(NCH):
                        for qs in range(4):
                            qt = qc * 4 + qs
                            S.op("dve", lambda e: e.tensor_tensor(out=impm[:], in0=impacc[:, qt, :], in1=cbig[:, qt, :], op=ALU.max),
                                 reads=[IMQ[qt], CBB, IPMB], writes=[IPMB])
                            S.op("dve", lambda e: e.tensor_tensor(out=impm[:], in0=impm[:], in1=cfut[:, qt, :], op=ALU.min),
                                 reads=[IPMB, CFB], writes=[IPMB])
                            S.op("dve", lambda e: e.max(out=v8a[:], in_=impm[:]), reads=[IPMB], writes=[V8AB])
                            S.op("dve", lambda e: e.match_replace(out=imp2[:], in_to_replace=v8a[:], in_values=impm[:], imm_value=-3.0e38),
                                 reads=[IPMB, V8AB], writes=[IP2B])
                            S.op("dve", lambda e: e.max(out=v8b[:], in_=imp2[:]), reads=[IP2B], writes=[V8BB])
                            S.op("dve", lambda e: e.tensor_scalar(out=biasq[:, qs, :], in0=impm[:], scalar1=v8b[:, 7:8], scalar2=NEGB,
                                                                  op0=ALU.is_lt, op1=ALU.mult),
                                 reads=[IPMB, V8BB], writes=[BQB])
                        for qs in range(4):
                            S.op("pe", lambda e: e.matmul(PS[4][0:64, qs * 128:(qs + 1) * 128], lhsT=biasq[:, qs, :], rhs=ident[:, :],
                                                          start=True, stop=True), reads=[BQB, IDB], writes=[PB[4]])
                        S.op("act", lambda e: e.activation(out=biasT[:, qc * CH:(qc + 1) * CH], in_=PS[4][0:64, :], func=AF.Copy),
                             reads=[PB[4]], writes=[BTB])
                    for hp in range(4):
                        h = 4 * g + hp
                        qa = qaug[qload[0] % 2]
                        qload[0] += 1
                        S.dma("sp", qa[0][0:64, :], qnT[h * 64:(h + 1) * 64, :], writes=[qa[1]])
                        S.dma("sp", qa[0][64:128, :], biasT[:, :], reads=[BTB], writes=[qa[1]])
                        for qc in range(NCH):
                            nts = [0] if qc < 4 else [0, 1]
                            pn = pnset[(hp * NCH + qc) % 2]
                            ci = hp * NCH + qc

                            def epi(br, pbk, which, h=h, qc=qc, ci=ci):
                                def fn():
                                    tok = slice(qc * CH, (qc + 1) * CH)
                                    col = h * 3 + br
                                    gb = 7
                                    grot[0] += 1
                                    yc, ycb = ycs[ci % 2]
                                    S.op("pe", lambda e: e.matmul(PS[gb][0:64, :], lhsT=gsel[0:24, col * 64:(col + 1) * 64], rhs=ng_sb[0:24, tok],
                                                                  start=True, stop=True), reads=[GSB, NGB], writes=[PB[gb]])
                                    if br == 0:
                                        S.op("dve", lambda e: e.tensor_scalar(out=rLt[:], in0=PS[pbk][64:128, :], scalar1=1e-30, scalar2=None,
                                                                              op0=ALU.add), reads=[PB[pbk], RLTB], writes=[RLTB])
                                        S.op("dve", lambda e: e.reciprocal(out=rLt[:], in_=rLt[:]), reads=[RLTB], writes=[RLTB])
                                    else:
                                        S.op("dve", lambda e: e.reciprocal(out=rLt[:], in_=PS[pbk][64:128, :]), reads=[PB[pbk], RLTB], writes=[RLTB])
                                    S.op("dve", lambda e: e.tensor_tensor(out=tA[:], in0=PS[pbk][0:64, :], in1=rLt[:], op=ALU.mult),
                                         reads=[PB[pbk], RLTB, TAB], writes=[TAB])
                                    if which == 0:
                                        S.op("dve", lambda e: e.tensor_tensor(out=acc[:], in0=PS[gb][0:64, :], in1=tA[:], op=ALU.mult),
                                             reads=[PB[gb], TAB, ACCB], writes=[ACCB])
                                    else:
                                        S.op("dve", lambda e: e.tensor_tensor(out=tA[:], in0=PS[gb][0:64, :], in1=tA[:], op=ALU.mult),
                                             reads=[PB[gb], TAB], writes=[TAB])
                                        if which == 1:
                                            S.op("pool", lambda e: e.tensor_tensor(out=acc[:], in0=acc[:], in1=tA[:], op=ALU.add),
                                                 reads=[TAB, ACCB], writes=[ACCB])
                                        else:
                                            S.op("pool", lambda e: e.tensor_tensor(out=yc[:], in0=acc[:], in1=tA[:], op=ALU.add),
                                                 reads=[TAB, ACCB], writes=[ycb])
                                            S.dma("sp", yT[1024 + 64 * h:1024 + 64 * (h + 1), tok], yc[:], reads=[ycb])
                                return lambda: st.defer(2, fn)

                            def cmp_o(nts=nts, pn=pn, g=g):
                                for nt in nts:
                                    S.op("pe", lambda e: e.matmul(PS[6][:, :], lhsT=vcE[g][0][:, nt, :], rhs=pn[nt][0][:, :],
                                                                  start=(nt == nts[0]), stop=(nt == nts[-1])),
                                         reads=[vcE[g][1], pn[nt][1]], writes=[PB[6]])
                                epi(0, 6, 0)()
                            for nt in nts:
                                st.push(*mk_cmp(g, qa, qc, nt, pn, cmp_o if nt == nts[-1] else None))
                            jls = [jl for jl in (-1, -4, -3, -2, 0, 1, 2, 3) if 4 * qc + jl >= 0]
                            for wi, jl in enumerate(jls):
                                kt = 4 * qc + jl
                                if jl >= 0:
                                    c0, c1, mk = 128 * jl, CH, "lo"
                                else:
                                    c0, c1, mk = 0, 128 * (jl + 5), "hi"
                                lastw = (wi == len(jls) - 1)
                                st.push(*mk_att(qa, qc, kt, c0, c1, kwg, slice(0, 64), KWB, vwE, VWB, 5, wi == 0, lastw, mk,
                                                epi(2, 5, 1) if lastw else None))
                            nkt = 4 * qc + 4
                            for kt in range(nkt):
                                j = kt - 4 * qc
                                c0 = 128 * j if j > 0 else 0
                                lasts = (kt == nkt - 1)
                                st.push(*mk_att(qa, qc, kt, c0, CH, ksE, slice(0, 128), KSB, vsE, VSB, 4, kt == 0, lasts,
                                                "lo" if j >= 0 else None, epi(1, 4, 2) if lasts else None))
                    st.flush()
                S.barrier()
            if stop_phase <= 3:
                break

            def layer_norm(ph, r, RB, c, goff, boff, dst, tmps):
                (sqt, mean, MEB, msq, MSB, tt, outf, rbs) = tmps
                tok = slice(c * CH, (c + 1) * CH)
                for ot in range(8):
                    rb_t, rb_b = rbs[ot % 2]
                    S.op("pool", lambda e, ot=ot: e.tensor_copy(out=rb_t[:], in_=r[:, ot, :]), reads=[RB], writes=[rb_b])
                    S.op("pe", lambda e, ot=ot: e.matmul(PS[4][:, :], lhsT=ones_bf[:, :], rhs=rb_t[:], start=(ot == 0), stop=(ot == 7)),
                         reads=[ONB, rb_b], writes=[PB[4]])
                    sq_t, sq_b = sqt[ot % 2]
                    S.op("act", lambda e, ot=ot: e.activation(out=sq_t[:], in_=r[:, ot, :], func=AF.Square), reads=[RB], writes=[sq_b])
                    S.op("pe", lambda e, ot=ot: e.matmul(PS[5][:, :], lhsT=ones_bf[:, :], rhs=sq_t[:], start=(ot == 0), stop=(ot == 7)),
                         reads=[ONB, sq_b], writes=[PB[5]])
                S.op("act", lambda e: e.activation(out=mean[:], in_=PS[4][:, :], func=AF.Copy, scale=1.0 / 1024.0),
                     reads=[PB[4]], writes=[MEB])
                S.op("act", lambda e: e.activation(out=msq[:], in_=mean[:], func=AF.Square), reads=[MEB], writes=[MSB])
                S.op("dve", lambda e: e.scalar_tensor_tensor(out=msq[:], in0=PS[5][:, :], scalar=1.0 / 1024.0, in1=msq[:],
                                                             op0=ALU.mult, op1=ALU.subtract), reads=[PB[5], MSB], writes=[MSB])
                S.op("dve", lambda e: e.tensor_scalar(out=msq[:], in0=msq[:], scalar1=LN_EPS, scalar2=None, op0=ALU.add),
                     reads=[MSB], writes=[MSB])
                S.op("act", lambda e: e.activation(out=msq[:], in_=msq[:], func=AF.Ln), reads=[MSB], writes=[MSB])
                S.op("act", lambda e: e.activation(out=msq[:], in_=msq[:], func=AF.Exp, scale=-0.5), reads=[MSB], writes=[MSB])
                for ot in range(8):
                    t_t, t_b = tt[ot % 2]
                    o_t, o_b = outf[ot % 2]
                    S.op("dve", lambda e, ot=ot: e.tensor_tensor(out=t_t[:], in0=r[:, ot, :], in1=mean[:], op=ALU.subtract),
                         reads=[RB, MEB], writes=[t_b])
                    S.op("dve", lambda e: e.tensor_tensor(out=t_t[:], in0=t_t[:], in1=msq[:], op=ALU.mult),
                         reads=[t_b, MSB], writes=[t_b])
                    S.op("act", lambda e, ot=ot: e.activation(out=o_t[:], in_=t_t[:], func=AF.Identity,
                                                              scale=pv[:, goff + ot:goff + ot + 1], bias=pv[:, boff + ot:boff + ot + 1]),
                         reads=[t_b, PVB], writes=[o_b])
                    S.dma("sp", dst[ot * 128:(ot + 1) * 128, tok], o_t[:], reads=[o_b])
                    S.op("pool", lambda e, ot=ot: e.tensor_copy(out=xT_bf[:, ot, tok], in_=o_t[:]),
                         reads=[o_b], writes=[XB[c]])

            def ln_tmps(ph):
                sqt = [T(ph, "sqt%d" % i, [128, CH], BF16) for i in range(2)]
                rbs = [T(ph, "rbs%d" % i, [128, CH], BF16) for i in range(2)]
                mean, MEB = T(ph, "mean", [128, CH], F32)
                msq, MSB = T(ph, "msq", [128, CH], F32)
                tt = [T(ph, "lt%d" % i, [128, CH], F32) for i in range(2)]
                outf = [T(ph, "lo%d" % i, [128, CH], F32) for i in range(2)]
                return (sqt, mean, MEB, msq, MSB, tt, outf, rbs)

            with ExitStack() as ph:
                wbr, WBRB = T(ph, "wbr", [128, 12, 1024], BF16)
                wo, WOB = T(ph, "wo", [128, 8, 1024], BF16)
                WBRL = [Buf() for _ in range(3)]
                for br in range(3):
                    S.dma("pool", wbr[:, br * 4:(br + 1) * 4, :], w_branch[l, br].rearrange("(kc p) m -> p kc m", p=128), writes=[WBRL[br]])
                S.dma("pool", wo[:], w_o[l].rearrange("(kc p) m -> p kc m", p=128), writes=[WOB])
                ych = [T(ph, "ych%d" % i, [128, 12, CH], BF16) for i in range(2)]
                gtl = [T(ph, "gtl%d" % i, [128, 3, CH], BF16) for i in range(4)]
                xo = [T(ph, "xo%d" % i, [128, CH], F32) for i in range(4)]
                mg_v = mgT.rearrange("(br ot p) t -> ot p br t", br=3, ot=8)
                merged, MGB = T(ph, "merged", [128, 8, CH], BF16)
                r, RB = T(ph, "r", [128, 8, CH], F32)
                tmp, TMB = T(ph, "tmp", [128, CH], F32)
                macc, MAB = T(ph, "macc", [128, CH], F32)
                tmps = ln_tmps(ph)

                def load_y(c):
                    tok = slice(c * CH, (c + 1) * CH)
                    S.dma("sp", ych[c % 2][0][:], yT[:, tok].rearrange("(kc p) t -> p kc t", p=128), writes=[ych[c % 2][1]])

                def load_g(idx):
                    if idx >= NCH * 8:
                        return
                    c_, ot_ = idx // 8, idx % 8
                    g_t, g_b = gtl[idx % 4]
                    S.dma("sp", g_t[:], mg_v[ot_][:, :, c_ * CH:(c_ + 1) * CH], writes=[g_b])

                def load_x(c_, ot_):
                    x_t, x_b = xo[ot_ % 4]
                    S.dma("sp", x_t[:], xr_in[ot_ * 128:(ot_ + 1) * 128, c_ * CH:(c_ + 1) * CH], writes=[x_b])

                load_y(0)
                load_g(0)
                load_g(1)
                for c in range(NCH):
                    if c + 1 < NCH:
                        load_y(c + 1)
                    y_t, y_b = ych[c % 2]
                    for ot in range(8):
                        load_g(c * 8 + ot + 2)
                        if ot == 5:
                            load_x(c, 0)
                            load_x(c, 1)
                        g_t, g_b = gtl[(c * 8 + ot) % 4]
                        for br in range(3):
                            pb = next_ps6()
                            for kc in range(4):
                                S.op("pe", lambda e, kc=kc: e.matmul(PS[pb][:, :], lhsT=wbr[:, br * 4 + kc, ot * 128:(ot + 1) * 128],
                                                                     rhs=y_t[:, br * 4 + kc, :], start=(kc == 0), stop=(kc == 3)),
                                     reads=[WBRL[br], y_b], writes=[PB[pb]])
                            if br == 0:
                                S.op("dve", lambda e: e.tensor_tensor(out=macc[:], in0=PS[pb][:, :], in1=g_t[:, br, :], op=ALU.mult),
                                     reads=[PB[pb], g_b, MAB], writes=[MAB])
                            else:
                                S.op("dve", lambda e: e.tensor_tensor(out=tmp[:], in0=PS[pb][:, :], in1=g_t[:, br, :], op=ALU.mult),
                                     reads=[PB[pb], g_b, TMB], writes=[TMB])
                                if br == 1:
                                    S.op("pool", lambda e: e.tensor_tensor(out=macc[:], in0=macc[:], in1=tmp[:], op=ALU.add),
                                         reads=[TMB, MAB], writes=[MAB])
                                else:
                                    S.op("pool", lambda e: e.tensor_tensor(out=merged[:, ot, :], in0=macc[:], in1=tmp[:], op=ALU.add),
                                         reads=[TMB, MAB], writes=[MGB])
                    for ot in range(8):
                        if ot + 2 < 8:
                            load_x(c, ot + 2)
                        x_t, x_b = xo[ot % 4]
                        pb = next_ps6()
                        for kc in range(8):
                            S.op("pe", lambda e, kc=kc: e.matmul(PS[pb][:, :], lhsT=wo[:, kc, ot * 128:(ot + 1) * 128], rhs=merged[:, kc, :],
                                                                 start=(kc == 0), stop=(kc == 7)),
                                 reads=[WOB, MGB], writes=[PB[pb]])
                        S.op("dve", lambda e: e.scalar_tensor_tensor(out=r[:, ot, :], in0=x_t[:], scalar=ALPHA, in1=PS[pb][:, :],
                                                                     op0=ALU.mult, op1=ALU.add),
                             reads=[x_b, PB[pb]], writes=[RB])
                    layer_norm(ph, r, RB, c, PV_G1, PV_B1, xr_mid, tmps)
                S.barrier()
            if stop_phase <= 4:
                break

            p5o = ExitStack()
            wd, WDB = T(p5o, "wd", [128, 22, 1024], BF16)
            for q4 in range(2):
                S.dma("pool", wd[:, q4 * 11:(q4 + 1) * 11, :],
                      w_down[l, q4 * 1408:(q4 + 1) * 1408, :].rearrange("(kc p) m -> p kc m", p=128), writes=[WDB])
            with ExitStack() as ph:
                wu = [T(ph, "wu%d" % i, [128, 8, 256], BF16) for i in range(2)]
                hb = [[T(ph, "hb%d_%d" % (w_, i), [128, 2 + CH], F32) for i in range(2)] for w_ in range(2)]
                tcv = [[T(ph, "tcv%d_%d" % (w_, i), [128, CH], F32) for i in range(2)] for w_ in range(2)]
                gls = [T(ph, "gl%d" % i, [128, CH], F32) for i in range(2)]
                gsg = [T(ph, "gsg%d" % i, [128, CH], BF16) for i in range(2)]

                def load_wu(ft):
                    w_t, w_b = wu[ft % 2]
                    S.dma("pool", w_t[:, :, 0:128], w_up[l, :, ft * 128:(ft + 1) * 128].rearrange("(kc p) m -> p kc m", p=128), writes=[w_b])
                    S.dma("pool", w_t[:, :, 128:256], w_up[l, :, 2816 + ft * 128:2816 + (ft + 1) * 128].rearrange("(kc p) m -> p kc m", p=128),
                          writes=[w_b])

                def tail5(ft, c, par):
                    tok = slice(c * CH, (c + 1) * CH)
                    gl, GLB = gls[par]
                    g_t, g_b = gsg[par]
                    S.op("act", lambda e: e.activation(out=gl[:], in_=tcv[0][par][0][:], func=AF.Gelu), reads=[tcv[0][par][1], GLB], writes=[GLB])
                    S.op("pool", lambda e: e.tensor_tensor(out=g_t[:], in0=gl[:], in1=tcv[1][par][0][:], op=ALU.mult),
                         reads=[GLB, tcv[1][par][1]], writes=[g_b])
                    S.dma("sp", gT[ft * 128:(ft + 1) * 128, tok], g_t[:], reads=[g_b])

                load_wu(0)
                pend5 = None
                it5 = 0
                for ft in range(22):
                    if ft + 1 < 22:
                        load_wu(ft + 1)
                    w_t, w_b = wu[ft % 2]
                    for c in range(NCH):
                        tok = slice(c * CH, (c + 1) * CH)
                        par = it5 % 2
                        it5 += 1
                        for which in range(2):
                            pb = next_ps6((0, 1, 2, 3, 4, 5, 6, 7))
                            for kc in range(8):
                                S.op("pe", lambda e, kc=kc: e.matmul(PS[pb][:, :], lhsT=w_t[:, kc, which * 128:(which + 1) * 128],
                                                                     rhs=xT_bf[:, kc, tok], start=(kc == 0), stop=(kc == 7)),
                                     reads=[w_b, XB[c]], writes=[PB[pb]])
                            h_t, h_b = hb[which][c % 2]
                            p_t, p_b = hb[which][(c + 1) % 2]
                            col = ft + 22 * which
                            fw = lambda k: pv[:, PV_FW + col * 3 + k:PV_FW + col * 3 + k + 1]
                            t_t, t_b = tcv[which][par]
                            S.op("act", lambda e: e.activation(out=h_t[:, 2:2 + CH], in_=PS[pb][:, :], func=AF.Copy),
                                 reads=[PB[pb]], writes=[h_b])
                            S.op("act", lambda e: e.activation(out=t_t[:], in_=h_t[:, 2:2 + CH], func=AF.Identity, scale=fw(2),
                                                               bias=pv[:, PV_FB + col:PV_FB + col + 1]),
                                 reads=[h_b, PVB, t_b], writes=[t_b])
                            if c == 0:
                                S.op("pool", lambda e: e.memset(h_t[:, 0:2], 0.0), writes=[h_b])
                            else:
                                S.op("pool", lambda e: e.tensor_copy(out=h_t[:, 0:2], in_=p_t[:, CH:CH + 2]), reads=[p_b], writes=[h_b])
                            S.op("dve", lambda e: e.scalar_tensor_tensor(out=t_t[:], in0=h_t[:, 0:CH], scalar=fw(0), in1=t_t[:],
                                                                         op0=ALU.mult, op1=ALU.add), reads=[h_b, PVB, t_b], writes=[t_b])
                            S.op("dve", lambda e: e.scalar_tensor_tensor(out=t_t[:], in0=h_t[:, 1:CH + 1], scalar=fw(1), in1=t_t[:],
                                                                         op0=ALU.mult, op1=ALU.add), reads=[h_b, PVB, t_b], writes=[t_b])
                        if pend5 is not None:
                            tail5(*pend5)
                        pend5 = (ft, c, par)
                tail5(*pend5)
                S.barrier()

            with ExitStack() as ph:
                gch = [T(ph, "gdch%d" % i, [128, 22, CH], BF16) for i in range(2)]
                xo = [T(ph, "xdo%d" % i, [128, CH], F32) for i in range(4)]
                r, RB = T(ph, "r2", [128, 8, CH], F32)
                tmps = ln_tmps(ph)

                def load_gc(c):
                    tok = slice(c * CH, (c + 1) * CH)
                    S.dma("sp", gch[c % 2][0][:], gT[:, tok].rearrange("(kc p) t -> p kc t", p=128), writes=[gch[c % 2][1]])

                def load_x5(c_, ot_):
                    x_t, x_b = xo[ot_ % 4]
                    S.dma("sp", x_t[:], xr_mid[ot_ * 128:(ot_ + 1) * 128, c_ * CH:(c_ + 1) * CH], writes=[x_b])

                load_gc(0)
                for c in range(NCH):
                    if c + 1 < NCH:
                        load_gc(c + 1)
                    load_x5(c, 0)
                    load_x5(c, 1)
                    g_t, g_b = gch[c % 2]
                    for ot in range(8):
                        if ot + 2 < 8:
                            load_x5(c, ot + 2)
                        x_t, x_b = xo[ot % 4]
                        pb = next_ps6()
                        for kc in range(22):
                            S.op("pe", lambda e, kc=kc: e.matmul(PS[pb][:, :], lhsT=wd[:, kc, ot * 128:(ot + 1) * 128], rhs=g_t[:, kc, :],
                                                                 start=(kc == 0), stop=(kc == 21)),
                                 reads=[WDB, g_b], writes=[PB[pb]])
                        S.op("dve", lambda e: e.scalar_tensor_tensor(out=r[:, ot, :], in0=x_t[:], scalar=ALPHA, in1=PS[pb][:, :],
                                                                     op0=ALU.mult, op1=ALU.add),
                             reads=[x_b, PB[pb]], writes=[RB])
                    layer_norm(ph, r, RB, c, PV_G2, PV_B2, xr_out, tmps)
                S.barrier()
            p5o.close()
        S.barrier()
    return nc


def host_constants():
    f32 = np.float32
    pos = np.arange(S_LEN, dtype=f32)
    inv_freq = (f32(500000.0) ** (-np.arange(0, 16, 2, dtype=f32) / f32(16))).astype(f32)
    ang = (pos[:, None] * inv_freq[None, :]).astype(f32)
    cos = np.cos(ang.astype(np.float64)).astype(f32).T
    sin = np.sin(ang.astype(np.float64)).astype(f32).T
    C64 = np.ones((64, S_LEN), f32)
    S64 = np.zeros((64, S_LEN), f32)
    C64[0:8] = cos
    C64[8:16] = cos
    S64[0:8] = -sin
    S64[8:16] = sin
    ropeC = np.concatenate([C64, C64], 0)
    ropeS = np.concatenate([S64, S64], 0)
    permT = np.zeros((128, 128), f32)
    for m in range(128):
        r = m % 64
        if r < 8:
            permT[m + 8, m] = 1.0
        elif r < 16:
            permT[m - 8, m] = 1.0
    ident = np.eye(128, dtype=f32)
    expand = np.zeros((64, S_LEN), f32)
    for j in range(64):
        expand[j, 64 * j:64 * (j + 1)] = 1.0
    n = np.arange(256)
    cstart = n * 16
    sstart = np.arange(64) * 64
    ov = np.maximum(np.minimum((cstart + 32)[:, None], (sstart + 64)[None, :])
                    - np.maximum(cstart[:, None], sstart[None, :]), 0).astype(f32) / f32(32)
    ov[255] = 0.0
    ovl = np.concatenate([ov, np.ones((256, 1), f32)], 1)
    ovl1 = np.ascontiguousarray(ovl.reshape(2, 128, 65).transpose(1, 0, 2))
    t = np.arange(S_LEN)
    cur = t // 64
    jj = np.arange(64)
    forced = (jj[None, :] == 0) | ((jj[None, :] <= cur[:, None]) & (jj[None, :] > cur[:, None] - 2))
    future = jj[None, :] > cur[:, None]
    cbig = np.where(forced, f32(1e30), f32(0.0)).astype(f32)
    cfut = np.where(future, f32(-1e30), f32(1e30)).astype(f32)
    cbig = np.ascontiguousarray(cbig.reshape(32, 128, 64).transpose(1, 0, 2))
    cfut = np.ascontiguousarray(cfut.reshape(32, 128, 64).transpose(1, 0, 2))
    gsel = np.zeros((24, 24 * 64), f32)
    for j in range(24):
        gsel[j, j * 64:(j + 1) * 64] = 1.0
    return dict(ropeC=ropeC, ropeS=ropeS, permT=permT, ident=ident, expand=expand, ovl1=ovl1,
                cbig=cbig, cfut=cfut, gsel=gsel)


def host_params(inp):
    f32 = np.float32
    pvec = np.zeros((4, 128, PV_N), f32)
    for l in range(4):
        ca = inp["conv_a_w"][l]
        pvec[l, :, PV_CA:PV_CA + 12] = ca.reshape(3, 4, 128).transpose(2, 1, 0).reshape(128, 12)
        pvec[l, :, PV_SL] = inp["diff_subln"][l]
        pvec[l, :, PV_G1:PV_G1 + 8] = inp["ln1_g"][l].reshape(8, 128).T
        pvec[l, :, PV_B1:PV_B1 + 8] = inp["ln1_b"][l].reshape(8, 128).T
        pvec[l, :, PV_G2:PV_G2 + 8] = inp["ln2_g"][l].reshape(8, 128).T
        pvec[l, :, PV_B2:PV_B2 + 8] = inp["ln2_b"][l].reshape(8, 128).T
        fw = inp["ffn_conv_w"][l]
        pvec[l, :, PV_FW:PV_FW + 132] = fw.reshape(3, 44, 128).transpose(2, 1, 0).reshape(128, 132)
        pvec[l, :, PV_FB:PV_FB + 44] = inp["ffn_conv_b"][l].reshape(44, 128).T
    pos = inp["nsa_cmp_pos"]
    pT = np.ascontiguousarray(pos.transpose(0, 1, 3, 2))
    posT = np.concatenate([pT, pT], axis=2)
    lamrep = np.ascontiguousarray(np.broadcast_to(inp["diff_lambda"].reshape(4, 1, 256), (4, 128, 256))).astype(f32)
    return dict(pvec=pvec, posT=np.ascontiguousarray(posT), lamrep=lamrep)


_NC_CACHE = {}


def make_in_maps(inputs, n_cores=8):
    inp = {k: np.asarray(v) for k, v in inputs.items()}
    shared = dict(host_constants())
    shared.update(host_params(inp))
    for k in ("w_in", "w_branch", "w_o", "ffn_w_up", "ffn_w_down", "nsa_phi_w1", "nsa_phi_w2"):
        shared[k] = np.ascontiguousarray(inp[k], dtype=np.float32)
    maps = []
    for c in range(n_cores):
        m = dict(shared)
        m["xT"] = np.ascontiguousarray(inp["x"][c % 4].T)
        maps.append(m)
    return maps


def kernel(**inputs):
    if "nc" not in _NC_CACHE:
        _NC_CACHE["nc"] = build()
    nc = _NC_CACHE["nc"]
    maps = make_in_maps(inputs, 8)
    res = run_bass_kernel_spmd(nc, maps, core_ids=list(range(8)))
    out = np.stack([np.ascontiguousarray(res.results[b]["outT"].T) for b in range(4)], 0)
    return out.astype(np.float32)
```
